# Optimizing a Trainium2 kernel written in Bass

```python
import math
import jax
import jax.numpy as jnp
from jax import lax
import numpy as np

D_MODEL = 1024
BATCH = 4
SEQ = 8192
DEPTH = 4
DEC_BATCH = 8
DEC_SEQ = 64
PAST_LEN = 1024

CHUNK = 64
N_BRANCH = 4
BRANCH_W = D_MODEL // 2
EPS = 1e-6

LRU_W = BRANCH_W
LRU_BLOCKS = 8
LRU_BW = LRU_W // LRU_BLOCKS
LRU_CONV = 4
LRU_C = 8.0

SSD_INNER = BRANCH_W
SSD_P = 64
SSD_H = SSD_INNER // SSD_P
SSD_G = 4
SSD_R = SSD_H // SSD_G
SSD_N = 64
SSD_CONV = 4
SSD_CONV_DIM = SSD_INNER + 2 * SSD_G * SSD_N
SSD_CHUNK = CHUNK

CF_W = BRANCH_W
CF_CONV = 31

SB_H = 8
SB_DH = BRANCH_W // SB_H
SB_W = SB_H * SB_DH
SB_QBLOCK = 128

IN_SPLITS = (LRU_W, LRU_W, SSD_INNER, SSD_CONV_DIM, SSD_H, 2 * CF_W, CF_W, 3 * SB_W, SB_W, N_BRANCH * D_MODEL)
D_IN = sum(IN_SPLITS)

kernel_name = 'hybrid_streaming_encoder_step'


def split_cols(x, sizes):
    idx = [int(i) for i in np.cumsum(sizes)[:-1]]
    return jnp.split(x, idx, axis=-1)


def rms_norm(x, g):
    xf = x.astype(jnp.float32)
    y = xf * lax.rsqrt(jnp.mean(xf * xf, axis=-1, keepdims=True) + EPS)
    return (y * g.astype(jnp.float32)).astype(x.dtype)


def layer_norm(x, g, b):
    xf = x.astype(jnp.float32)
    xc = xf - jnp.mean(xf, axis=-1, keepdims=True)
    var = jnp.mean(xc * xc, axis=-1, keepdims=True)
    return (xc * lax.rsqrt(var + EPS) * g.astype(jnp.float32) + b.astype(jnp.float32)).astype(x.dtype)


def causal_dwconv(x, buf, w, b):
    k = w.shape[0]
    xf = jnp.concatenate([buf.astype(x.dtype), x], axis=1)
    y = lax.conv_general_dilated(xf, w[:, None, :].astype(x.dtype), window_strides=(1,), padding='VALID',
                                 dimension_numbers=('NWC', 'WIO', 'NWC'), feature_group_count=x.shape[-1])
    return y + b.astype(x.dtype), xf[:, xf.shape[1] - (k - 1):]


def rg_lru(xc, h0, wa, ba, wx, bx, lam):
    bsz, t, _ = xc.shape
    xb = xc.reshape(bsz, t, LRU_BLOCKS, LRU_BW)
    gate_a = jnp.einsum('btnc,ncd->btnd', xb, wa).reshape(bsz, t, LRU_W) + ba
    gate_x = jnp.einsum('btnc,ncd->btnd', xb, wx).reshape(bsz, t, LRU_W) + bx
    r = jax.nn.sigmoid(gate_a.astype(jnp.float32))
    i = jax.nn.sigmoid(gate_x.astype(jnp.float32))
    log_a = -LRU_C * r * jax.nn.softplus(-lam.astype(jnp.float32))
    a = jnp.exp(log_a)
    b = jnp.sqrt(-jnp.expm1(2.0 * log_a)) * (i * xc.astype(jnp.float32))
    b = b.at[:, 0].add(a[:, 0] * h0.astype(jnp.float32))

    def combine(lhs, rhs):
        return lhs[0] * rhs[0], rhs[0] * lhs[1] + rhs[1]

    _, h = lax.associative_scan(combine, (a, b), axis=1)
    return h.astype(xc.dtype), h[:, -1].astype(xc.dtype)


def ssd_scan(x, dt, a, bm, cm, h0):
    bsz, t = x.shape[:2]
    L = SSD_CHUNK if t % SSD_CHUNK == 0 else t
    nc = t // L
    f32 = jnp.float32
    xdt = (x.astype(f32) * dt[..., None]).reshape(bsz, nc, L, SSD_G, SSD_R, SSD_P)
    adt = (dt * a).reshape(bsz, nc, L, SSD_G, SSD_R)
    bm = bm.astype(f32).reshape(bsz, nc, L, SSD_G, SSD_N)
    cm = cm.astype(f32).reshape(bsz, nc, L, SSD_G, SSD_N)
    acs = jnp.cumsum(adt, axis=2)
    seg = acs[:, :, :, None] - acs[:, :, None, :]
    causal = jnp.tril(jnp.ones((L, L), dtype=bool))[:, :, None, None]
    decay = jnp.exp(jnp.where(causal, seg, -jnp.inf))
    cb = jnp.einsum('bclgn,bcsgn->bclsg', cm, bm)
    y_diag = jnp.einsum('bclsgr,bcsgrp->bclgrp', cb[..., None] * decay, xdt)
    decay_states = jnp.exp(acs[:, :, -1:] - acs)
    states = jnp.einsum('bclgn,bclgr,bclgrp->bcgrpn', bm, decay_states, xdt)
    chunk_decay = jnp.exp(acs[:, :, -1])

    def step(h, inp):
        dec, st = inp
        return dec[..., None, None] * h + st, h

    h0g = h0.astype(f32).reshape(bsz, SSD_G, SSD_R, SSD_P, SSD_N)
    h_last, h_in = lax.scan(step, h0g, (jnp.moveaxis(chunk_decay, 1, 0), jnp.moveaxis(states, 1, 0)))
    h_in = jnp.moveaxis(h_in, 0, 1)
    y_off = jnp.einsum('bclgn,bcgrpn,bclgr->bclgrp', cm, h_in, jnp.exp(acs))
    y = (y_diag + y_off).reshape(bsz, t, SSD_H, SSD_P)
    return y, h_last.reshape(bsz, SSD_H, SSD_P, SSD_N)


def stick_breaking(q, k, v, q_pos):
    z = jnp.einsum('bqhd,bkhd->bhqk', q, k).astype(jnp.float32) * (SB_DH ** -0.5)
    k_pos = jnp.arange(k.shape[1])
    mask = k_pos[None, :] < q_pos[:, None]
    log_1m = jnp.where(mask, jax.nn.log_sigmoid(-z), 0.0)
    after = lax.cumsum(log_1m, axis=3, reverse=True) - log_1m
    w = jnp.where(mask, jnp.exp(jax.nn.log_sigmoid(z) + after), 0.0)
    return jnp.einsum('bhqk,bkhd->bqhd', w.astype(v.dtype), v)


def stick_breaking_prompt(q, k, v):
    bsz, t = q.shape[:2]
    nb = t // SB_QBLOCK
    kb = k.reshape(bsz, nb, SB_QBLOCK, SB_H, SB_DH).swapaxes(0, 1)
    vb = v.reshape(bsz, nb, SB_QBLOCK, SB_H, SB_DH).swapaxes(0, 1)
    idx = jnp.arange(SB_QBLOCK)
    later = (idx[:, None] > idx[None, :]).astype(jnp.float32)
    diag_mask = idx[None, :] < idx[:, None]
    scale = SB_DH ** -0.5
    outs = []
    for qi in range(nb):
        qblk = q[:, qi * SB_QBLOCK:(qi + 1) * SB_QBLOCK]

        def step(carry, inp, qblk=qblk):
            acc, out = carry
            kk, vv, is_diag = inp
            z = jnp.einsum('bqhd,bkhd->bhqk', qblk, kk).astype(jnp.float32) * scale
            mask = jnp.logical_or(jnp.logical_not(is_diag), diag_mask)
            log_1m = jnp.where(mask, jax.nn.log_sigmoid(-z), 0.0)
            after = jnp.einsum('bhqj,js->bhqs', log_1m, later) + acc[..., None]
            w = jnp.where(mask, jnp.exp(jax.nn.log_sigmoid(z) + after), 0.0)
            out = out + jnp.einsum('bhqk,bkhd->bqhd', w.astype(vv.dtype), vv).astype(jnp.float32)
            return (acc + jnp.sum(log_1m, axis=-1), out), None

        init = (jnp.zeros((bsz, SB_H, SB_QBLOCK), jnp.float32),
                jnp.zeros((bsz, SB_QBLOCK, SB_H, SB_DH), jnp.float32))
        is_diag = jnp.arange(qi + 1) == qi
        (_, o), _ = lax.scan(step, init, (kb[:qi + 1], vb[:qi + 1], is_diag), reverse=True)
        outs.append(o)
    return jnp.concatenate(outs, axis=1).astype(v.dtype)


def hybrid_layer(x, p, st, past_k, past_v):
    bsz, t, _ = x.shape
    h = rms_norm(x, p['norm_pre'])
    proj = h @ p['w_in']
    lru_x, lru_g, ssd_z, ssd_xbc, ssd_dt, cf_in, cf_g, sb_qkv, sb_g, merge = split_cols(proj, IN_SPLITS)

    lru_xc, lru_conv_new = causal_dwconv(lru_x, st['lru_conv'], p['lru_conv_w'], p['lru_conv_b'])
    lru_h, lru_h_new = rg_lru(lru_xc, st['lru_h'], p['lru_wa'], p['lru_ba'], p['lru_wx'], p['lru_bx'], p['lru_lambda'])
    u_a = lru_h * jax.nn.silu(lru_g)

    xbc, ssd_conv_new = causal_dwconv(ssd_xbc, st['ssd_conv'], p['ssd_conv_w'], p['ssd_conv_b'])
    xbc = jax.nn.silu(xbc)
    sx, sbm, scm = split_cols(xbc, (SSD_INNER, SSD_G * SSD_N, SSD_G * SSD_N))
    dt = jax.nn.softplus(ssd_dt.astype(jnp.float32) + p['ssd_dt_bias'].astype(jnp.float32))
    a = -jnp.exp(p['ssd_a_log'].astype(jnp.float32))
    xh = sx.reshape(bsz, t, SSD_H, SSD_P)
    y, ssd_new = ssd_scan(xh, dt, a, sbm.reshape(bsz, t, SSD_G, SSD_N), scm.reshape(bsz, t, SSD_G, SSD_N), st['ssd'])
    y = y + p['ssd_d'].astype(jnp.float32)[:, None] * xh.astype(jnp.float32)
    u_b = rms_norm(y.astype(x.dtype).reshape(bsz, t, SSD_INNER) * jax.nn.silu(ssd_z), p['ssd_norm'])

    glu = cf_in[..., :CF_W] * jax.nn.sigmoid(cf_in[..., CF_W:])
    cfc, cf_conv_new = causal_dwconv(glu, st['cf_conv'], p['cf_conv_w'], p['cf_conv_b'])
    u_c = jax.nn.silu(layer_norm(cfc, p['cf_ln_g'], p['cf_ln_b'])) * jax.nn.silu(cf_g)

    q, k, v = [m.reshape(bsz, t, SB_H, SB_DH) for m in split_cols(sb_qkv, (SB_W, SB_W, SB_W))]
    if past_k is None:
        o = stick_breaking_prompt(q, k, v)
    else:
        k_all = jnp.concatenate([past_k.astype(k.dtype), k], axis=1)
        v_all = jnp.concatenate([past_v.astype(v.dtype), v], axis=1)
        q_pos = past_k.shape[1] + jnp.arange(t)
        o = stick_breaking(q, k_all, v_all, q_pos)
    u_d = o.reshape(bsz, t, SB_W) * jax.nn.silu(sb_g)

    u = jnp.stack([u_a, u_b, u_c, u_d], axis=2)
    y_br = jnp.einsum('btnc,ncd->btnd', u, p['w_down'])
    gates = jax.nn.sigmoid(merge.reshape(bsz, t, N_BRANCH, D_MODEL))
    mixed = jnp.sum(gates * y_br, axis=2)
    out = rms_norm(mixed @ p['w_out'], p['norm_post'])
    new = dict(lru_h=lru_h_new, lru_conv=lru_conv_new, ssd=ssd_new.astype(x.dtype), ssd_conv=ssd_conv_new,
               cf_conv=cf_conv_new, k=k, v=v)
    return x + out, new


def setup_inputs(seed: int = 0) -> dict:
    key = jax.random.key(seed)
    ks = jax.random.split(key, 32)
    f32 = jnp.float32

    def nrm(i, shape, scale):
        return scale * jax.random.normal(ks[i], shape, f32)

    u = jax.random.uniform(ks[0], (DEPTH, LRU_W), f32, minval=0.9, maxval=0.999)
    s = u ** (1.0 / LRU_C)
    lru_lambda = jnp.log(s) - jnp.log1p(-s)
    dt0 = jnp.exp(jax.random.uniform(ks[1], (DEPTH, SSD_H), f32, minval=math.log(0.001), maxval=math.log(0.1)))
    ssd_dt_bias = dt0 + jnp.log(-jnp.expm1(-dt0))
    ssd_a_log = jnp.log(jax.random.uniform(ks[2], (DEPTH, SSD_H), f32, minval=1.0, maxval=16.0))
    return {
        'x_prompt': nrm(3, (BATCH, SEQ, D_MODEL), 1.0),
        'x_sample': nrm(4, (DEC_BATCH, DEC_SEQ, D_MODEL), 1.0),
        'state_lru_h': nrm(5, (DEPTH, DEC_BATCH, LRU_W), 0.5),
        'state_lru_conv': nrm(6, (DEPTH, DEC_BATCH, LRU_CONV - 1, LRU_W), 1.0),
        'state_ssd': nrm(7, (DEPTH, DEC_BATCH, SSD_H, SSD_P, SSD_N), 0.1),
        'state_ssd_conv': nrm(8, (DEPTH, DEC_BATCH, SSD_CONV - 1, SSD_CONV_DIM), 1.0),
        'state_cf_conv': nrm(9, (DEPTH, DEC_BATCH, CF_CONV - 1, CF_W), 1.0),
        'cache_sb_k': nrm(10, (DEPTH, DEC_BATCH, PAST_LEN, SB_H, SB_DH), 1.0),
        'cache_sb_v': nrm(11, (DEPTH, DEC_BATCH, PAST_LEN, SB_H, SB_DH), 1.0),
        'norm_pre': 1.0 + nrm(12, (DEPTH, D_MODEL), 0.05),
        'norm_post': 1.0 + nrm(13, (DEPTH, D_MODEL), 0.05),
        'w_in': nrm(14, (DEPTH, D_MODEL, D_IN), D_MODEL ** -0.5),
        'lru_conv_w': nrm(15, (DEPTH, LRU_CONV, LRU_W), LRU_CONV ** -0.5),
        'lru_conv_b': nrm(16, (DEPTH, LRU_W), 0.01),
        'lru_wa': nrm(17, (DEPTH, LRU_BLOCKS, LRU_BW, LRU_BW), LRU_BW ** -0.5),
        'lru_ba': nrm(18, (DEPTH, LRU_W), 0.01),
        'lru_wx': nrm(19, (DEPTH, LRU_BLOCKS, LRU_BW, LRU_BW), LRU_BW ** -0.5),
        'lru_bx': nrm(20, (DEPTH, LRU_W), 0.01),
        'lru_lambda': lru_lambda,
        'ssd_conv_w': nrm(21, (DEPTH, SSD_CONV, SSD_CONV_DIM), SSD_CONV ** -0.5),
        'ssd_conv_b': nrm(22, (DEPTH, SSD_CONV_DIM), 0.01),
        'ssd_dt_bias': ssd_dt_bias,
        'ssd_a_log': ssd_a_log,
        'ssd_d': 1.0 + nrm(23, (DEPTH, SSD_H), 0.1),
        'ssd_norm': 1.0 + nrm(24, (DEPTH, SSD_INNER), 0.05),
        'cf_conv_w': nrm(25, (DEPTH, CF_CONV, CF_W), CF_CONV ** -0.5),
        'cf_conv_b': nrm(26, (DEPTH, CF_W), 0.01),
        'cf_ln_g': 1.0 + nrm(27, (DEPTH, CF_W), 0.05),
        'cf_ln_b': nrm(28, (DEPTH, CF_W), 0.01),
        'w_down': nrm(29, (DEPTH, N_BRANCH, BRANCH_W, D_MODEL), BRANCH_W ** -0.5),
        'w_out': nrm(30, (DEPTH, D_MODEL, D_MODEL), D_MODEL ** -0.5),
    }


def reference(x_prompt, x_sample, state_lru_h, state_lru_conv, state_ssd, state_ssd_conv, state_cf_conv,
              cache_sb_k, cache_sb_v, norm_pre, norm_post, w_in, lru_conv_w, lru_conv_b, lru_wa, lru_ba,
              lru_wx, lru_bx, lru_lambda, ssd_conv_w, ssd_conv_b, ssd_dt_bias, ssd_a_log, ssd_d, ssd_norm,
              cf_conv_w, cf_conv_b, cf_ln_g, cf_ln_b, w_down, w_out):
    def params(l):
        return dict(norm_pre=norm_pre[l], norm_post=norm_post[l], w_in=w_in[l], lru_conv_w=lru_conv_w[l],
                    lru_conv_b=lru_conv_b[l], lru_wa=lru_wa[l], lru_ba=lru_ba[l], lru_wx=lru_wx[l],
                    lru_bx=lru_bx[l], lru_lambda=lru_lambda[l], ssd_conv_w=ssd_conv_w[l],
                    ssd_conv_b=ssd_conv_b[l], ssd_dt_bias=ssd_dt_bias[l], ssd_a_log=ssd_a_log[l],
                    ssd_d=ssd_d[l], ssd_norm=ssd_norm[l], cf_conv_w=cf_conv_w[l], cf_conv_b=cf_conv_b[l],
                    cf_ln_g=cf_ln_g[l], cf_ln_b=cf_ln_b[l], w_down=w_down[l], w_out=w_out[l])

    bp = x_prompt.shape[0]
    dtp = x_prompt.dtype
    zero_state = dict(lru_h=jnp.zeros((bp, LRU_W), dtp),
                      lru_conv=jnp.zeros((bp, LRU_CONV - 1, LRU_W), dtp),
                      ssd=jnp.zeros((bp, SSD_H, SSD_P, SSD_N), dtp),
                      ssd_conv=jnp.zeros((bp, SSD_CONV - 1, SSD_CONV_DIM), dtp),
                      cf_conv=jnp.zeros((bp, CF_CONV - 1, CF_W), dtp))
    y_p = x_prompt
    p_new = []
    for l in range(DEPTH):
        y_p, s = hybrid_layer(y_p, params(l), zero_state, None, None)
        p_new.append(s)

    y_s = x_sample
    s_new = []
    for l in range(DEPTH):
        st = dict(lru_h=state_lru_h[l], lru_conv=state_lru_conv[l], ssd=state_ssd[l],
                  ssd_conv=state_ssd_conv[l], cf_conv=state_cf_conv[l])
        y_s, s = hybrid_layer(y_s, params(l), st, cache_sb_k[l], cache_sb_v[l])
        s_new.append(s)

    def stack(lst, name):
        return jnp.stack([d[name] for d in lst])

    return (y_p, y_s,
            stack(p_new, 'lru_h'), stack(s_new, 'lru_h'),
            stack(p_new, 'lru_conv'), stack(s_new, 'lru_conv'),
            stack(p_new, 'ssd'), stack(s_new, 'ssd'),
            stack(p_new, 'ssd_conv'), stack(s_new, 'ssd_conv'),
            stack(p_new, 'cf_conv'), stack(s_new, 'cf_conv'),
            stack(p_new, 'k'), stack(s_new, 'k'),
            stack(p_new, 'v'), stack(s_new, 'v'))
```

```python
import contextlib
import numpy as np
import ml_dtypes
import concourse.bass as bass
import concourse.mybir as mybir
from concourse.bass_utils import run_bass_kernel_spmd

F32 = mybir.dt.float32
BF16 = mybir.dt.bfloat16
AF = mybir.ActivationFunctionType
ALU = mybir.AluOpType

CENG = ("pe", "act", "dve", "pool")
NSLOT = 24

D = 1024
DIN = 10248
EPS = 1e-6
COL = dict(lru_x=0, lru_g=512, ssd_z=1024, ssd_x=1536, ssd_bc=2048, ssd_dt=2560, cf_a=2568, cf_b=3080,
           cf_g=3592, q=4104, k=4616, v=5128, sb_g=5640, merge=6152)
V_GPRE, V_LCW, V_LCB, V_LBA, V_LBX, V_LLAM = 0, 8, 24, 28, 32, 36
V_SCW, V_SCB, V_SD, V_SNORM = 40, 72, 80, 84
V_CCW, V_CCB, V_CLG, V_CLB = 88, 212, 216, 220
NV = 224
C_ID, C_U, C_TRI, C_NEG, C_CM, C_ONE, C_VALP, C_VALS = 0, 128, 256, 384, 896, 1024, 1152, 1153
NC = 1154
SV_H0, SV_LH, SV_SH, SV_CH = 0, 4, 16, 40
NSV = 160


def _rect(ap):
    t = ap.tensor
    dims = list(ap.ap)
    if str(ap.space) == "DRAM":
        lo = ap.offset
        hi = lo
        for st, n in dims:
            if st >= 0:
                hi += st * (n - 1)
            else:
                lo += st * (n - 1)
        return ("D:" + t.name, 0, 1, lo, hi + 1)
    pst, pn = dims[0]
    if pst == 0:
        p0 = 0
        base = ap.offset
    else:
        p0 = ap.offset // pst
        base = ap.offset - p0 * pst
    lo = base
    hi = base
    for st, n in dims[1:]:
        if st >= 0:
            hi += st * (n - 1)
        else:
            lo += st * (n - 1)
    esz = mybir.dt.size(ap.dtype)
    if str(ap.space) == "PSUM":
        b0 = (lo * esz) // 2048 * 2048
        b1 = ((hi + 1) * esz + 2047) // 2048 * 2048
        q0 = p0 // 32 * 32
        q1 = (p0 + pn + 31) // 32 * 32
        return ("P:" + t.name, q0, q1, b0, b1)
    return ("S:" + t.name, p0, p0 + pn, lo * esz, (hi + 1) * esz)


class _Rec:
    __slots__ = ("p0", "p1", "f0", "f1", "w", "r")

    def __init__(self, p0, p1, f0, f1):
        self.p0, self.p1, self.f0, self.f1 = p0, p1, f0, f1
        self.w = {}
        self.r = {}


def _merge(dst, src):
    for k, v in src.items():
        if dst.get(k, -1) < v:
            dst[k] = v


class Ins:
    __slots__ = ("eng", "fn", "deps", "kind", "idx", "slot", "seq", "waits", "sig")


class Prog:
    def __init__(self, nc):
        self.nc = nc
        self.ins = []
        self.track = {}
        self.cnt = {e: 0 for e in CENG}
        self.ndma = 0
        self.slot_seq = [0] * NSLOT
        import os
        self.maxins = int(os.environ.get("KMAXINS", "100000000"))

    def _access(self, ident, reads, writes):
        deps = {}
        rr = [_rect(a) for a in reads]
        ww = [_rect(a) for a in writes]
        for key, p0, p1, f0, f1 in rr:
            for rec in self.track.get(key, ()):
                if rec.p0 < p1 and p0 < rec.p1 and rec.f0 < f1 and f0 < rec.f1:
                    _merge(deps, rec.w)
                    if key[0] == "P":
                        for k2, v2 in rec.r.items():
                            if k2 != ident[0] and deps.get(k2, -1) < v2:
                                deps[k2] = v2
        for key, p0, p1, f0, f1 in ww:
            for rec in self.track.get(key, ()):
                if rec.p0 < p1 and p0 < rec.p1 and rec.f0 < f1 and f0 < rec.f1:
                    _merge(deps, rec.w)
                    _merge(deps, rec.r)
        k, v = ident
        for key, p0, p1, f0, f1 in rr:
            lst = self.track.setdefault(key, [])
            hit = False
            for rec in lst:
                if rec.p0 < p1 and p0 < rec.p1 and rec.f0 < f1 and f0 < rec.f1:
                    if rec.r.get(k, -1) < v:
                        rec.r[k] = v
                    if rec.p0 <= p0 and p1 <= rec.p1 and rec.f0 <= f0 and f1 <= rec.f1:
                        hit = True
            if not hit:
                rec = _Rec(p0, p1, f0, f1)
                rec.r[k] = v
                lst.append(rec)
                if len(lst) > 64:
                    self._collapse(key)
        for key, p0, p1, f0, f1 in ww:
            lst = self.track.setdefault(key, [])
            keep = [rec for rec in lst
                    if not (p0 <= rec.p0 and rec.p1 <= p1 and f0 <= rec.f0 and rec.f1 <= f1)]
            rec = _Rec(p0, p1, f0, f1)
            rec.w[k] = v
            keep.append(rec)
            self.track[key] = keep
            if len(keep) > 64:
                self._collapse(key)
        return deps

    def _collapse(self, key):
        keep = self.track[key]
        big = _Rec(min(r.p0 for r in keep), max(r.p1 for r in keep),
                   min(r.f0 for r in keep), max(r.f1 for r in keep))
        for r in keep:
            _merge(big.w, r.w)
            _merge(big.r, r.r)
        self.track[key] = [big]

    def op(self, eng, fn, reads, writes):
        if len(self.ins) >= self.maxins:
            return None
        i = Ins()
        i.eng, i.fn, i.kind = eng, fn, "op"
        i.idx = self.cnt[eng]
        self.cnt[eng] += 1
        i.deps = self._access((eng, i.idx), reads, writes)
        if eng == "pe":
            i.deps.pop("pe", None)
        self.ins.append(i)
        return i

    def dma(self, queue, out, in_, **kw):
        if len(self.ins) >= self.maxins:
            return None
        i = Ins()
        i.eng, i.kind = queue, "dma"
        i.slot = self.ndma % NSLOT
        self.ndma += 1
        i.seq = self.slot_seq[i.slot]
        self.slot_seq[i.slot] += 1
        i.fn = lambda e: e.dma_start(out=out, in_=in_, **kw)
        i.deps = self._access((("d", i.slot), i.seq), [in_], [out])
        if i.seq > 0:
            k = ("d", i.slot)
            if i.deps.get(k, -1) < i.seq - 1:
                i.deps[k] = i.seq - 1
        self.ins.append(i)
        return i

    def emit(self):
        nc = self.nc
        known = {e: {} for e in list(CENG) + ["sp"]}
        clock = {}
        need_sig = set()
        for i in self.ins:
            kn = known[i.eng]
            waits = []
            for k, v in i.deps.items():
                if kn.get(k, -1) >= v:
                    continue
                waits.append((k, v))
                _merge(kn, clock[(k, v)])
                if kn.get(k, -1) < v:
                    kn[k] = v
                need_sig.add((k, v))
            i.waits = waits
            if i.kind == "op":
                clock[(i.eng, i.idx)] = dict(kn)
            else:
                clock[(("d", i.slot), i.seq)] = dict(kn)
        sigcount = {e: 0 for e in CENG}
        semval = {}
        for i in self.ins:
            if i.kind == "op":
                if (i.eng, i.idx) in need_sig:
                    sigcount[i.eng] += 1
                    i.sig = True
                    semval[(i.eng, i.idx)] = sigcount[i.eng]
                else:
                    i.sig = False
            else:
                semval[(("d", i.slot), i.seq)] = 16 * (i.seq + 1)
        per = {e: [] for e in list(CENG) + ["sp"]}
        for i in self.ins:
            per[i.eng].append(i)
        self.stats = {e: len(per[e]) for e in per}
        self.stats["nwait"] = sum(len(i.waits) for i in self.ins)
        self.stats["sig"] = dict(sigcount)
        with contextlib.ExitStack() as es:
            sems = {}
            for e in CENG:
                sems[e] = es.enter_context(nc.semaphore("c_" + e))
            for s in range(NSLOT):
                sems[("d", s)] = es.enter_context(nc.semaphore("d%d" % s))
            block = es.enter_context(nc.Block())

            def run(engname):
                def body(eobj):
                    for i in per[engname]:
                        for k, v in i.waits:
                            eobj.wait_ge(sems[k], semval[(k, v)])
                        r = i.fn(eobj)
                        if i.kind == "dma":
                            r.then_inc(sems[("d", i.slot)], 16)
                        elif i.sig:
                            r.then_inc(sems[i.eng], 1)
                    if engname == "sp":
                        for s in range(NSLOT):
                            if self.slot_seq[s] > 0:
                                eobj.wait_ge(sems[("d", s)], 16 * self.slot_seq[s])
                return body

            block.tensor(run("pe"))
            block.scalar(run("act"))
            block.vector(run("dve"))
            block.gpsimd(run("pool"))
            block.sync(run("sp"))

    def mm(self, out, lhsT, rhs, start=True, stop=True, **kw):
        return self.op("pe", lambda e: e.matmul(out, lhsT, rhs, start=start, stop=stop, **kw),
                       [lhsT, rhs], [out])

    def tr(self, out, in_, ident):
        return self.op("pe", lambda e: e.transpose(out, in_, ident), [in_, ident], [out])

    def act(self, out, in_, func, bias=None, scale=1.0, accum_out=None):
        reads = [in_]
        kw = {}
        if bias is not None:
            kw["bias"] = bias
            if not isinstance(bias, (int, float)):
                reads.append(bias)
        if not isinstance(scale, (int, float)):
            reads.append(scale)
        writes = [out]
        if accum_out is not None:
            kw["accum_out"] = accum_out
            writes.append(accum_out)
        return self.op("act", lambda e: e.activation(out=out, in_=in_, func=func, scale=scale, **kw),
                       reads, writes)

    def tt(self, out, in0, in1, op, eng="dve"):
        return self.op(eng, lambda e: e.tensor_tensor(out=out, in0=in0, in1=in1, op=op), [in0, in1], [out])

    def ts(self, out, in0, s1, s2=None, op0=ALU.mult, op1=None, eng="dve"):
        reads = [in0]
        if not isinstance(s1, (int, float)):
            reads.append(s1)
        if s2 is not None and not isinstance(s2, (int, float)):
            reads.append(s2)
        if op1 is None:
            return self.op(eng, lambda e: e.tensor_scalar(out=out, in0=in0, scalar1=s1, scalar2=None, op0=op0),
                           reads, [out])
        return self.op(eng, lambda e: e.tensor_scalar(out=out, in0=in0, scalar1=s1, scalar2=s2, op0=op0, op1=op1),
                       reads, [out])

    def stt(self, out, in0, scalar, in1, op0, op1):
        reads = [in0, in1]
        if not isinstance(scalar, (int, float)):
            reads.append(scalar)
        return self.op("dve", lambda e: e.scalar_tensor_tensor(out=out, in0=in0, scalar=scalar, in1=in1,
                                                               op0=op0, op1=op1), reads, [out])

    def copy(self, out, in_, eng="dve"):
        if eng == "act":
            return self.op("act", lambda e: e.copy(out=out, in_=in_), [in_], [out])
        return self.op(eng, lambda e: e.tensor_copy(out=out, in_=in_), [in_], [out])

    def memset(self, ap, val, eng="dve"):
        return self.op(eng, lambda e: e.memset(ap, val), [], [ap])

    def scan(self, out, d0, d1, init):
        reads = [d0, d1]
        if not isinstance(init, (int, float)):
            reads.append(init)
        return self.op("dve", lambda e: e.tensor_tensor_scan(out=out, data0=d0, data1=d1, initial=init,
                                                             op0=ALU.mult, op1=ALU.add), reads, [out])


def fap(ap, dims):
    return bass.AP(ap.tensor, ap.offset, [list(ap.ap[0])] + [list(d) for d in dims])


class Cfg:
    def __init__(self, TP=8192, TS=1024, DEPTH=4, PAST=1024, SVALID=64, stop=99):
        self.TP, self.TS, self.DEPTH, self.PAST, self.SVALID = TP, TS, DEPTH, PAST, SVALID
        self.stop = stop


class Seq:
    pass


class _Stop(Exception):
    pass


class Builder:
    def __init__(self, cfg):
        self.cfg = cfg
        self.nc = bass.Bass("TRN2", target_bir_lowering=False)
        self.bank_rr = 0
        self.wrr = 0
        self.grr = 0

    def din(self, name, shape, dt=F32):
        return self.nc.dram_tensor(name, list(shape), dt, kind="ExternalInput").ap()

    def dout(self, name, shape, dt=F32):
        return self.nc.dram_tensor(name, list(shape), dt, kind="ExternalOutput").ap()

    def bank(self, n=1, lo=0, hi=8):
        if not hasattr(self, "_brr"):
            self._brr = {}
        rr = self._brr.get((lo, hi), lo)
        if n == 2 and (rr - lo) % 2:
            rr += 1
        if rr + n > hi:
            rr = lo
        self._brr[(lo, hi)] = rr + n
        return self.psum[:, rr * 512:(rr + n) * 512]

    def stage(self, n):
        if n >= self.cfg.stop:
            raise _Stop()

    def wbuf(self):
        b = self.t["w"][self.wrr % 3]
        self.wrr += 1
        return b

    def gt(self, w=512):
        b = self.G[:, self.grr % 4, 0:w]
        self.grr += 1
        return b

    def build(self):
        cfg = self.cfg
        nc = self.nc
        L, TP, TS, PAST = cfg.DEPTH, cfg.TP, cfg.TS, cfg.PAST
        d = {}
        d["xp"] = self.din("xp", [max(TP, 128), D])
        d["xs"] = self.din("xs", [128, D])
        d["w_in"] = self.din("w_in", [L, D, DIN])
        d["w_down"] = self.din("w_down", [L, 4, 512, D])
        d["w_out"] = self.din("w_out", [L, D, D])
        d["wab"] = self.din("wab", [L, 8, 128, 128])
        d["vecs"] = self.din("vecs", [L, 128, NV])
        d["vrep"] = self.din("vrep", [L, 128, 16])
        d["gpost"] = self.din("gpost", [L, D])
        d["consts"] = self.din("consts", [128, NC])
        d["svec"] = self.din("svec", [L, 128, NSV])
        d["sssd"] = self.din("sssd", [L, 128, 256])
        d["pk"] = self.din("pk", [L, 4, 128, PAST])
        d["pv"] = self.din("pv", [L, PAST, 512])
        o = {}
        o["yp"] = self.dout("yp", [max(TP, 128), D])
        o["ys"] = self.dout("ys", [128, D])
        for sfx, T in (("p", max(TP, 128)), ("s", 128)):
            o["lruh_" + sfx] = self.dout("lruh_" + sfx, [L, 128, 4])
            o["lruc_" + sfx] = self.dout("lruc_" + sfx, [L, 128, 12])
            o["ssd_" + sfx] = self.dout("ssd_" + sfx, [L, 128, 256])
            o["ssdc_" + sfx] = self.dout("ssdc_" + sfx, [L, 128, 24])
            o["cfc_" + sfx] = self.dout("cfc_" + sfx, [L, 128, 120])
            o["k_" + sfx] = self.dout("k_" + sfx, [L, T, 512])
            o["v_" + sfx] = self.dout("v_" + sfx, [L, T, 512])
        self.d, self.o = d, o
        TSC = max(TP, 128)
        self.winb = nc.dram_tensor("winb", [L, D, DIN], BF16).ap()
        self.wdnb = nc.dram_tensor("wdnb", [L, 4, 512, D], BF16).ap()
        self.woutb = nc.dram_tensor("woutb", [L, D, D], BF16).ap()
        self.kscr = nc.dram_tensor("kscr", [4, 128, TSC], BF16).ap()
        self.vscr = nc.dram_tensor("vscr", [TSC, 512], BF16).ap()

        with contextlib.ExitStack() as es:
            P = self.P = Prog(nc)

            def sb(name, shape, dt=F32):
                return es.enter_context(nc.sbuf_tensor(name, list(shape), dt))

            self.psum = es.enter_context(nc.psum_tensor("psum", [128, 4096], F32))
            TSm = self.TSm = max(TS, 128) if TP > 0 else 128
            t = self.t = {}
            t["cst"] = sb("cst", [128, NC])
            t["identb"] = sb("identb", [128, 128], BF16)
            t["negL"] = sb("negL", [128, 128], BF16)
            t["cmb"] = sb("cmb", [128, 128], BF16)
            t["onesm"] = sb("onesm", [128, 128], BF16)
            t["negone"] = sb("negone", [128, 2], BF16)
            t["vec"] = sb("vec", [128, NV])
            t["vrep"] = sb("vrept", [128, 16])
            t["gpost"] = sb("gpostt", [128, D])
            t["cl"] = sb("cl", [128, 4])
            t["aneg"] = sb("aneg", [128, 8])
            t["sm"] = sb("sm", [128, 64])
            t["dl"] = sb("dl", [128, 16, 128], BF16)
            t["ds"] = sb("ds", [128, 32, 128], BF16)
            t["dc"] = sb("dc", [128, 31, 128], BF16)
            t["wab"] = sb("wabt", [128, 8, 128], BF16)
            t["wdt"] = sb("wdt", [128, 8, 8], BF16)
            t["w"] = [sb("wbuf%d" % i, [128, 8, 512], BF16) for i in range(3)]
            t["hT"] = sb("hT", [128, 8, TSm], BF16)
            t["xin"] = [sb("xin%d" % i, [128, D]) for i in range(2)]
            t["mixed"] = sb("mixed", [128, 8, TSm])
            t["uT"] = sb("uT", [128, 4, TSm], BF16)
            t["sg"] = sb("sg", [128, 4, TSm], BF16)
            t["QT"] = sb("QT", [128, 4, TSm], BF16)
            self.F = sb("F", [128, 8, TSm])
            self.F16 = self.F.bitcast(BF16)
            self.G = sb("G", [128, 4, 512])
            self.H = sb("H", [128, 4, TSm + 32], BF16)
            self.junk = sb("junk", [128, D], BF16)
            if TSm < 1024:
                self.Xs = sb("Xs", [128, 3072])
                self.Xa = sb("Xa", [128, 8192])
                self.Xh = sb("Xh", [128, 4, 1024], BF16)
            t["wab32"] = sb("wab32", [128, 8, 128])
            t["wdt32"] = sb("wdt32", [128, 8, 8])
            t["lhist"] = sb("lhist", [128, 4, 3], BF16)
            t["shist"] = sb("shist", [128, 8, 3], BF16)
            t["chist"] = sb("chist", [128, 4, 30], BF16)
            t["hst"] = sb("hst", [128, 4])
            t["S"] = sb("S", [128, 256])
            t["S16"] = sb("S16", [128, 256], BF16)
            t["tail"] = sb("tail", [128, 160])
            t["svec"] = sb("svect", [128, NSV])
            t["dts"] = sb("dts", [128, 4, 64])
            t["xdt"] = sb("xdt", [128, 512], BF16)
            t["xdt2"] = sb("xdt2", [128, 512], BF16)
            t["btm"] = sb("btm", [128, 256], BF16)
            t["cdec"] = sb("cdec", [128, 512], BF16)
            t["mt"] = sb("mt", [128, 1024], BF16)
            t["v16"] = sb("v16", [128, 512], BF16)
            t["acc"] = sb("acc", [128, 32])
            t["eacc"] = sb("eacc", [128, 32])

            self.setup_consts()
            self.convert_weights()
            seqp = Seq()
            seqp.name, seqp.T, seqp.TS, seqp.QW = "p", TP, TS, min(512, TS)
            seqp.xin, seqp.y, seqp.nvalid, seqp.past, seqp.valcol = d["xp"], o["yp"], TP, 0, C_VALP
            seqs = Seq()
            seqs.name, seqs.T, seqs.TS, seqs.QW = "s", 128, 128, 128
            seqs.xin, seqs.y, seqs.nvalid, seqs.past, seqs.valcol = d["xs"], o["ys"], cfg.SVALID, PAST, C_VALS
            try:
                for l in range(L):
                    self.layer_setup(l)
                    if TP > 0:
                        self.run_seq(seqp, l)
                    self.run_seq(seqs, l)
            except _Stop:
                pass
            P.emit()
        return nc

    def f(self, i, w=None):
        w = self.TSm if w is None else w
        return self.F[:, i, 0:w]

    def h(self, i, w=None):
        w = self.TSm + 32 if w is None else w
        return self.H[:, i, 0:w]

    def ssd_scr(self, off, n):
        if self.TSm >= 1024:
            return bass.AP(self.F, 4 * self.TSm + off, [[8 * self.TSm, 128], [1, n]])
        return self.Xs[:, off:off + n]

    def att_scr(self, off, n):
        if self.TSm >= 1024:
            return bass.AP(self.F, off, [[8 * self.TSm, 128], [1, n]])
        return self.Xa[:, off:off + n]

    def att_h(self, i):
        if self.TSm >= 1024:
            return self.H[:, i, 0:1024]
        return self.Xh[:, i, :]

    def mixb(self, oc, o0, w):
        TSm = self.TSm
        return bass.AP(self.F16, 8 * TSm + oc * TSm + o0, [[16 * TSm, 128], [1, w]])

    def stage32(self, i):
        if self.TSm >= 1024:
            return bass.AP(self.t["mixed"], i * 4096, [[8 * self.TSm, 128], [1, 4096]])
        return self.Xa[:, i * 4096:(i + 1) * 4096]

    def convert_weights(self):
        P, d = self.P, self.d
        L = self.cfg.DEPTH
        n = 0
        jobs = []
        for l in range(L):
            c0 = 0
            while c0 < DIN:
                nc_ = min(512, DIN - c0)
                jobs.append((d["w_in"][l, :, c0:c0 + nc_].rearrange("(kc p) n -> p kc n", p=128),
                             self.winb[l, :, c0:c0 + nc_].rearrange("(kc p) n -> p kc n", p=128), 8, nc_))
                c0 += nc_
            for b in range(4):
                jobs.append((d["w_down"][l, b].rearrange("(kc p) n -> p kc n", p=128),
                             self.wdnb[l, b].rearrange("(kc p) n -> p kc n", p=128), 4, 1024))
            for hf in range(2):
                jobs.append((d["w_out"][l, :, hf * 512:(hf + 1) * 512].rearrange("(kc p) n -> p kc n", p=128),
                             self.woutb[l, :, hf * 512:(hf + 1) * 512].rearrange("(kc p) n -> p kc n", p=128), 8, 512))
        for (src, dst, a, b) in jobs:
            s32 = self.stage32(n % 2)
            s32v = fap(s32[:, 0:1], [[b, a], [1, b]])
            wb = self.t["w"][n % 3]
            wbv = fap(wb[:, 0, 0:1], [[b, a], [1, b]])
            P.dma("sp", s32v, src)
            P.copy(wbv, s32v, eng=("dve", "act", "pool")[n % 3])
            P.dma("sp", dst, wbv)
            n += 1

    def setup_consts(self):
        P, t = self.P, self.t
        cst = t["cst"]
        P.dma("sp", cst[:], self.d["consts"])
        P.copy(t["identb"][:], cst[:, C_ID:C_ID + 128])
        P.tt(self.G[:, 0, 0:128], cst[:, C_U:C_U + 128], cst[:, C_ID:C_ID + 128], ALU.add)
        P.ts(t["negL"][:], self.G[:, 0, 0:128], -1.0)
        P.copy(t["cmb"][:], cst[:, C_CM:C_CM + 128])
        P.ts(t["onesm"][:], cst[:, C_ONE:C_ONE + 128], 1.0 / 512.0)
        P.memset(t["negone"][:], -1.0)
        P.memset(t["tail"][:], 0.0)

    def layer_setup(self, l):
        P, t, d = self.P, self.t, self.d
        vec = t["vec"]
        P.dma("sp", vec[:], d["vecs"][l])
        P.dma("sp", t["vrep"][:], d["vrep"][l])
        gp = d["gpost"]
        P.dma("sp", t["gpost"][:], bass.AP(gp.tensor, l * D, [[0, 128], [1, D]]))
        P.dma("sp", t["wab32"][:], d["wab"][l].rearrange("i p m -> p i m"))
        P.copy(t["wab"][:], t["wab32"][:])
        P.dma("sp", t["wdt32"][:],
              d["w_in"][l, :, COL["ssd_dt"]:COL["ssd_dt"] + 8].rearrange("(kc p) n -> p kc n", p=128))
        P.copy(t["wdt"][:], t["wdt32"][:])
        sm = t["sm"]
        lam = vec[:, V_LLAM:V_LLAM + 4]
        P.act(sm[:, 0:4], lam, AF.Abs)
        P.act(sm[:, 4:8], sm[:, 0:4], AF.Exp, scale=-1.0)
        P.act(sm[:, 8:12], sm[:, 4:8], AF.Ln, bias=1.0)
        P.ts(sm[:, 12:16], lam, -1.0, 0.0, op0=ALU.mult, op1=ALU.max)
        P.tt(sm[:, 16:20], sm[:, 8:12], sm[:, 12:16], ALU.add)
        P.ts(t["cl"][:], sm[:, 16:20], -8.0)
        P.act(sm[:, 24:32], t["vrep"][:, 8:16], AF.Exp)
        P.ts(t["aneg"][:], sm[:, 24:32], -1.0)
        idf = t["cst"][:, C_ID:C_ID + 128]
        for i in range(16):
            P.ts(t["dl"][:, i, :], idf, vec[:, V_LCW + i:V_LCW + i + 1], eng="pool" if i % 2 else "dve")
        for i in range(32):
            P.ts(t["ds"][:, i, :], idf, vec[:, V_SCW + i:V_SCW + i + 1], eng="pool" if i % 2 else "dve")

    def load_w(self, l, c0, ncol=512):
        buf = self.wbuf()
        src = self.winb[l, :, c0:c0 + ncol].rearrange("(kc p) n -> p kc n", p=128)
        self.P.dma("sp", buf[:, :, 0:ncol], src)
        return buf

    def proj(self, wb, wc0, o0, w):
        P, t = self.P, self.t
        ps = self.bank()[:, 0:w]
        for k in range(8):
            P.mm(ps, wb[:, k, wc0:wc0 + 128], t["hT"][:, k, o0:o0 + w], start=(k == 0), stop=(k == 7))
        return ps

    def ttiles(self, TS):
        w = min(512, TS)
        return [(o0, w) for o0 in range(0, TS, w)]

    def rsqrt_rep(self, out, ps):
        P = self.P
        P.act(out, ps, AF.Ln, bias=EPS)
        P.act(out, out, AF.Exp, scale=-0.5)

    def run_seq(self, sq, l):
        P, t, d, o = self.P, self.t, self.d, self.o
        TS = sq.TS
        nst = sq.T // TS
        if sq.name == "p":
            for nm in ("lhist", "shist", "chist", "hst", "S", "S16"):
                P.memset(t[nm][:], 0.0, eng="pool")
        else:
            sv = t["svec"]
            P.dma("sp", sv[:], d["svec"][l])
            P.dma("sp", t["S"][:], d["sssd"][l])
            P.copy(t["S16"][:], t["S"][:])
            P.copy(t["hst"][:], sv[:, SV_H0:SV_H0 + 4])
            P.copy(t["lhist"][:], sv[:, SV_LH:SV_LH + 12].rearrange("p (c j) -> p c j", c=4))
            P.copy(t["shist"][:], sv[:, SV_SH:SV_SH + 24].rearrange("p (c j) -> p c j", c=8))
            P.copy(t["chist"][:], sv[:, SV_CH:SV_CH + 120].rearrange("p (c j) -> p c j", c=4))
        for st in range(nst):
            self.supertile(sq, l, st, last=(st == nst - 1))
        sfx = sq.name
        P.dma("sp", o["lruh_" + sfx][l], t["hst"][:])
        P.dma("sp", o["ssd_" + sfx][l], t["S"][:])
        P.dma("sp", o["lruc_" + sfx][l], t["tail"][:, 0:12])
        P.dma("sp", o["ssdc_" + sfx][l], t["tail"][:, 16:40])
        P.dma("sp", o["cfc_" + sfx][l], t["tail"][:, 40:160])

    def merge(self, sq, l, b, TS):
        P, t, d = self.P, self.t, self.d
        wd = self.wbuf()
        wdv = fap(wd[:], [[1024, 4], [1, 1024]])
        P.dma("sp", wdv, self.wdnb[l, b].rearrange("(kc p) n -> p kc n", p=128))
        for half in range(2):
            wm = self.load_w(l, COL["merge"] + b * 1024 + half * 512)
            for ocl in range(4):
                oc = half * 4 + ocl
                for (o0, w) in self.ttiles(TS):
                    py = self.bank()[:, 0:w]
                    for k in range(4):
                        P.mm(py, bass.AP(wd, k * 1024 + oc * 128, [[4096, 128], [1, 128]]),
                             t["uT"][:, k, o0:o0 + w], start=(k == 0), stop=(k == 3))
                    pg = self.proj(wm, ocl * 128, o0, w)
                    g1 = self.gt(w)
                    P.act(g1, pg, AF.Sigmoid)
                    if b == 0:
                        P.tt(t["mixed"][:, oc, o0:o0 + w], py, g1, ALU.mult)
                    else:
                        g2 = self.gt(w)
                        P.tt(g2, py, g1, ALU.mult)
                        dst = self.mixb(oc, o0, w) if b == 3 else t["mixed"][:, oc, o0:o0 + w]
                        P.tt(dst, t["mixed"][:, oc, o0:o0 + w], g2, ALU.add, eng="pool")

    def supertile(self, sq, l, st, last):
        P, t, d, o = self.P, self.t, self.d, self.o
        TS = sq.TS
        nblk = TS // 128
        t0 = st * TS
        tts = self.ttiles(TS)
        vec, hT, sm, sg, uT = t["vec"], t["hT"], t["sm"], t["sg"], t["uT"]
        f, h = self.f, self.h
        nv = min(sq.nvalid - t0, TS)
        xsrc = sq.xin if l == 0 else sq.y
        cst = t["cst"]
        ident = cst[:, C_ID:C_ID + 128]
        sfx = sq.name

        def in_tile(c0, c1, o0, w):
            return o0 <= c0 and c1 <= o0 + w

        xn = fap(self.G[:, 0, :], [[1, 1024]])
        for b in range(nblk):
            xt = t["xin"][b % 2]
            P.dma("sp", xt[:], xsrc[t0 + b * 128:t0 + (b + 1) * 128, :])
            ssq = sm[:, 32:33]
            P.act(self.junk[:], xt[:], AF.Square, accum_out=ssq)
            rs = sm[:, 34:35]
            P.act(rs, ssq, AF.Ln, bias=EPS, scale=1.0 / D)
            P.act(rs, rs, AF.Exp, scale=-0.5)
            P.ts(xn, xt[:], rs)
            for half in range(2):
                ps = self.bank()
                for j in range(4):
                    jj = half * 4 + j
                    P.tr(ps[:, j * 128:(j + 1) * 128], xn[:, jj * 128:(jj + 1) * 128], ident)
                gp = fap(vec[:, V_GPRE + half * 4:V_GPRE + half * 4 + 4], [[1, 4], [0, 128]])
                P.tt(hT[:, half * 4:half * 4 + 4, b * 128:(b + 1) * 128],
                     ps.rearrange("p (j q) -> p j q", j=4), gp, ALU.mult)

        self.stage(1)
        wb = self.load_w(l, COL["lru_g"])
        for c in range(4):
            for (o0, w) in tts:
                ps = self.proj(wb, c * 128, o0, w)
                P.act(sg[:, c, o0:o0 + w], ps, AF.Silu)
        self.stage(2)
        wb = self.load_w(l, COL["lru_x"])
        for c in range(4):
            lxh = h(c % 2)
            P.copy(lxh[:, 0:3], t["lhist"][:, c, :])
            for (o0, w) in tts:
                ps = self.proj(wb, c * 128, o0, w)
                P.copy(lxh[:, 3 + o0:3 + o0 + w], ps, eng="act")
                if last and in_tile(nv - 3, nv, o0, w):
                    P.copy(t["tail"][:, c * 3:c * 3 + 3], ps[:, nv - 3 - o0:nv - o0])
            P.copy(t["lhist"][:, c, :], lxh[:, TS:TS + 3])
            self.stage(2.1)
            xc, xc16 = f(0, TS), h(2, TS)
            for (o0, w) in tts:
                psc = self.bank()[:, 0:w]
                for k in range(4):
                    P.mm(psc, t["dl"][:, k * 4 + c, :], lxh[:, o0 + k:o0 + k + w], start=(k == 0), stop=(k == 3))
                P.act(xc[:, o0:o0 + w], psc, AF.Identity, bias=vec[:, V_LCB + c:V_LCB + c + 1])
                P.copy(xc16[:, o0:o0 + w], xc[:, o0:o0 + w])
            self.stage(2.2)
            r_, i_ = f(1, TS), f(2, TS)
            for (o0, w) in tts:
                pa = self.bank()[:, 0:w]
                P.mm(pa, t["wab"][:, c, :], xc16[:, o0:o0 + w])
                P.act(r_[:, o0:o0 + w], pa, AF.Sigmoid, bias=vec[:, V_LBA + c:V_LBA + c + 1])
                px = self.bank()[:, 0:w]
                P.mm(px, t["wab"][:, 4 + c, :], xc16[:, o0:o0 + w])
                P.act(i_[:, o0:o0 + w], px, AF.Sigmoid, bias=vec[:, V_LBX + c:V_LBX + c + 1])
            self.stage(2.3)
            a_, th, m_ = f(3, TS), f(4, TS), f(5, TS)
            clc = t["cl"][:, c:c + 1]
            P.act(a_, r_, AF.Exp, scale=clc)
            P.act(th, r_, AF.Tanh, scale=clc)
            self.stage(2.4)
            P.tt(m_, a_, a_, ALU.mult)
            P.stt(m_, m_, 1.0, th, ALU.add, ALU.mult)
            self.stage(2.5)
            P.act(m_, m_, AF.Sqrt, scale=-1.0)
            P.tt(i_, i_, xc, ALU.mult)
            P.tt(i_, i_, m_, ALU.mult)
            self.stage(2.6)
            hh = f(1, TS)
            P.scan(hh, a_, i_, t["hst"][:, c:c + 1])
            P.copy(t["hst"][:, c:c + 1], hh[:, nv - 1:nv])
            self.stage(2.7)
            P.tt(uT[:, c, 0:TS], hh, sg[:, c, 0:TS], ALU.mult)
        self.stage(3)
        self.merge(sq, l, 0, TS)

        self.stage(4)
        wb = self.load_w(l, COL["ssd_z"])
        for c in range(4):
            for (o0, w) in tts:
                ps = self.proj(wb, c * 128, o0, w)
                P.act(sg[:, c, o0:o0 + w], ps, AF.Silu)
        bc = t["QT"]
        for grp in range(2):
            wb = self.load_w(l, COL["ssd_x"] + grp * 512)
            for c in range(4):
                c8 = grp * 4 + c
                sxh = h(c8 % 2)
                P.copy(sxh[:, 0:3], t["shist"][:, c8, :])
                for (o0, w) in tts:
                    ps = self.proj(wb, c * 128, o0, w)
                    P.copy(sxh[:, 3 + o0:3 + o0 + w], ps, eng="act")
                    if last and in_tile(nv - 3, nv, o0, w):
                        P.copy(t["tail"][:, 16 + c8 * 3:16 + c8 * 3 + 3], ps[:, nv - 3 - o0:nv - o0])
                P.copy(t["shist"][:, c8, :], sxh[:, TS:TS + 3])
                for (o0, w) in tts:
                    psc = self.bank()[:, 0:w]
                    for k in range(4):
                        P.mm(psc, t["ds"][:, k * 8 + c8, :], sxh[:, o0 + k:o0 + k + w], start=(k == 0), stop=(k == 3))
                    dst = f(c)[:, o0:o0 + w] if grp == 0 else bc[:, c, o0:o0 + w]
                    P.act(dst, psc, AF.Silu, bias=vec[:, V_SCB + c8:V_SCB + c8 + 1])
        self.stage(5)
        dts = t["dts"]
        pd = self.bank()
        for b in range(nblk):
            for k in range(8):
                P.mm(pd[:, b * 8:b * 8 + 8], hT[:, k, b * 128:(b + 1) * 128], t["wdt"][:, k, :],
                     start=(k == 0), stop=(k == 7))
        nb8 = nblk * 8
        dx, dabs, dln, dt_, adt = dts[:, 0, 0:nb8], dts[:, 1, 0:nb8], dts[:, 2, 0:nb8], dts[:, 3, 0:nb8], dts[:, 1, 0:nb8]
        P.tt(dx.rearrange("p (b h) -> p b h", h=8), pd[:, 0:nb8].rearrange("p (b h) -> p b h", h=8),
             fap(t["vrep"][:, 0:8], [[0, nblk], [1, 8]]), ALU.add)
        P.act(dabs, dx, AF.Abs)
        P.act(dln, dabs, AF.Exp, scale=-1.0)
        P.act(dln, dln, AF.Ln, bias=1.0)
        P.ts(dabs, dx, 0.0, op0=ALU.max)
        P.tt(dt_, dabs, dln, ALU.add)
        P.ts(dt_, dt_, cst[:, sq.valcol:sq.valcol + 1])
        P.tt(adt.rearrange("p (b h) -> p b h", h=8), dt_.rearrange("p (b h) -> p b h", h=8),
             fap(t["aneg"][:], [[0, nblk], [1, 8]]), ALU.mult)
        self.stage(6)
        tri = cst[:, C_TRI:C_TRI + 128]
        Umat = cst[:, C_U:C_U + 128]
        ones = cst[:, C_ONE:C_ONE + 128]
        S, S16 = t["S"], t["S16"]
        for b in range(nblk):
            bs = slice(b * 128, (b + 1) * 128)
            dtb = dts[:, 3, b * 8:b * 8 + 8]
            adtb = dts[:, 1, b * 8:b * 8 + 8]
            psx = self.bank()
            for c in range(4):
                P.tr(psx[:, c * 128:(c + 1) * 128], f(c)[:, bs], ident)
            psx3 = psx.rearrange("p (h q) -> p h q", h=8)
            P.tt(t["xdt"][:].rearrange("p (h q) -> p h q", h=8), psx3, fap(dtb, [[1, 8], [0, 64]]), ALU.mult)
            pds = self.bank()
            P.mm(pds[:, 0:8], Umat, adtb)
            P.mm(pds[:, 8:16], ones, adtb)
            P.act(sm[:, 40:56], pds[:, 0:16], AF.Exp)
            P.tt(sm[:, 56:64], sm[:, 40:48], dtb, ALU.mult)
            P.tt(t["xdt2"][:].rearrange("p (h q) -> p h q", h=8), psx3, fap(sm[:, 56:64], [[1, 8], [0, 64]]), ALU.mult)
            pbt = self.bank()
            for cb in range(2):
                P.mm(pbt[:, cb * 128:(cb + 1) * 128], bc[:, cb, bs], t["identb"][:])
            P.copy(t["btm"][:], pbt[:, 0:256], eng="act")
            pcb = self.bank(2)
            for g_ in range(4):
                cc, gl = g_ // 2, g_ % 2
                P.mm(pcb[:, gl * 512 + cc * 128:gl * 512 + (cc + 1) * 128], bc[gl * 64:(gl + 1) * 64, cc, bs],
                     bc[gl * 64:(gl + 1) * 64, 2 + cc, bs])
            rall = self.ssd_scr(0, 1024)
            dec = self.ssd_scr(1024, 1024)
            P.tt(rall.rearrange("p (h q) -> p h q", h=8), fap(tri, [[0, 8], [1, 128]]),
                 fap(adtb, [[1, 8], [0, 128]]), ALU.mult)
            pseg = self.bank(2)
            for hf in range(2):
                P.mm(pseg[:, hf * 512:(hf + 1) * 512], Umat, rall[:, hf * 512:(hf + 1) * 512], start=True, stop=False)
                P.mm(pseg[:, hf * 512:(hf + 1) * 512], ident,
                     cst[:, C_NEG:C_NEG + 512], start=False, stop=True)
            P.act(dec, pseg, AF.Exp)
            for cc in range(2):
                P.tt(t["mt"][:, cc * 512:(cc + 1) * 512].rearrange("p (g r q) -> p g r q", g=2, r=2),
                     dec[:, cc * 512:(cc + 1) * 512].rearrange("p (g r q) -> p g r q", g=2, r=2),
                     fap(pcb[:, cc * 128:cc * 128 + 1], [[512, 2], [0, 2], [1, 128]]), ALU.mult)
            ea = self.ssd_scr(2048, 1024)
            for gl in range(2):
                pe_ = self.bank()
                rsel = bass.AP(rall.tensor, rall.offset + 2 * gl * 128, [list(rall.ap[0]), [512, 2], [128, 2], [1, 128]])
                P.mm(pe_, ones, rsel)
                hs = slice(gl * 64, (gl + 1) * 64)
                eav = ea[hs, gl * 512:(gl + 1) * 512]
                P.act(eav, pe_[hs, :], AF.Exp)
                P.tt(t["cdec"][hs, :].rearrange("p (c r q) -> p c r q", c=2, r=2),
                     eav.rearrange("p (c r q) -> p c r q", c=2, r=2),
                     fap(bc[hs, 2, bs], [[self.TSm, 2], [0, 2], [1, 128]]), ALU.mult)
            py = self.bank(2)
            for c in range(4):
                cc, gl = c // 2, c % 2
                for r in range(2):
                    hh_ = 2 * c + r
                    outp = py[r * 64:(r + 1) * 64, gl * 512 + cc * 128:gl * 512 + (cc + 1) * 128]
                    P.mm(outp, t["xdt"][:, hh_ * 64:(hh_ + 1) * 64], t["mt"][:, hh_ * 128:(hh_ + 1) * 128],
                         start=True, stop=False)
                    P.mm(outp, S16[gl * 64:(gl + 1) * 64, (cc * 2 + r) * 64:(cc * 2 + r + 1) * 64],
                         t["cdec"][gl * 64:(gl + 1) * 64, (cc * 2 + r) * 128:(cc * 2 + r + 1) * 128],
                         start=False, stop=True)
            for c in range(4):
                cc, gl = c // 2, c % 2
                P.stt(f(c)[:, bs], f(c)[:, bs], vec[:, V_SD + c:V_SD + c + 1],
                      py[:, gl * 512 + cc * 128:gl * 512 + (cc + 1) * 128], ALU.mult, ALU.add)
            pst = self.bank()
            for g_ in range(4):
                cc, gl = g_ // 2, g_ % 2
                P.mm(pst[gl * 64:(gl + 1) * 64, cc * 128:(cc + 1) * 128], t["btm"][:, g_ * 64:(g_ + 1) * 64],
                     t["xdt2"][:, g_ * 128:(g_ + 1) * 128])
            for gl in range(2):
                hs = slice(gl * 64, (gl + 1) * 64)
                sv4 = S[hs, :].rearrange("p (c r q) -> p c r q", c=2, r=2)
                P.tt(sv4, sv4, fap(sm[hs, 48 + 2 * gl:48 + 2 * gl + 1], [[4, 2], [1, 2], [0, 64]]), ALU.mult)
            P.tt(S[:], S[:], pst[:, 0:256], ALU.add)
            P.copy(S16[:], S[:])
        self.stage(7)
        for c in range(4):
            P.tt(f(c, TS), f(c, TS), sg[:, c, 0:TS], ALU.mult)
            P.act(h(c, TS), f(c, TS), AF.Square)
        for (o0, w) in tts:
            pss = self.bank()[:, 0:w]
            for c in range(4):
                P.mm(pss, t["onesm"][:], h(c)[:, o0:o0 + w], start=(c == 0), stop=(c == 3))
            rstd = self.gt(w)
            self.rsqrt_rep(rstd, pss)
            for c in range(4):
                P.stt(uT[:, c, o0:o0 + w], f(c)[:, o0:o0 + w], vec[:, V_SNORM + c:V_SNORM + c + 1], rstd,
                      ALU.mult, ALU.mult)
        self.merge(sq, l, 1, TS)

        self.stage(8)
        wb = self.load_w(l, COL["cf_g"])
        for c in range(4):
            for (o0, w) in tts:
                ps = self.proj(wb, c * 128, o0, w)
                P.act(sg[:, c, o0:o0 + w], ps, AF.Silu)
        wbb = self.load_w(l, COL["cf_b"])
        wba = self.load_w(l, COL["cf_a"])
        idf = ident
        for c in range(4):
            for k in range(31):
                P.ts(t["dc"][:, k, :], idf, vec[:, V_CCW + k * 4 + c:V_CCW + k * 4 + c + 1],
                     eng="pool" if k % 2 else "dve")
            gh = h(c % 2)
            P.copy(gh[:, 0:30], t["chist"][:, c, :])
            for (o0, w) in tts:
                pb = self.proj(wbb, c * 128, o0, w)
                sig = self.gt(w)
                P.act(sig, pb, AF.Sigmoid)
                pa = self.proj(wba, c * 128, o0, w)
                P.tt(gh[:, 30 + o0:30 + o0 + w], pa, sig, ALU.mult)
                if last and in_tile(nv - 30, nv, o0, w):
                    P.tt(t["tail"][:, 40 + c * 30:40 + c * 30 + 30], pa[:, nv - 30 - o0:nv - o0],
                         sig[:, nv - 30 - o0:nv - o0], ALU.mult)
            P.copy(t["chist"][:, c, :], gh[:, TS:TS + 30])
            for (o0, w) in tts:
                psc = self.bank()[:, 0:w]
                for k in range(31):
                    P.mm(psc, t["dc"][:, k, :], gh[:, o0 + k:o0 + k + w], start=(k == 0), stop=(k == 30))
                P.act(f(c)[:, o0:o0 + w], psc, AF.Identity, bias=vec[:, V_CCB + c:V_CCB + c + 1])
        for c in range(4):
            P.copy(h(c, TS), f(c, TS), eng="pool" if c % 2 else "dve")
        for (o0, w) in tts:
            pm = self.bank()[:, 0:w]
            for c in range(4):
                P.mm(pm, t["onesm"][:], h(c)[:, o0:o0 + w], start=(c == 0), stop=(c == 3))
            for c in range(4):
                P.tt(f(c)[:, o0:o0 + w], f(c)[:, o0:o0 + w], pm, ALU.subtract)
        for c in range(4):
            P.act(h(c, TS), f(c, TS), AF.Square)
        for (o0, w) in tts:
            pv_ = self.bank()[:, 0:w]
            for c in range(4):
                P.mm(pv_, t["onesm"][:], h(c)[:, o0:o0 + w], start=(c == 0), stop=(c == 3))
            rstd = self.gt(w)
            self.rsqrt_rep(rstd, pv_)
            for c in range(4):
                P.tt(f(c)[:, o0:o0 + w], f(c)[:, o0:o0 + w], rstd, ALU.mult)
        for c in range(4):
            P.act(f(4 + c % 2, TS), f(c, TS), AF.Silu, scale=vec[:, V_CLG + c:V_CLG + c + 1],
                  bias=vec[:, V_CLB + c:V_CLB + c + 1])
            P.tt(uT[:, c, 0:TS], f(4 + c % 2, TS), sg[:, c, 0:TS], ALU.mult)
        self.merge(sq, l, 2, TS)

        self.stage(9)
        QT = t["QT"]
        wb = self.load_w(l, COL["q"])
        for c in range(4):
            for (o0, w) in tts:
                ps = self.proj(wb, c * 128, o0, w)
                P.act(QT[:, c, o0:o0 + w], ps, AF.Copy, scale=0.125)
        wb = self.load_w(l, COL["k"])
        for c in range(4):
            kt = h(c, TS)
            for (o0, w) in tts:
                ps = self.proj(wb, c * 128, o0, w)
                P.copy(kt[:, o0:o0 + w], ps, eng="act" if c % 2 else "dve")
            P.dma("sp", self.kscr[c, :, t0:t0 + TS], kt)
        for b in range(nblk):
            pk = self.bank()
            for k in range(8):
                P.mm(pk, hT[:, k, b * 128:(b + 1) * 128], wb[:, k, :], start=(k == 0), stop=(k == 7))
            kv = self.gt(512)
            P.copy(kv, pk, eng="act")
            P.dma("sp", o["k_" + sfx][l, t0 + b * 128:t0 + (b + 1) * 128, :], kv)
        wb = self.load_w(l, COL["v"])
        for b in range(nblk):
            pv_ = self.bank()
            for k in range(8):
                P.mm(pv_, hT[:, k, b * 128:(b + 1) * 128], wb[:, k, :], start=(k == 0), stop=(k == 7))
            kv = self.gt(512)
            P.copy(kv, pv_, eng="act")
            P.dma("sp", o["v_" + sfx][l, t0 + b * 128:t0 + (b + 1) * 128, :], kv)
            P.copy(t["v16"][:], pv_)
            P.dma("sp", self.vscr[t0 + b * 128:t0 + (b + 1) * 128, :], t["v16"][:])
        wb = self.load_w(l, COL["sb_g"])
        for c in range(4):
            for (o0, w) in tts:
                ps = self.proj(wb, c * 128, o0, w)
                P.act(sg[:, c, o0:o0 + w], ps, AF.Silu)

        self.stage(10)
        QW = sq.QW
        nsub = QW // 128
        for qt in range(TS // QW):
            self.attention(sq, l, t0, qt)
        self.stage(11)
        self.merge(sq, l, 3, TS)

        self.stage(12)
        wo = []
        for half in range(2):
            buf = self.wbuf()
            P.dma("sp", buf[:], self.woutb[l, :, half * 512:(half + 1) * 512].rearrange("(kc p) n -> p kc n", p=128))
            wo.append(buf)
        ot = fap(self.G[:, 0, :], [[1, 1024]])
        for b in range(nblk):
            xt = t["xin"][b % 2]
            P.dma("sp", xt[:], xsrc[t0 + b * 128:t0 + (b + 1) * 128, :])
            for half in range(2):
                po = self.bank()
                for k in range(8):
                    P.mm(po, self.mixb(k, b * 128, 128), wo[half][:, k, :], start=(k == 0), stop=(k == 7))
                P.copy(ot[:, half * 512:(half + 1) * 512], po, eng="act")
            ssq = sm[:, 36:37]
            P.act(self.junk[:], ot, AF.Square, accum_out=ssq)
            rs = sm[:, 38:39]
            P.act(rs, ssq, AF.Ln, bias=EPS, scale=1.0 / D)
            P.act(rs, rs, AF.Exp, scale=-0.5)
            P.stt(ot, ot, rs, t["gpost"][:], ALU.mult, ALU.mult)
            P.tt(ot, ot, xt[:], ALU.add)
            P.dma("sp", sq.y[t0 + b * 128:t0 + (b + 1) * 128, :], ot)

    def attention(self, sq, l, t0, qt):
        P, t, d = self.P, self.t, self.d
        QW = sq.QW
        nsub = QW // 128
        QT, sg, uT = t["QT"], t["sg"], t["uT"]
        q0 = qt * QW
        gq0 = t0 + q0
        acc, eacc = t["acc"], t["eacc"]
        TSm = self.TSm
        oacc = self.att_scr(2048, nsub * 512)
        tmpo = self.G[:, 3, :]
        P.memset(acc[:, 0:nsub * 8], 0.0)
        P.memset(eacc[:, 0:nsub * 8], 1.0)
        P.memset(oacc, 0.0, eng="pool")
        spans = []
        if sq.past == 0:
            assert QW == 512
            nsp = gq0 // 512 + 1
            for sp_ in range(nsp - 1, -1, -1):
                spans.append(("cur", sp_ * 512, 512, sp_ == nsp - 1))
        else:
            spans.append(("cur", 0, QW, True))
            for sp_ in range(sq.past // 512 - 1, -1, -1):
                spans.append(("past", sp_ * 512, 512, False))
        for (kind, k0, klen, diag) in spans:
            kvb = self.wbuf()
            nkb = klen // 128
            if kind == "cur":
                P.dma("sp", kvb[:, 0:4, 0:klen], self.kscr[:, :, k0:k0 + klen].rearrange("c p n -> p c n"))
                P.dma("sp", kvb[:, 4:4 + nkb, :], self.vscr[k0:k0 + klen, :].rearrange("(j p) n -> p j n", p=128))
            else:
                stg = self.att_scr(4096, 4096)
                sk = fap(stg[:, 0:1], [[512, 4], [1, klen]])
                svv = fap(stg[:, 2048:2049], [[512, nkb], [1, 512]])
                P.dma("sp", sk, d["pk"][l, :, :, k0:k0 + klen].rearrange("c p n -> p c n"))
                P.dma("sp", svv, d["pv"][l, k0:k0 + klen, :].rearrange("(j p) n -> p j n", p=128))
                P.copy(kvb[:, 0:4, 0:klen], sk)
                P.copy(kvb[:, 4:4 + nkb, :], svv, eng="act")
            for j in range(nkb - 1, -1, -1):
                c0 = j * 128 if diag else 0
                wq = QW - c0
                sub0 = c0 // 128
                pacc = self.bank(1, 6, 8)
                for hp in range(4):
                    pz = self.bank(2, 0, 4)
                    et = self.att_scr((hp % 2) * 1024, 1024)
                    spt = self.att_h(hp % 2)
                    wt = self.att_h(2 + hp % 2)
                    for r in range(2):
                        rs_ = slice(r * 64, (r + 1) * 64)
                        P.mm(pz[:, r * 512 + c0:r * 512 + QW], kvb[rs_, hp, j * 128:(j + 1) * 128],
                             QT[rs_, hp, q0 + c0:q0 + QW])
                    pzv = fap(pz[:, c0:c0 + 1], [[512, 2], [1, wq]])
                    etv = fap(et[:, c0:c0 + 1], [[QW, 2], [1, wq]])
                    spv = fap(spt[:, c0:c0 + 1], [[QW, 2], [1, wq]])
                    wtv = fap(wt[:, c0:c0 + 1], [[QW, 2], [1, wq]])
                    P.act(etv, pzv, AF.Exp)
                    P.act(spv, etv, AF.Ln, bias=1.0)
                    if diag:
                        dv = fap(spt[:, c0:c0 + 1], [[QW, 2], [1, 128]])
                        P.tt(dv, dv, fap(t["cmb"][:, 0:1], [[0, 2], [1, 128]]), ALU.mult)
                    for r in range(2):
                        P.mm(pz[:, r * 512 + c0:r * 512 + QW], t["negL"][:], spt[:, r * QW + c0:(r + 1) * QW],
                             start=False, stop=True, skip_group_check=True)
                    P.act(wtv, pzv, AF.Exp)
                    if diag:
                        dv = fap(wt[:, c0:c0 + 1], [[QW, 2], [1, 128]])
                        P.tt(dv, dv, fap(t["cmb"][:, 0:1], [[0, 2], [1, 128]]), ALU.mult)
                    po = self.bank(1, 4, 6)
                    for r in range(2):
                        hd = 2 * hp + r
                        for sub in range(sub0, nsub):
                            qs = slice(r * QW + sub * 128, r * QW + (sub + 1) * 128)
                            P.mm(pacc[:, sub * 8 + hd:sub * 8 + hd + 1], spt[:, qs], t["negone"][:, 0:1])
                            P.mm(po[:, (sub * 2 + r) * 64:(sub * 2 + r + 1) * 64], wt[:, qs],
                                 kvb[:, 4 + j, hd * 64:(hd + 1) * 64])
                    ns = nsub - sub0
                    pov = fap(po[:, sub0 * 128:sub0 * 128 + 1], [[128, ns], [64, 2], [1, 64]])
                    eav = fap(eacc[:, sub0 * 8 + 2 * hp:sub0 * 8 + 2 * hp + 1], [[8, ns], [1, 2], [0, 64]])
                    tv = fap(tmpo[:, 0:1], [[128, ns], [64, 2], [1, 64]])
                    P.tt(tv, pov, eav, ALU.mult)
                    ov = fap(oacc[:, sub0 * 512 + 2 * hp * 64:sub0 * 512 + 2 * hp * 64 + 1], [[512, ns], [64, 2], [1, 64]])
                    P.tt(ov, ov, tv, ALU.add, eng="pool")
                a_ = acc[:, sub0 * 8:nsub * 8]
                P.tt(a_, a_, pacc[:, sub0 * 8:nsub * 8], ALU.add)
                P.act(eacc[:, sub0 * 8:nsub * 8], a_, AF.Exp)
        ident = t["cst"][:, C_ID:C_ID + 128]
        for sub in range(nsub):
            pt = self.bank(1, 4, 8)
            for c in range(4):
                P.tr(pt[:, c * 128:(c + 1) * 128], oacc[:, sub * 512 + c * 128:sub * 512 + (c + 1) * 128], ident)
            cs = slice(q0 + sub * 128, q0 + (sub + 1) * 128)
            P.tt(uT[:, :, cs], pt.rearrange("p (c q) -> p c q", c=4), sg[:, :, cs], ALU.mult)


def make_consts():
    c = np.zeros((128, NC), np.float32)
    j = np.arange(128)[:, None]
    s = np.arange(128)[None, :]
    c[:, C_ID:C_ID + 128] = (j == s)
    c[:, C_U:C_U + 128] = (j > s)
    c[:, C_TRI:C_TRI + 128] = (j <= s)
    for rep in range(4):
        c[:, C_NEG + rep * 128:C_NEG + (rep + 1) * 128] = np.where(s < j, -30000.0, 0.0)
    c[:, C_CM:C_CM + 128] = (j < s)
    c[:, C_ONE:C_ONE + 128] = 1.0
    c[:, C_VALP] = 1.0
    return c


def fm(v, nch):
    return np.ascontiguousarray(np.asarray(v, np.float32).reshape(nch, 128).T)


def host_prep(inp, cfg):
    L = cfg.DEPTH
    f32 = np.float32
    vecs = np.zeros((L, 128, NV), f32)
    vrep = np.zeros((L, 128, 16), f32)
    wab = np.zeros((L, 8, 128, 128), f32)
    for l in range(L):
        v = vecs[l]
        v[:, V_GPRE:V_GPRE + 8] = fm(inp["norm_pre"][l], 8)
        for k in range(4):
            v[:, V_LCW + k * 4:V_LCW + k * 4 + 4] = fm(inp["lru_conv_w"][l, k], 4)
            v[:, V_SCW + k * 8:V_SCW + k * 8 + 8] = fm(inp["ssd_conv_w"][l, k], 8)
        v[:, V_LCB:V_LCB + 4] = fm(inp["lru_conv_b"][l], 4)
        v[:, V_LBA:V_LBA + 4] = fm(inp["lru_ba"][l], 4)
        v[:, V_LBX:V_LBX + 4] = fm(inp["lru_bx"][l], 4)
        v[:, V_LLAM:V_LLAM + 4] = fm(inp["lru_lambda"][l], 4)
        v[:, V_SCB:V_SCB + 8] = fm(inp["ssd_conv_b"][l], 8)
        v[:, V_SD:V_SD + 4] = fm(np.repeat(np.asarray(inp["ssd_d"][l]), 64), 4)
        v[:, V_SNORM:V_SNORM + 4] = fm(inp["ssd_norm"][l], 4)
        for k in range(31):
            v[:, V_CCW + k * 4:V_CCW + k * 4 + 4] = fm(inp["cf_conv_w"][l, k], 4)
        v[:, V_CCB:V_CCB + 4] = fm(inp["cf_conv_b"][l], 4)
        v[:, V_CLG:V_CLG + 4] = fm(inp["cf_ln_g"][l], 4)
        v[:, V_CLB:V_CLB + 4] = fm(inp["cf_ln_b"][l], 4)
        vrep[l, :, 0:8] = np.asarray(inp["ssd_dt_bias"][l])[None, :]
        vrep[l, :, 8:16] = np.asarray(inp["ssd_a_log"][l])[None, :]
        for which, nm in enumerate(("lru_wa", "lru_wx")):
            w = np.asarray(inp[nm][l])
            for c in range(4):
                wab[l, which * 4 + c, 0:64, 0:64] = w[2 * c]
                wab[l, which * 4 + c, 64:128, 64:128] = w[2 * c + 1]
    consts = make_consts()
    consts[:cfg.SVALID, C_VALS] = 1.0
    shared = dict(w_in=np.ascontiguousarray(inp["w_in"], f32), w_down=np.ascontiguousarray(inp["w_down"], f32),
                  w_out=np.ascontiguousarray(inp["w_out"], f32), wab=wab, vecs=vecs, vrep=vrep,
                  gpost=np.ascontiguousarray(inp["norm_post"], f32), consts=consts)
    return shared


def sample_inputs(inp, cfg, s):
    L = cfg.DEPTH
    f32 = np.float32
    SV = cfg.SVALID
    xs = np.zeros((128, D), f32)
    xs[:SV] = inp["x_sample"][s]
    svec = np.zeros((L, 128, NSV), f32)
    sssd = np.zeros((L, 128, 256), f32)
    pk = np.zeros((L, 4, 128, cfg.PAST), f32)
    for l in range(L):
        svec[l, :, SV_H0:SV_H0 + 4] = fm(inp["state_lru_h"][l, s], 4)
        lc = np.asarray(inp["state_lru_conv"][l, s])
        svec[l, :, SV_LH:SV_LH + 12] = lc.T.reshape(4, 128, 3).transpose(1, 0, 2).reshape(128, 12)
        sc = np.asarray(inp["state_ssd_conv"][l, s])
        svec[l, :, SV_SH:SV_SH + 24] = sc.T.reshape(8, 128, 3).transpose(1, 0, 2).reshape(128, 24)
        cc = np.asarray(inp["state_cf_conv"][l, s])
        svec[l, :, SV_CH:SV_CH + 120] = cc.T.reshape(4, 128, 30).transpose(1, 0, 2).reshape(128, 120)
        st = np.asarray(inp["state_ssd"][l, s])
        st = st.reshape(2, 2, 2, 64, 64)
        sssd[l] = st.transpose(1, 4, 0, 2, 3).reshape(128, 256)
        k = np.asarray(inp["cache_sb_k"][l, s]).reshape(cfg.PAST, 512)
        pk[l] = k.T.reshape(4, 128, cfg.PAST)
    pv = np.ascontiguousarray(np.asarray(inp["cache_sb_v"][:, s]).reshape(L, cfg.PAST, 512), f32)
    return dict(xs=xs, svec=svec, sssd=sssd, pk=pk, pv=pv)


def unpack_states(res, sfx, L):
    lruh = res["lruh_" + sfx].transpose(0, 2, 1).reshape(L, 512)
    lruc = res["lruc_" + sfx].reshape(L, 128, 4, 3).transpose(0, 3, 2, 1).reshape(L, 3, 512)
    ssdc = res["ssdc_" + sfx][:, :, 0:24].reshape(L, 128, 8, 3).transpose(0, 3, 2, 1).reshape(L, 3, 1024)
    cfc = res["cfc_" + sfx].reshape(L, 128, 4, 30).transpose(0, 3, 2, 1).reshape(L, 30, 512)
    ssd = res["ssd_" + sfx].reshape(L, 2, 64, 2, 2, 64)
    ssd = ssd.transpose(0, 3, 1, 4, 5, 2).reshape(L, 8, 64, 64)
    return lruh, lruc, ssd, ssdc, cfc


_NC_CACHE = {}


def run_cfg(inp, cfg, n_cores=8):
    key = (cfg.TP, cfg.TS, cfg.DEPTH, cfg.PAST, cfg.SVALID)
    if key not in _NC_CACHE:
        _NC_CACHE[key] = Builder(cfg).build()
    nc = _NC_CACHE[key]
    shared = host_prep(inp, cfg)
    L = cfg.DEPTH
    nb = inp["x_prompt"].shape[0] if cfg.TP > 0 else 0
    nsmp = inp["x_sample"].shape[0]
    in_maps = []
    for c in range(n_cores):
        m = dict(shared)
        if cfg.TP > 0:
            m["xp"] = np.ascontiguousarray(inp["x_prompt"][(c // 2) % nb], np.float32)
        else:
            m["xp"] = np.zeros((128, D), np.float32)
        m.update(sample_inputs(inp, cfg, c % nsmp))
        in_maps.append(m)
    res = run_bass_kernel_spmd(nc, in_maps, core_ids=list(range(n_cores))).results
    SV = cfg.SVALID
    outs = {}
    if cfg.TP > 0:
        pcs = [2 * b for b in range(nb)]
        outs["y_p"] = np.stack([res[c]["yp"] for c in pcs])
        st = [unpack_states(res[c], "p", L) for c in pcs]
        for i, nm in enumerate(("lru_h_p", "lru_conv_p", "ssd_p", "ssd_conv_p", "cf_conv_p")):
            outs[nm] = np.stack([s[i] for s in st], axis=1)
        outs["k_p"] = np.stack([res[c]["k_p"].reshape(L, cfg.TP, 8, 64) for c in pcs], axis=1)
        outs["v_p"] = np.stack([res[c]["v_p"].reshape(L, cfg.TP, 8, 64) for c in pcs], axis=1)
    scs = list(range(min(nsmp, n_cores)))
    outs["y_s"] = np.stack([res[c]["ys"][:SV] for c in scs])
    st = [unpack_states(res[c], "s", L) for c in scs]
    for i, nm in enumerate(("lru_h_s", "lru_conv_s", "ssd_s", "ssd_conv_s", "cf_conv_s")):
        outs[nm] = np.stack([s[i] for s in st], axis=1)
    outs["k_s"] = np.stack([res[c]["k_s"][:, :SV].reshape(L, SV, 8, 64) for c in scs], axis=1)
    outs["v_s"] = np.stack([res[c]["v_s"][:, :SV].reshape(L, SV, 8, 64) for c in scs], axis=1)
    return outs


ORDER = ("y_p", "y_s", "lru_h_p", "lru_h_s", "lru_conv_p", "lru_conv_s", "ssd_p", "ssd_s",
         "ssd_conv_p", "ssd_conv_s", "cf_conv_p", "cf_conv_s", "k_p", "k_s", "v_p", "v_s")


def kernel(**inputs):
    inp = {k: np.asarray(v) for k, v in inputs.items()}
    cfg = Cfg(TP=inp["x_prompt"].shape[1], TS=1024, DEPTH=inp["w_in"].shape[0],
              PAST=inp["cache_sb_k"].shape[2], SVALID=inp["x_sample"].shape[1])
    outs = run_cfg(inp, cfg, 8)
    return tuple(np.ascontiguousarray(outs[k], dtype=np.float32) for k in ORDER)
```

```python
import contextlib
import numpy as np
import ml_dtypes
import concourse.bass as bass
import concourse.mybir as mybir
from concourse.bass_utils import run_bass_kernel_spmd

F32 = mybir.dt.float32
BF16 = mybir.dt.bfloat16
AF = mybir.ActivationFunctionType
ALU = mybir.AluOpType

CENG = ("pe", "act", "dve", "pool")
NSLOT = 24

D = 1024
DIN = 10248
EPS = 1e-6
COL = dict(lru_x=0, lru_g=512, ssd_z=1024, ssd_x=1536, ssd_bc=2048, ssd_dt=2560, cf_a=2568, cf_b=3080,
           cf_g=3592, q=4104, k=4616, v=5128, sb_g=5640, merge=6152)
V_GPRE, V_LCW, V_LCB, V_LBA, V_LBX, V_LLAM = 0, 8, 24, 28, 32, 36
V_SCW, V_SCB, V_SD, V_SNORM = 40, 72, 80, 84
V_CCW, V_CCB, V_CLG, V_CLB = 88, 212, 216, 220
NV = 224
C_ID, C_U, C_TRI, C_NEG, C_CM, C_ONE, C_VALP, C_VALS = 0, 128, 256, 384, 896, 1024, 1152, 1153
NC = 1154
SV_H0, SV_LH, SV_SH, SV_CH = 0, 4, 16, 40
NSV = 160


def _rect(ap):
    t = ap.tensor
    dims = list(ap.ap)
    if str(ap.space) == "DRAM":
        lo = ap.offset
        hi = lo
        for st, n in dims:
            if st >= 0:
                hi += st * (n - 1)
            else:
                lo += st * (n - 1)
        return ("D:" + t.name, 0, 1, lo, hi + 1)
    pst, pn = dims[0]
    if pst == 0:
        p0 = 0
        base = ap.offset
    else:
        p0 = ap.offset // pst
        base = ap.offset - p0 * pst
    lo = base
    hi = base
    for st, n in dims[1:]:
        if st >= 0:
            hi += st * (n - 1)
        else:
            lo += st * (n - 1)
    esz = mybir.dt.size(ap.dtype)
    if str(ap.space) == "PSUM":
        b0 = (lo * esz) // 2048 * 2048
        b1 = ((hi + 1) * esz + 2047) // 2048 * 2048
        q0 = p0 // 32 * 32
        q1 = (p0 + pn + 31) // 32 * 32
        return ("P:" + t.name, q0, q1, b0, b1)
    return ("S:" + t.name, p0, p0 + pn, lo * esz, (hi + 1) * esz)


class _Rec:
    __slots__ = ("p0", "p1", "f0", "f1", "w", "r")

    def __init__(self, p0, p1, f0, f1):
        self.p0, self.p1, self.f0, self.f1 = p0, p1, f0, f1
        self.w = {}
        self.r = {}


def _merge(dst, src):
    for k, v in src.items():
        if dst.get(k, -1) < v:
            dst[k] = v


class Ins:
    __slots__ = ("eng", "fn", "deps", "kind", "idx", "slot", "seq", "waits", "sig")


class Prog:
    def __init__(self, nc):
        self.nc = nc
        self.ins = []
        self.track = {}
        self.cnt = {e: 0 for e in CENG}
        self.ndma = 0
        self.slot_seq = [0] * NSLOT
        import os
        self.maxins = int(os.environ.get("KMAXINS", "100000000"))

    def _access(self, ident, reads, writes):
        deps = {}
        rr = [_rect(a) for a in reads]
        ww = [_rect(a) for a in writes]
        for key, p0, p1, f0, f1 in rr:
            for rec in self.track.get(key, ()):
                if rec.p0 < p1 and p0 < rec.p1 and rec.f0 < f1 and f0 < rec.f1:
                    _merge(deps, rec.w)
                    if key[0] == "P":
                        for k2, v2 in rec.r.items():
                            if k2 != ident[0] and deps.get(k2, -1) < v2:
                                deps[k2] = v2
        for key, p0, p1, f0, f1 in ww:
            for rec in self.track.get(key, ()):
                if rec.p0 < p1 and p0 < rec.p1 and rec.f0 < f1 and f0 < rec.f1:
                    _merge(deps, rec.w)
                    _merge(deps, rec.r)
        k, v = ident
        for key, p0, p1, f0, f1 in rr:
            lst = self.track.setdefault(key, [])
            hit = False
            for rec in lst:
                if rec.p0 < p1 and p0 < rec.p1 and rec.f0 < f1 and f0 < rec.f1:
                    if rec.r.get(k, -1) < v:
                        rec.r[k] = v
                    if rec.p0 <= p0 and p1 <= rec.p1 and rec.f0 <= f0 and f1 <= rec.f1:
                        hit = True
            if not hit:
                rec = _Rec(p0, p1, f0, f1)
                rec.r[k] = v
                lst.append(rec)
                if len(lst) > 64:
                    self._collapse(key)
        for key, p0, p1, f0, f1 in ww:
            lst = self.track.setdefault(key, [])
            keep = [rec for rec in lst
                    if not (p0 <= rec.p0 and rec.p1 <= p1 and f0 <= rec.f0 and rec.f1 <= f1)]
            rec = _Rec(p0, p1, f0, f1)
            rec.w[k] = v
            keep.append(rec)
            self.track[key] = keep
            if len(keep) > 64:
                self._collapse(key)
        return deps

    def _collapse(self, key):
        keep = self.track[key]
        big = _Rec(min(r.p0 for r in keep), max(r.p1 for r in keep),
                   min(r.f0 for r in keep), max(r.f1 for r in keep))
        for r in keep:
            _merge(big.w, r.w)
            _merge(big.r, r.r)
        self.track[key] = [big]

    def op(self, eng, fn, reads, writes):
        if len(self.ins) >= self.maxins:
            return None
        i = Ins()
        i.eng, i.fn, i.kind = eng, fn, "op"
        i.idx = self.cnt[eng]
        self.cnt[eng] += 1
        i.deps = self._access((eng, i.idx), reads, writes)
        if eng == "pe":
            i.deps.pop("pe", None)
        self.ins.append(i)
        return i

    def dma(self, queue, out, in_, **kw):
        if len(self.ins) >= self.maxins:
            return None
        i = Ins()
        i.eng, i.kind = queue, "dma"
        i.slot = self.ndma % NSLOT
        self.ndma += 1
        i.seq = self.slot_seq[i.slot]
        self.slot_seq[i.slot] += 1
        i.fn = lambda e: e.dma_start(out=out, in_=in_, **kw)
        i.deps = self._access((("d", i.slot), i.seq), [in_], [out])
        if i.seq > 0:
            k = ("d", i.slot)
            if i.deps.get(k, -1) < i.seq - 1:
                i.deps[k] = i.seq - 1
        self.ins.append(i)
        return i

    def emit(self):
        nc = self.nc
        known = {e: {} for e in list(CENG) + ["sp"]}
        clock = {}
        need_sig = set()
        for i in self.ins:
            kn = known[i.eng]
            waits = []
            for k, v in i.deps.items():
                if kn.get(k, -1) >= v:
                    continue
                waits.append((k, v))
                _merge(kn, clock[(k, v)])
                if kn.get(k, -1) < v:
                    kn[k] = v
                need_sig.add((k, v))
            i.waits = waits
            if i.kind == "op":
                clock[(i.eng, i.idx)] = dict(kn)
            else:
                clock[(("d", i.slot), i.seq)] = dict(kn)
        sigcount = {e: 0 for e in CENG}
        semval = {}
        for i in self.ins:
            if i.kind == "op":
                if (i.eng, i.idx) in need_sig:
                    sigcount[i.eng] += 1
                    i.sig = True
                    semval[(i.eng, i.idx)] = sigcount[i.eng]
                else:
                    i.sig = False
            else:
                semval[(("d", i.slot), i.seq)] = 16 * (i.seq + 1)
        per = {e: [] for e in list(CENG) + ["sp"]}
        for i in self.ins:
            per[i.eng].append(i)
        self.stats = {e: len(per[e]) for e in per}
        self.stats["nwait"] = sum(len(i.waits) for i in self.ins)
        self.stats["sig"] = dict(sigcount)
        with contextlib.ExitStack() as es:
            sems = {}
            for e in CENG:
                sems[e] = es.enter_context(nc.semaphore("c_" + e))
            for s in range(NSLOT):
                sems[("d", s)] = es.enter_context(nc.semaphore("d%d" % s))
            block = es.enter_context(nc.Block())

            def run(engname):
                def body(eobj):
                    for i in per[engname]:
                        for k, v in i.waits:
                            eobj.wait_ge(sems[k], semval[(k, v)])
                        r = i.fn(eobj)
                        if i.kind == "dma":
                            r.then_inc(sems[("d", i.slot)], 16)
                        elif i.sig:
                            r.then_inc(sems[i.eng], 1)
                    if engname == "sp":
                        for s in range(NSLOT):
                            if self.slot_seq[s] > 0:
                                eobj.wait_ge(sems[("d", s)], 16 * self.slot_seq[s])
                return body

            block.tensor(run("pe"))
            block.scalar(run("act"))
            block.vector(run("dve"))
            block.gpsimd(run("pool"))
            block.sync(run("sp"))

    def mm(self, out, lhsT, rhs, start=True, stop=True, **kw):
        return self.op("pe", lambda e: e.matmul(out, lhsT, rhs, start=start, stop=stop, **kw),
                       [lhsT, rhs], [out])

    def tr(self, out, in_, ident):
        return self.op("pe", lambda e: e.transpose(out, in_, ident), [in_, ident], [out])

    def act(self, out, in_, func, bias=None, scale=1.0, accum_out=None):
        reads = [in_]
        kw = {}
        if bias is not None:
            kw["bias"] = bias
            if not isinstance(bias, (int, float)):
                reads.append(bias)
        if not isinstance(scale, (int, float)):
            reads.append(scale)
        writes = [out]
        if accum_out is not None:
            kw["accum_out"] = accum_out
            writes.append(accum_out)
        return self.op("act", lambda e: e.activation(out=out, in_=in_, func=func, scale=scale, **kw),
                       reads, writes)

    def tt(self, out, in0, in1, op, eng="dve"):
        return self.op(eng, lambda e: e.tensor_tensor(out=out, in0=in0, in1=in1, op=op), [in0, in1], [out])

    def ts(self, out, in0, s1, s2=None, op0=ALU.mult, op1=None, eng="dve"):
        reads = [in0]
        if not isinstance(s1, (int, float)):
            reads.append(s1)
        if s2 is not None and not isinstance(s2, (int, float)):
            reads.append(s2)
        if op1 is None:
            return self.op(eng, lambda e: e.tensor_scalar(out=out, in0=in0, scalar1=s1, scalar2=None, op0=op0),
                           reads, [out])
        return self.op(eng, lambda e: e.tensor_scalar(out=out, in0=in0, scalar1=s1, scalar2=s2, op0=op0, op1=op1),
                       reads, [out])

    def stt(self, out, in0, scalar, in1, op0, op1):
        reads = [in0, in1]
        if not isinstance(scalar, (int, float)):
            reads.append(scalar)
        return self.op("dve", lambda e: e.scalar_tensor_tensor(out=out, in0=in0, scalar=scalar, in1=in1,
                                                               op0=op0, op1=op1), reads, [out])

    def copy(self, out, in_, eng="dve"):
        if eng == "act":
            return self.op("act", lambda e: e.copy(out=out, in_=in_), [in_], [out])
        return self.op(eng, lambda e: e.tensor_copy(out=out, in_=in_), [in_], [out])

    def memset(self, ap, val, eng="dve"):
        return self.op(eng, lambda e: e.memset(ap, val), [], [ap])

    def scan(self, out, d0, d1, init):
        reads = [d0, d1]
        if not isinstance(init, (int, float)):
            reads.append(init)
        return self.op("dve", lambda e: e.tensor_tensor_scan(out=out, data0=d0, data1=d1, initial=init,
                                                             op0=ALU.mult, op1=ALU.add), reads, [out])


def fap(ap, dims):
    return bass.AP(ap.tensor, ap.offset, [list(ap.ap[0])] + [list(d) for d in dims])


class Cfg:
    def __init__(self, TP=8192, TS=1024, DEPTH=4, PAST=1024, SVALID=64, stop=99):
        self.TP, self.TS, self.DEPTH, self.PAST, self.SVALID = TP, TS, DEPTH, PAST, SVALID
        self.stop = stop


class Seq:
    pass


class _Stop(Exception):
    pass


class Builder:
    def __init__(self, cfg):
        self.cfg = cfg
        self.nc = bass.Bass("TRN2", target_bir_lowering=False)
        self.bank_rr = 0
        self.wrr = 0
        self.grr = 0

    def din(self, name, shape, dt=F32):
        return self.nc.dram_tensor(name, list(shape), dt, kind="ExternalInput").ap()

    def dout(self, name, shape, dt=F32):
        return self.nc.dram_tensor(name, list(shape), dt, kind="ExternalOutput").ap()

    def bank(self, n=1, lo=0, hi=8):
        if not hasattr(self, "_brr"):
            self._brr = {}
        rr = self._brr.get((lo, hi), lo)
        if n == 2 and (rr - lo) % 2:
            rr += 1
        if rr + n > hi:
            rr = lo
        self._brr[(lo, hi)] = rr + n
        return self.psum[:, rr * 512:(rr + n) * 512]

    def stage(self, n):
        if n >= self.cfg.stop:
            raise _Stop()

    def wbuf(self):
        b = self.t["w"][self.wrr % 3]
        self.wrr += 1
        return b

    def gt(self, w=512):
        b = self.G[:, self.grr % 4, 0:w]
        self.grr += 1
        return b

    def build(self):
        cfg = self.cfg
        nc = self.nc
        L, TP, TS, PAST = cfg.DEPTH, cfg.TP, cfg.TS, cfg.PAST
        d = {}
        d["xp"] = self.din("xp", [max(TP, 128), D])
        d["xs"] = self.din("xs", [128, D])
        d["w_in"] = self.din("w_in", [L, D, DIN])
        d["w_down"] = self.din("w_down", [L, 4, 512, D])
        d["w_out"] = self.din("w_out", [L, D, D])
        d["wab"] = self.din("wab", [L, 8, 128, 128])
        d["vecs"] = self.din("vecs", [L, 128, NV])
        d["vrep"] = self.din("vrep", [L, 128, 16])
        d["gpost"] = self.din("gpost", [L, D])
        d["consts"] = self.din("consts", [128, NC])
        d["svec"] = self.din("svec", [L, 128, NSV])
        d["sssd"] = self.din("sssd", [L, 128, 256])
        d["pk"] = self.din("pk", [L, 4, 128, PAST])
        d["pv"] = self.din("pv", [L, PAST, 512])
        o = {}
        o["yp"] = self.dout("yp", [max(TP, 128), D])
        o["ys"] = self.dout("ys", [128, D])
        for sfx, T in (("p", max(TP, 128)), ("s", 128)):
            o["lruh_" + sfx] = self.dout("lruh_" + sfx, [L, 128, 4])
            o["lruc_" + sfx] = self.dout("lruc_" + sfx, [L, 128, 12])
            o["ssd_" + sfx] = self.dout("ssd_" + sfx, [L, 128, 256])
            o["ssdc_" + sfx] = self.dout("ssdc_" + sfx, [L, 128, 24])
            o["cfc_" + sfx] = self.dout("cfc_" + sfx, [L, 128, 120])
            o["k_" + sfx] = self.dout("k_" + sfx, [L, T, 512])
            o["v_" + sfx] = self.dout("v_" + sfx, [L, T, 512])
        self.d, self.o = d, o
        TSC = max(TP, 128)
        self.winb = nc.dram_tensor("winb", [L, D, DIN], BF16).ap()
        self.wdnb = nc.dram_tensor("wdnb", [L, 4, 512, D], BF16).ap()
        self.woutb = nc.dram_tensor("woutb", [L, D, D], BF16).ap()
        self.kscr = nc.dram_tensor("kscr", [4, 128, TSC], BF16).ap()
        self.vscr = nc.dram_tensor("vscr", [TSC, 512], BF16).ap()

        with contextlib.ExitStack() as es:
            P = self.P = Prog(nc)

            def sb(name, shape, dt=F32):
                return es.enter_context(nc.sbuf_tensor(name, list(shape), dt))

            self.psum = es.enter_context(nc.psum_tensor("psum", [128, 4096], F32))
            TSm = self.TSm = max(TS, 128) if TP > 0 else 128
            t = self.t = {}
            t["cst"] = sb("cst", [128, NC])
            t["identb"] = sb("identb", [128, 128], BF16)
            t["negL"] = sb("negL", [128, 128], BF16)
            t["cmb"] = sb("cmb", [128, 128], BF16)
            t["onesm"] = sb("onesm", [128, 128], BF16)
            t["negone"] = sb("negone", [128, 2], BF16)
            t["vec"] = sb("vec", [128, NV])
            t["vrep"] = sb("vrept", [128, 16])
            t["gpost"] = sb("gpostt", [128, D])
            t["cl"] = sb("cl", [128, 4])
            t["aneg"] = sb("aneg", [128, 8])
            t["sm"] = sb("sm", [128, 64])
            t["dl"] = sb("dl", [128, 16, 128], BF16)
            t["ds"] = sb("ds", [128, 32, 128], BF16)
            t["dc"] = sb("dc", [128, 31, 128], BF16)
            t["wab"] = sb("wabt", [128, 8, 128], BF16)
            t["wdt"] = sb("wdt", [128, 8, 8], BF16)
            t["w"] = [sb("wbuf%d" % i, [128, 8, 512], BF16) for i in range(3)]
            t["hT"] = sb("hT", [128, 8, TSm], BF16)
            t["xin"] = [sb("xin%d" % i, [128, D]) for i in range(2)]
            t["mixed"] = sb("mixed", [128, 8, TSm])
            t["uT"] = sb("uT", [128, 4, TSm], BF16)
            t["sg"] = sb("sg", [128, 4, TSm], BF16)
            t["QT"] = sb("QT", [128, 4, TSm], BF16)
            self.F = sb("F", [128, 8, TSm])
            self.F16 = self.F.bitcast(BF16)
            self.G = sb("G", [128, 4, 512])
            self.H = sb("H", [128, 4, TSm + 32], BF16)
            self.junk = sb("junk", [128, D], BF16)
            if TSm < 1024:
                self.Xs = sb("Xs", [128, 3072])
                self.Xa = sb("Xa", [128, 8192])
                self.Xh = sb("Xh", [128, 4, 1024], BF16)
            t["wab32"] = sb("wab32", [128, 8, 128])
            t["wdt32"] = sb("wdt32", [128, 8, 8])
            t["lhist"] = sb("lhist", [128, 4, 3], BF16)
            t["shist"] = sb("shist", [128, 8, 3], BF16)
            t["chist"] = sb("chist", [128, 4, 30], BF16)
            t["hst"] = sb("hst", [128, 4])
            t["S"] = sb("S", [128, 256])
            t["S16"] = sb("S16", [128, 256], BF16)
            t["tail"] = sb("tail", [128, 160])
            t["svec"] = sb("svect", [128, NSV])
            t["dts"] = sb("dts", [128, 4, 64])
            t["xdt"] = sb("xdt", [128, 512], BF16)
            t["xdt2"] = sb("xdt2", [128, 512], BF16)
            t["btm"] = sb("btm", [128, 256], BF16)
            t["cdec"] = sb("cdec", [128, 512], BF16)
            t["mt"] = sb("mt", [128, 1024], BF16)
            t["v16"] = sb("v16", [128, 512], BF16)
            t["acc"] = sb("acc", [128, 32])
            t["eacc"] = sb("eacc", [128, 32])
            t["eacc2"] = sb("eacc2", [128, 32])

            self.setup_consts()
            self.convert_weights()
            seqp = Seq()
            seqp.name, seqp.T, seqp.TS, seqp.QW = "p", TP, TS, min(512, TS)
            seqp.xin, seqp.y, seqp.nvalid, seqp.past, seqp.valcol = d["xp"], o["yp"], TP, 0, C_VALP
            seqs = Seq()
            seqs.name, seqs.T, seqs.TS, seqs.QW = "s", 128, 128, 128
            seqs.xin, seqs.y, seqs.nvalid, seqs.past, seqs.valcol = d["xs"], o["ys"], cfg.SVALID, PAST, C_VALS
            try:
                for l in range(L):
                    self.layer_setup(l)
                    if TP > 0:
                        self.run_seq(seqp, l)
                    self.run_seq(seqs, l)
            except _Stop:
                pass
            P.emit()
        return nc

    def f(self, i, w=None):
        w = self.TSm if w is None else w
        return self.F[:, i, 0:w]

    def h(self, i, w=None):
        w = self.TSm + 32 if w is None else w
        return self.H[:, i, 0:w]

    def ssd_scr(self, off, n):
        if self.TSm >= 1024:
            return bass.AP(self.F, 4 * self.TSm + off, [[8 * self.TSm, 128], [1, n]])
        return self.Xs[:, off:off + n]

    def att_scr(self, off, n):
        if self.TSm >= 1024:
            return bass.AP(self.F, off, [[8 * self.TSm, 128], [1, n]])
        return self.Xa[:, off:off + n]

    def att_h(self, i):
        if self.TSm >= 1024:
            return self.H[:, i, 0:1024]
        return self.Xh[:, i, :]

    def mixb(self, oc, o0, w):
        TSm = self.TSm
        return bass.AP(self.F16, 8 * TSm + oc * TSm + o0, [[16 * TSm, 128], [1, w]])

    def stage32(self, i):
        if self.TSm >= 1024:
            return bass.AP(self.t["mixed"], i * 4096, [[8 * self.TSm, 128], [1, 4096]])
        return self.Xa[:, i * 4096:(i + 1) * 4096]

    def convert_weights(self):
        P, d = self.P, self.d
        L = self.cfg.DEPTH
        n = 0
        jobs = []
        for l in range(L):
            c0 = 0
            while c0 < DIN:
                nc_ = min(512, DIN - c0)
                jobs.append((d["w_in"][l, :, c0:c0 + nc_].rearrange("(kc p) n -> p kc n", p=128),
                             self.winb[l, :, c0:c0 + nc_].rearrange("(kc p) n -> p kc n", p=128), 8, nc_))
                c0 += nc_
            for b in range(4):
                jobs.append((d["w_down"][l, b].rearrange("(kc p) n -> p kc n", p=128),
                             self.wdnb[l, b].rearrange("(kc p) n -> p kc n", p=128), 4, 1024))
            for hf in range(2):
                jobs.append((d["w_out"][l, :, hf * 512:(hf + 1) * 512].rearrange("(kc p) n -> p kc n", p=128),
                             self.woutb[l, :, hf * 512:(hf + 1) * 512].rearrange("(kc p) n -> p kc n", p=128), 8, 512))
        for (src, dst, a, b) in jobs:
            s32 = self.stage32(n % 2)
            s32v = fap(s32[:, 0:1], [[b, a], [1, b]])
            wb = self.t["w"][n % 3]
            wbv = fap(wb[:, 0, 0:1], [[b, a], [1, b]])
            P.dma("sp", s32v, src)
            P.copy(wbv, s32v, eng=("dve", "act", "pool")[n % 3])
            P.dma("sp", dst, wbv)
            n += 1

    def setup_consts(self):
        P, t = self.P, self.t
        cst = t["cst"]
        P.dma("sp", cst[:], self.d["consts"])
        P.copy(t["identb"][:], cst[:, C_ID:C_ID + 128])
        P.tt(self.G[:, 0, 0:128], cst[:, C_U:C_U + 128], cst[:, C_ID:C_ID + 128], ALU.add)
        P.ts(t["negL"][:], self.G[:, 0, 0:128], -1.0)
        P.copy(t["cmb"][:], cst[:, C_CM:C_CM + 128])
        P.ts(t["onesm"][:], cst[:, C_ONE:C_ONE + 128], 1.0 / 512.0)
        P.memset(t["negone"][:], -1.0)
        P.memset(t["tail"][:], 0.0)

    def layer_setup(self, l):
        P, t, d = self.P, self.t, self.d
        vec = t["vec"]
        P.dma("sp", vec[:], d["vecs"][l])
        P.dma("sp", t["vrep"][:], d["vrep"][l])
        gp = d["gpost"]
        P.dma("sp", t["gpost"][:], bass.AP(gp.tensor, l * D, [[0, 128], [1, D]]))
        P.dma("sp", t["wab32"][:], d["wab"][l].rearrange("i p m -> p i m"))
        P.copy(t["wab"][:], t["wab32"][:])
        P.dma("sp", t["wdt32"][:],
              d["w_in"][l, :, COL["ssd_dt"]:COL["ssd_dt"] + 8].rearrange("(kc p) n -> p kc n", p=128))
        P.copy(t["wdt"][:], t["wdt32"][:])
        sm = t["sm"]
        lam = vec[:, V_LLAM:V_LLAM + 4]
        P.act(sm[:, 0:4], lam, AF.Abs)
        P.act(sm[:, 4:8], sm[:, 0:4], AF.Exp, scale=-1.0)
        P.act(sm[:, 8:12], sm[:, 4:8], AF.Ln, bias=1.0)
        P.ts(sm[:, 12:16], lam, -1.0, 0.0, op0=ALU.mult, op1=ALU.max)
        P.tt(sm[:, 16:20], sm[:, 8:12], sm[:, 12:16], ALU.add)
        P.ts(t["cl"][:], sm[:, 16:20], -8.0)
        P.act(sm[:, 24:32], t["vrep"][:, 8:16], AF.Exp)
        P.ts(t["aneg"][:], sm[:, 24:32], -1.0)
        idf = t["cst"][:, C_ID:C_ID + 128]
        for i in range(16):
            P.ts(t["dl"][:, i, :], idf, vec[:, V_LCW + i:V_LCW + i + 1], eng="pool" if i % 2 else "dve")
        for i in range(32):
            P.ts(t["ds"][:, i, :], idf, vec[:, V_SCW + i:V_SCW + i + 1], eng="pool" if i % 2 else "dve")

    def load_w(self, l, c0, ncol=512):
        buf = self.wbuf()
        src = self.winb[l, :, c0:c0 + ncol].rearrange("(kc p) n -> p kc n", p=128)
        self.P.dma("sp", buf[:, :, 0:ncol], src)
        return buf

    def proj(self, wb, wc0, o0, w):
        P, t = self.P, self.t
        ps = self.bank()[:, 0:w]
        for k in range(8):
            P.mm(ps, wb[:, k, wc0:wc0 + 128], t["hT"][:, k, o0:o0 + w], start=(k == 0), stop=(k == 7))
        return ps

    def ttiles(self, TS):
        w = min(512, TS)
        return [(o0, w) for o0 in range(0, TS, w)]

    def rsqrt_rep(self, out, ps):
        P = self.P
        P.act(out, ps, AF.Ln, bias=EPS)
        P.act(out, out, AF.Exp, scale=-0.5)

    def run_seq(self, sq, l):
        P, t, d, o = self.P, self.t, self.d, self.o
        TS = sq.TS
        nst = sq.T // TS
        if sq.name == "p":
            for nm in ("lhist", "shist", "chist", "hst", "S", "S16"):
                P.memset(t[nm][:], 0.0, eng="pool")
        else:
            sv = t["svec"]
            P.dma("sp", sv[:], d["svec"][l])
            P.dma("sp", t["S"][:], d["sssd"][l])
            P.copy(t["S16"][:], t["S"][:])
            P.copy(t["hst"][:], sv[:, SV_H0:SV_H0 + 4])
            P.copy(t["lhist"][:], sv[:, SV_LH:SV_LH + 12].rearrange("p (c j) -> p c j", c=4))
            P.copy(t["shist"][:], sv[:, SV_SH:SV_SH + 24].rearrange("p (c j) -> p c j", c=8))
            P.copy(t["chist"][:], sv[:, SV_CH:SV_CH + 120].rearrange("p (c j) -> p c j", c=4))
        for st in range(nst):
            self.supertile(sq, l, st, last=(st == nst - 1))
        sfx = sq.name
        P.dma("sp", o["lruh_" + sfx][l], t["hst"][:])
        P.dma("sp", o["ssd_" + sfx][l], t["S"][:])
        P.dma("sp", o["lruc_" + sfx][l], t["tail"][:, 0:12])
        P.dma("sp", o["ssdc_" + sfx][l], t["tail"][:, 16:40])
        P.dma("sp", o["cfc_" + sfx][l], t["tail"][:, 40:160])

    def merge(self, sq, l, b, TS):
        P, t, d = self.P, self.t, self.d
        wd = self.wbuf()
        wdv = fap(wd[:], [[1024, 4], [1, 1024]])
        P.dma("sp", wdv, self.wdnb[l, b].rearrange("(kc p) n -> p kc n", p=128))
        for half in range(2):
            wm = self.load_w(l, COL["merge"] + b * 1024 + half * 512)
            for ocl in range(4):
                oc = half * 4 + ocl
                for (o0, w) in self.ttiles(TS):
                    py = self.bank()[:, 0:w]
                    for k in range(4):
                        P.mm(py, bass.AP(wd, k * 1024 + oc * 128, [[4096, 128], [1, 128]]),
                             t["uT"][:, k, o0:o0 + w], start=(k == 0), stop=(k == 3))
                    pg = self.proj(wm, ocl * 128, o0, w)
                    g1 = self.gt(w)
                    P.act(g1, pg, AF.Sigmoid)
                    if b == 0:
                        P.tt(t["mixed"][:, oc, o0:o0 + w], py, g1, ALU.mult)
                    else:
                        g2 = self.gt(w)
                        P.tt(g2, py, g1, ALU.mult)
                        dst = self.mixb(oc, o0, w) if b == 3 else t["mixed"][:, oc, o0:o0 + w]
                        P.tt(dst, t["mixed"][:, oc, o0:o0 + w], g2, ALU.add, eng="pool")

    def supertile(self, sq, l, st, last):
        P, t, d, o = self.P, self.t, self.d, self.o
        TS = sq.TS
        nblk = TS // 128
        t0 = st * TS
        tts = self.ttiles(TS)
        vec, hT, sm, sg, uT = t["vec"], t["hT"], t["sm"], t["sg"], t["uT"]
        f, h = self.f, self.h
        nv = min(sq.nvalid - t0, TS)
        xsrc = sq.xin if l == 0 else sq.y
        cst = t["cst"]
        ident = cst[:, C_ID:C_ID + 128]
        sfx = sq.name

        def in_tile(c0, c1, o0, w):
            return o0 <= c0 and c1 <= o0 + w

        xn = fap(self.G[:, 0, :], [[1, 1024]])
        for b in range(nblk):
            xt = t["xin"][b % 2]
            P.dma("sp", xt[:], xsrc[t0 + b * 128:t0 + (b + 1) * 128, :])
            ssq = sm[:, 32:33]
            P.act(self.junk[:], xt[:], AF.Square, accum_out=ssq)
            rs = sm[:, 34:35]
            P.act(rs, ssq, AF.Ln, bias=EPS, scale=1.0 / D)
            P.act(rs, rs, AF.Exp, scale=-0.5)
            P.ts(xn, xt[:], rs)
            for half in range(2):
                ps = self.bank()
                for j in range(4):
                    jj = half * 4 + j
                    P.tr(ps[:, j * 128:(j + 1) * 128], xn[:, jj * 128:(jj + 1) * 128], ident)
                gp = fap(vec[:, V_GPRE + half * 4:V_GPRE + half * 4 + 4], [[1, 4], [0, 128]])
                P.tt(hT[:, half * 4:half * 4 + 4, b * 128:(b + 1) * 128],
                     ps.rearrange("p (j q) -> p j q", j=4), gp, ALU.mult)

        self.stage(1)
        wb = self.load_w(l, COL["lru_g"])
        for c in range(4):
            for (o0, w) in tts:
                ps = self.proj(wb, c * 128, o0, w)
                P.act(sg[:, c, o0:o0 + w], ps, AF.Silu)
        self.stage(2)
        wb = self.load_w(l, COL["lru_x"])
        for c in range(4):
            lxh = h(c % 2)
            P.copy(lxh[:, 0:3], t["lhist"][:, c, :])
            for (o0, w) in tts:
                ps = self.proj(wb, c * 128, o0, w)
                P.copy(lxh[:, 3 + o0:3 + o0 + w], ps, eng="act")
                if last and in_tile(nv - 3, nv, o0, w):
                    P.copy(t["tail"][:, c * 3:c * 3 + 3], ps[:, nv - 3 - o0:nv - o0])
            P.copy(t["lhist"][:, c, :], lxh[:, TS:TS + 3])
            self.stage(2.1)
            xc, xc16 = f(0, TS), h(2, TS)
            for (o0, w) in tts:
                psc = self.bank()[:, 0:w]
                for k in range(4):
                    P.mm(psc, t["dl"][:, k * 4 + c, :], lxh[:, o0 + k:o0 + k + w], start=(k == 0), stop=(k == 3))
                P.act(xc[:, o0:o0 + w], psc, AF.Identity, bias=vec[:, V_LCB + c:V_LCB + c + 1])
                P.copy(xc16[:, o0:o0 + w], xc[:, o0:o0 + w])
            self.stage(2.2)
            r_, i_ = f(1, TS), f(2, TS)
            for (o0, w) in tts:
                pa = self.bank()[:, 0:w]
                P.mm(pa, t["wab"][:, c, :], xc16[:, o0:o0 + w])
                P.act(r_[:, o0:o0 + w], pa, AF.Sigmoid, bias=vec[:, V_LBA + c:V_LBA + c + 1])
                px = self.bank()[:, 0:w]
                P.mm(px, t["wab"][:, 4 + c, :], xc16[:, o0:o0 + w])
                P.act(i_[:, o0:o0 + w], px, AF.Sigmoid, bias=vec[:, V_LBX + c:V_LBX + c + 1])
            self.stage(2.3)
            a_, th, m_ = f(3, TS), f(4, TS), f(5, TS)
            clc = t["cl"][:, c:c + 1]
            P.act(a_, r_, AF.Exp, scale=clc)
            P.act(th, r_, AF.Tanh, scale=clc)
            self.stage(2.4)
            P.tt(m_, a_, a_, ALU.mult)
            P.stt(m_, m_, 1.0, th, ALU.add, ALU.mult)
            self.stage(2.5)
            P.act(m_, m_, AF.Sqrt, scale=-1.0)
            P.tt(i_, i_, xc, ALU.mult)
            P.tt(i_, i_, m_, ALU.mult)
            self.stage(2.6)
            hh = f(1, TS)
            P.scan(hh, a_, i_, t["hst"][:, c:c + 1])
            P.copy(t["hst"][:, c:c + 1], hh[:, nv - 1:nv])
            self.stage(2.7)
            P.tt(uT[:, c, 0:TS], hh, sg[:, c, 0:TS], ALU.mult)
        self.stage(3)
        self.merge(sq, l, 0, TS)

        self.stage(4)
        wb = self.load_w(l, COL["ssd_z"])
        for c in range(4):
            for (o0, w) in tts:
                ps = self.proj(wb, c * 128, o0, w)
                P.act(sg[:, c, o0:o0 + w], ps, AF.Silu)
        bc = t["QT"]
        for grp in range(2):
            wb = self.load_w(l, COL["ssd_x"] + grp * 512)
            for c in range(4):
                c8 = grp * 4 + c
                sxh = h(c8 % 2)
                P.copy(sxh[:, 0:3], t["shist"][:, c8, :])
                for (o0, w) in tts:
                    ps = self.proj(wb, c * 128, o0, w)
                    P.copy(sxh[:, 3 + o0:3 + o0 + w], ps, eng="act")
                    if last and in_tile(nv - 3, nv, o0, w):
                        P.copy(t["tail"][:, 16 + c8 * 3:16 + c8 * 3 + 3], ps[:, nv - 3 - o0:nv - o0])
                P.copy(t["shist"][:, c8, :], sxh[:, TS:TS + 3])
                for (o0, w) in tts:
                    psc = self.bank()[:, 0:w]
                    for k in range(4):
                        P.mm(psc, t["ds"][:, k * 8 + c8, :], sxh[:, o0 + k:o0 + k + w], start=(k == 0), stop=(k == 3))
                    dst = f(c)[:, o0:o0 + w] if grp == 0 else bc[:, c, o0:o0 + w]
                    P.act(dst, psc, AF.Silu, bias=vec[:, V_SCB + c8:V_SCB + c8 + 1])
        self.stage(5)
        dts = t["dts"]
        pd = self.bank()
        for b in range(nblk):
            for k in range(8):
                P.mm(pd[:, b * 8:b * 8 + 8], hT[:, k, b * 128:(b + 1) * 128], t["wdt"][:, k, :],
                     start=(k == 0), stop=(k == 7))
        nb8 = nblk * 8
        dx, dabs, dln, dt_, adt = dts[:, 0, 0:nb8], dts[:, 1, 0:nb8], dts[:, 2, 0:nb8], dts[:, 3, 0:nb8], dts[:, 1, 0:nb8]
        P.tt(dx.rearrange("p (b h) -> p b h", h=8), pd[:, 0:nb8].rearrange("p (b h) -> p b h", h=8),
             fap(t["vrep"][:, 0:8], [[0, nblk], [1, 8]]), ALU.add)
        P.act(dabs, dx, AF.Abs)
        P.act(dln, dabs, AF.Exp, scale=-1.0)
        P.act(dln, dln, AF.Ln, bias=1.0)
        P.ts(dabs, dx, 0.0, op0=ALU.max)
        P.tt(dt_, dabs, dln, ALU.add)
        P.ts(dt_, dt_, cst[:, sq.valcol:sq.valcol + 1])
        P.tt(adt.rearrange("p (b h) -> p b h", h=8), dt_.rearrange("p (b h) -> p b h", h=8),
             fap(t["aneg"][:], [[0, nblk], [1, 8]]), ALU.mult)
        self.stage(6)
        tri = cst[:, C_TRI:C_TRI + 128]
        Umat = cst[:, C_U:C_U + 128]
        ones = cst[:, C_ONE:C_ONE + 128]
        S, S16 = t["S"], t["S16"]
        for b in range(nblk):
            bs = slice(b * 128, (b + 1) * 128)
            dtb = dts[:, 3, b * 8:b * 8 + 8]
            adtb = dts[:, 1, b * 8:b * 8 + 8]
            psx = self.bank()
            for c in range(4):
                P.tr(psx[:, c * 128:(c + 1) * 128], f(c)[:, bs], ident)
            psx3 = psx.rearrange("p (h q) -> p h q", h=8)
            P.tt(t["xdt"][:].rearrange("p (h q) -> p h q", h=8), psx3, fap(dtb, [[1, 8], [0, 64]]), ALU.mult)
            pds = self.bank()
            P.mm(pds[:, 0:8], Umat, adtb)
            P.mm(pds[:, 8:16], ones, adtb)
            P.act(sm[:, 40:56], pds[:, 0:16], AF.Exp)
            P.tt(sm[:, 56:64], sm[:, 40:48], dtb, ALU.mult)
            P.tt(t["xdt2"][:].rearrange("p (h q) -> p h q", h=8), psx3, fap(sm[:, 56:64], [[1, 8], [0, 64]]), ALU.mult)
            pbt = self.bank()
            for cb in range(2):
                P.mm(pbt[:, cb * 128:(cb + 1) * 128], bc[:, cb, bs], t["identb"][:])
            P.copy(t["btm"][:], pbt[:, 0:256], eng="act")
            pcb = self.bank(2)
            for g_ in range(4):
                cc, gl = g_ // 2, g_ % 2
                P.mm(pcb[:, gl * 512 + cc * 128:gl * 512 + (cc + 1) * 128], bc[gl * 64:(gl + 1) * 64, cc, bs],
                     bc[gl * 64:(gl + 1) * 64, 2 + cc, bs])
            rall = self.ssd_scr(0, 1024)
            dec = self.ssd_scr(1024, 1024)
            P.tt(rall.rearrange("p (h q) -> p h q", h=8), fap(tri, [[0, 8], [1, 128]]),
                 fap(adtb, [[1, 8], [0, 128]]), ALU.mult)
            pseg = self.bank(2)
            for hf in range(2):
                P.mm(pseg[:, hf * 512:(hf + 1) * 512], Umat, rall[:, hf * 512:(hf + 1) * 512], start=True, stop=False)
                P.mm(pseg[:, hf * 512:(hf + 1) * 512], ident,
                     cst[:, C_NEG:C_NEG + 512], start=False, stop=True)
            P.act(dec, pseg, AF.Exp)
            for cc in range(2):
                P.tt(t["mt"][:, cc * 512:(cc + 1) * 512].rearrange("p (g r q) -> p g r q", g=2, r=2),
                     dec[:, cc * 512:(cc + 1) * 512].rearrange("p (g r q) -> p g r q", g=2, r=2),
                     fap(pcb[:, cc * 128:cc * 128 + 1], [[512, 2], [0, 2], [1, 128]]), ALU.mult)
            ea = self.ssd_scr(2048, 1024)
            for gl in range(2):
                pe_ = self.bank()
                rsel = bass.AP(rall.tensor, rall.offset + 2 * gl * 128, [list(rall.ap[0]), [512, 2], [128, 2], [1, 128]])
                P.mm(pe_, ones, rsel)
                hs = slice(gl * 64, (gl + 1) * 64)
                eav = ea[hs, gl * 512:(gl + 1) * 512]
                P.act(eav, pe_[hs, :], AF.Exp)
                P.tt(t["cdec"][hs, :].rearrange("p (c r q) -> p c r q", c=2, r=2),
                     eav.rearrange("p (c r q) -> p c r q", c=2, r=2),
                     fap(bc[hs, 2, bs], [[self.TSm, 2], [0, 2], [1, 128]]), ALU.mult)
            py = self.bank(2)
            for c in range(4):
                cc, gl = c // 2, c % 2
                for r in range(2):
                    hh_ = 2 * c + r
                    outp = py[r * 64:(r + 1) * 64, gl * 512 + cc * 128:gl * 512 + (cc + 1) * 128]
                    P.mm(outp, t["xdt"][:, hh_ * 64:(hh_ + 1) * 64], t["mt"][:, hh_ * 128:(hh_ + 1) * 128],
                         start=True, stop=False)
                    P.mm(outp, S16[gl * 64:(gl + 1) * 64, (cc * 2 + r) * 64:(cc * 2 + r + 1) * 64],
                         t["cdec"][gl * 64:(gl + 1) * 64, (cc * 2 + r) * 128:(cc * 2 + r + 1) * 128],
                         start=False, stop=True)
            for c in range(4):
                cc, gl = c // 2, c % 2
                P.stt(f(c)[:, bs], f(c)[:, bs], vec[:, V_SD + c:V_SD + c + 1],
                      py[:, gl * 512 + cc * 128:gl * 512 + (cc + 1) * 128], ALU.mult, ALU.add)
            pst = self.bank()
            for g_ in range(4):
                cc, gl = g_ // 2, g_ % 2
                P.mm(pst[gl * 64:(gl + 1) * 64, cc * 128:(cc + 1) * 128], t["btm"][:, g_ * 64:(g_ + 1) * 64],
                     t["xdt2"][:, g_ * 128:(g_ + 1) * 128])
            for gl in range(2):
                hs = slice(gl * 64, (gl + 1) * 64)
                sv4 = S[hs, :].rearrange("p (c r q) -> p c r q", c=2, r=2)
                P.tt(sv4, sv4, fap(sm[hs, 48 + 2 * gl:48 + 2 * gl + 1], [[4, 2], [1, 2], [0, 64]]), ALU.mult)
            P.tt(S[:], S[:], pst[:, 0:256], ALU.add)
            P.copy(S16[:], S[:])
        self.stage(7)
        for c in range(4):
            P.tt(f(c, TS), f(c, TS), sg[:, c, 0:TS], ALU.mult)
            P.act(h(c, TS), f(c, TS), AF.Square)
        for (o0, w) in tts:
            pss = self.bank()[:, 0:w]
            for c in range(4):
                P.mm(pss, t["onesm"][:], h(c)[:, o0:o0 + w], start=(c == 0), stop=(c == 3))
            rstd = self.gt(w)
            self.rsqrt_rep(rstd, pss)
            for c in range(4):
                P.stt(uT[:, c, o0:o0 + w], f(c)[:, o0:o0 + w], vec[:, V_SNORM + c:V_SNORM + c + 1], rstd,
                      ALU.mult, ALU.mult)
        self.merge(sq, l, 1, TS)

        self.stage(8)
        wb = self.load_w(l, COL["cf_g"])
        for c in range(4):
            for (o0, w) in tts:
                ps = self.proj(wb, c * 128, o0, w)
                P.act(sg[:, c, o0:o0 + w], ps, AF.Silu)
        wbb = self.load_w(l, COL["cf_b"])
        wba = self.load_w(l, COL["cf_a"])
        idf = ident
        for c in range(4):
            for k in range(31):
                P.ts(t["dc"][:, k, :], idf, vec[:, V_CCW + k * 4 + c:V_CCW + k * 4 + c + 1],
                     eng="pool" if k % 2 else "dve")
            gh = h(c % 2)
            P.copy(gh[:, 0:30], t["chist"][:, c, :])
            for (o0, w) in tts:
                pb = self.proj(wbb, c * 128, o0, w)
                sig = self.gt(w)
                P.act(sig, pb, AF.Sigmoid)
                pa = self.proj(wba, c * 128, o0, w)
                P.tt(gh[:, 30 + o0:30 + o0 + w], pa, sig, ALU.mult)
                if last and in_tile(nv - 30, nv, o0, w):
                    P.tt(t["tail"][:, 40 + c * 30:40 + c * 30 + 30], pa[:, nv - 30 - o0:nv - o0],
                         sig[:, nv - 30 - o0:nv - o0], ALU.mult)
            P.copy(t["chist"][:, c, :], gh[:, TS:TS + 30])
            for (o0, w) in tts:
                psc = self.bank()[:, 0:w]
                for k in range(31):
                    P.mm(psc, t["dc"][:, k, :], gh[:, o0 + k:o0 + k + w], start=(k == 0), stop=(k == 30))
                P.act(f(c)[:, o0:o0 + w], psc, AF.Identity, bias=vec[:, V_CCB + c:V_CCB + c + 1])
        for c in range(4):
            P.copy(h(c, TS), f(c, TS), eng="pool" if c % 2 else "dve")
        for (o0, w) in tts:
            pm = self.bank()[:, 0:w]
            for c in range(4):
                P.mm(pm, t["onesm"][:], h(c)[:, o0:o0 + w], start=(c == 0), stop=(c == 3))
            for c in range(4):
                P.tt(f(c)[:, o0:o0 + w], f(c)[:, o0:o0 + w], pm, ALU.subtract)
        for c in range(4):
            P.act(h(c, TS), f(c, TS), AF.Square)
        for (o0, w) in tts:
            pv_ = self.bank()[:, 0:w]
            for c in range(4):
                P.mm(pv_, t["onesm"][:], h(c)[:, o0:o0 + w], start=(c == 0), stop=(c == 3))
            rstd = self.gt(w)
            self.rsqrt_rep(rstd, pv_)
            for c in range(4):
                P.tt(f(c)[:, o0:o0 + w], f(c)[:, o0:o0 + w], rstd, ALU.mult)
        for c in range(4):
            P.act(f(4 + c % 2, TS), f(c, TS), AF.Silu, scale=vec[:, V_CLG + c:V_CLG + c + 1],
                  bias=vec[:, V_CLB + c:V_CLB + c + 1])
            P.tt(uT[:, c, 0:TS], f(4 + c % 2, TS), sg[:, c, 0:TS], ALU.mult)
        self.merge(sq, l, 2, TS)

        self.stage(9)
        QT = t["QT"]
        wb = self.load_w(l, COL["q"])
        for c in range(4):
            for (o0, w) in tts:
                ps = self.proj(wb, c * 128, o0, w)
                P.act(QT[:, c, o0:o0 + w], ps, AF.Copy, scale=0.125)
        wb = self.load_w(l, COL["k"])
        for c in range(4):
            kt = h(c, TS)
            for (o0, w) in tts:
                ps = self.proj(wb, c * 128, o0, w)
                P.copy(kt[:, o0:o0 + w], ps, eng="act" if c % 2 else "dve")
            P.dma("sp", self.kscr[c, :, t0:t0 + TS], kt)
        for b in range(nblk):
            pk = self.bank()
            for k in range(8):
                P.mm(pk, hT[:, k, b * 128:(b + 1) * 128], wb[:, k, :], start=(k == 0), stop=(k == 7))
            kv = self.gt(512)
            P.copy(kv, pk, eng="act")
            P.dma("sp", o["k_" + sfx][l, t0 + b * 128:t0 + (b + 1) * 128, :], kv)
        wb = self.load_w(l, COL["v"])
        for b in range(nblk):
            pv_ = self.bank()
            for k in range(8):
                P.mm(pv_, hT[:, k, b * 128:(b + 1) * 128], wb[:, k, :], start=(k == 0), stop=(k == 7))
            kv = self.gt(512)
            P.copy(kv, pv_, eng="act")
            P.dma("sp", o["v_" + sfx][l, t0 + b * 128:t0 + (b + 1) * 128, :], kv)
            P.copy(t["v16"][:], pv_)
            P.dma("sp", self.vscr[t0 + b * 128:t0 + (b + 1) * 128, :], t["v16"][:])
        wb = self.load_w(l, COL["sb_g"])
        for c in range(4):
            for (o0, w) in tts:
                ps = self.proj(wb, c * 128, o0, w)
                P.act(sg[:, c, o0:o0 + w], ps, AF.Silu)

        self.stage(10)
        QW = sq.QW
        nsub = QW // 128
        for qt in range(TS // QW):
            self.attention(sq, l, t0, qt)
        self.stage(11)
        self.merge(sq, l, 3, TS)

        self.stage(12)
        wo = []
        for half in range(2):
            buf = self.wbuf()
            P.dma("sp", buf[:], self.woutb[l, :, half * 512:(half + 1) * 512].rearrange("(kc p) n -> p kc n", p=128))
            wo.append(buf)
        ot = fap(self.G[:, 0, :], [[1, 1024]])
        for b in range(nblk):
            xt = t["xin"][b % 2]
            P.dma("sp", xt[:], xsrc[t0 + b * 128:t0 + (b + 1) * 128, :])
            for half in range(2):
                po = self.bank()
                for k in range(8):
                    P.mm(po, self.mixb(k, b * 128, 128), wo[half][:, k, :], start=(k == 0), stop=(k == 7))
                P.copy(ot[:, half * 512:(half + 1) * 512], po, eng="act")
            ssq = sm[:, 36:37]
            P.act(self.junk[:], ot, AF.Square, accum_out=ssq)
            rs = sm[:, 38:39]
            P.act(rs, ssq, AF.Ln, bias=EPS, scale=1.0 / D)
            P.act(rs, rs, AF.Exp, scale=-0.5)
            P.stt(ot, ot, rs, t["gpost"][:], ALU.mult, ALU.mult)
            P.tt(ot, ot, xt[:], ALU.add)
            P.dma("sp", sq.y[t0 + b * 128:t0 + (b + 1) * 128, :], ot)

    def attention(self, sq, l, t0, qt):
        P, t, d = self.P, self.t, self.d
        QW = sq.QW
        nsub = QW // 128
        QT, sg, uT = t["QT"], t["sg"], t["uT"]
        q0 = qt * QW
        gq0 = t0 + q0
        acc = t["acc"]
        eaccs = [t["eacc"], t["eacc2"]]
        oacc = self.att_scr(2048, nsub * 512)
        tmpo = self.G[:, 3, :]
        P.memset(acc[:, 0:nsub * 8], 0.0)
        P.memset(eaccs[0][:, 0:nsub * 8], 1.0)
        P.memset(oacc, 0.0, eng="pool")
        spans = []
        if sq.past == 0:
            assert QW == 512
            nsp = gq0 // 512 + 1
            for sp_ in range(nsp - 1, -1, -1):
                spans.append(("cur", sp_ * 512, 512, sp_ == nsp - 1))
        else:
            spans.append(("cur", 0, QW, True))
            for sp_ in range(sq.past // 512 - 1, -1, -1):
                spans.append(("past", sp_ * 512, 512, False))

        kvbufs = {}

        def load_span(si):
            if si >= len(spans) or si in kvbufs:
                return
            kind, k0, klen, diag = spans[si]
            kvb = self.wbuf()
            nkb = klen // 128
            if kind == "cur":
                P.dma("sp", kvb[:, 0:4, 0:klen], self.kscr[:, :, k0:k0 + klen].rearrange("c p n -> p c n"))
                P.dma("sp", kvb[:, 4:4 + nkb, :], self.vscr[k0:k0 + klen, :].rearrange("(j p) n -> p j n", p=128))
            else:
                stg = self.att_scr(4096, 4096)
                sk = fap(stg[:, 0:1], [[512, 4], [1, klen]])
                svv = fap(stg[:, 2048:2049], [[512, nkb], [1, 512]])
                P.dma("sp", sk, d["pk"][l, :, :, k0:k0 + klen].rearrange("c p n -> p c n"))
                P.dma("sp", svv, d["pv"][l, k0:k0 + klen, :].rearrange("(j p) n -> p j n", p=128))
                P.copy(kvb[:, 0:4, 0:klen], sk)
                P.copy(kvb[:, 4:4 + nkb, :], svv, eng="act")
            kvbufs[si] = kvb

        items = []
        blk = 0
        for si, (kind, k0, klen, diag) in enumerate(spans):
            nkb = klen // 128
            for j in range(nkb - 1, -1, -1):
                for hp in range(4):
                    items.append(dict(si=si, j=j, hp=hp, diag=diag, c0=(j * 128 if diag else 0), blk=blk,
                                      first=(hp == 0), last=(hp == 3), sfirst=(j == nkb - 1 and hp == 0)))
                blk += 1
        n = len(items)
        paccs = {}

        def bufs(i):
            b = i % 2
            pz = self.psum[:, b * 1024:(b + 1) * 1024]
            et = self.att_scr(b * 1024, 1024)
            spt = self.att_h(b)
            wt = self.att_h(2 + b)
            po = self.psum[:, (4 + b) * 512:(5 + b) * 512]
            return pz, et, spt, wt, po

        def views(it, pz, et, spt, wt):
            c0 = it["c0"]
            wq = QW - c0
            return (fap(pz[:, c0:c0 + 1], [[512, 2], [1, wq]]), fap(et[:, c0:c0 + 1], [[QW, 2], [1, wq]]),
                    fap(spt[:, c0:c0 + 1], [[QW, 2], [1, wq]]), fap(wt[:, c0:c0 + 1], [[QW, 2], [1, wq]]))

        def stA(i):
            it = items[i]
            if it["sfirst"]:
                load_span(it["si"])
                load_span(it["si"] + 1)
            kvb = kvbufs[it["si"]]
            pz, et, spt, wt, po = bufs(i)
            c0, j, hp = it["c0"], it["j"], it["hp"]
            for r in range(2):
                rs_ = slice(r * 64, (r + 1) * 64)
                P.mm(pz[:, r * 512 + c0:r * 512 + QW], kvb[rs_, hp, j * 128:(j + 1) * 128],
                     QT[rs_, hp, q0 + c0:q0 + QW])

        def stB(i):
            it = items[i]
            pz, et, spt, wt, po = bufs(i)
            pzv, etv, spv, wtv = views(it, pz, et, spt, wt)
            c0 = it["c0"]
            P.act(etv, pzv, AF.Exp)
            P.act(spv, etv, AF.Ln, bias=1.0)
            if it["diag"]:
                dv = fap(spt[:, c0:c0 + 1], [[QW, 2], [1, 128]])
                P.tt(dv, dv, fap(t["cmb"][:, 0:1], [[0, 2], [1, 128]]), ALU.mult)

        def stC(i):
            it = items[i]
            pz, et, spt, wt, po = bufs(i)
            c0, hp = it["c0"], it["hp"]
            sub0 = c0 // 128
            for r in range(2):
                P.mm(pz[:, r * 512 + c0:r * 512 + QW], t["negL"][:], spt[:, r * QW + c0:(r + 1) * QW],
                     start=False, stop=True, skip_group_check=True)
            pacc = self.psum[:, (6 + it["blk"] % 2) * 512:(7 + it["blk"] % 2) * 512]
            for r in range(2):
                hd = 2 * hp + r
                for sub in range(sub0, nsub):
                    qs = slice(r * QW + sub * 128, r * QW + (sub + 1) * 128)
                    P.mm(pacc[:, sub * 8 + hd:sub * 8 + hd + 1], spt[:, qs], t["negone"][:, 0:1])

        def stD(i):
            it = items[i]
            pz, et, spt, wt, po = bufs(i)
            pzv, etv, spv, wtv = views(it, pz, et, spt, wt)
            c0 = it["c0"]
            P.act(wtv, pzv, AF.Exp)
            if it["diag"]:
                dv = fap(wt[:, c0:c0 + 1], [[QW, 2], [1, 128]])
                P.tt(dv, dv, fap(t["cmb"][:, 0:1], [[0, 2], [1, 128]]), ALU.mult)
            if it["last"]:
                sub0 = c0 // 128
                pacc = self.psum[:, (6 + it["blk"] % 2) * 512:(7 + it["blk"] % 2) * 512]
                a_ = acc[:, sub0 * 8:nsub * 8]
                P.tt(a_, a_, pacc[:, sub0 * 8:nsub * 8], ALU.add)
                P.act(eaccs[(it["blk"] + 1) % 2][:, 0:nsub * 8], acc[:, 0:nsub * 8], AF.Exp)

        def stE(i):
            it = items[i]
            kvb = kvbufs[it["si"]]
            pz, et, spt, wt, po = bufs(i)
            c0, j, hp = it["c0"], it["j"], it["hp"]
            sub0 = c0 // 128
            for r in range(2):
                hd = 2 * hp + r
                for sub in range(sub0, nsub):
                    qs = slice(r * QW + sub * 128, r * QW + (sub + 1) * 128)
                    P.mm(po[:, (sub * 2 + r) * 64:(sub * 2 + r + 1) * 64], wt[:, qs],
                         kvb[:, 4 + j, hd * 64:(hd + 1) * 64])

        def stF(i):
            it = items[i]
            pz, et, spt, wt, po = bufs(i)
            c0, hp = it["c0"], it["hp"]
            sub0 = c0 // 128
            ns = nsub - sub0
            eacc = eaccs[it["blk"] % 2]
            pov = fap(po[:, sub0 * 128:sub0 * 128 + 1], [[128, ns], [64, 2], [1, 64]])
            eav = fap(eacc[:, sub0 * 8 + 2 * hp:sub0 * 8 + 2 * hp + 1], [[8, ns], [1, 2], [0, 64]])
            tv = fap(tmpo[:, 0:1], [[128, ns], [64, 2], [1, 64]])
            P.tt(tv, pov, eav, ALU.mult)
            ov = fap(oacc[:, sub0 * 512 + 2 * hp * 64:sub0 * 512 + 2 * hp * 64 + 1], [[512, ns], [64, 2], [1, 64]])
            P.tt(ov, ov, tv, ALU.add, eng="pool")

        for s_ in range(n + 2):
            if s_ < n:
                stA(s_)
                stB(s_)
            if 0 <= s_ - 1 < n:
                stC(s_ - 1)
                stD(s_ - 1)
            if 0 <= s_ - 2 < n:
                stE(s_ - 2)
                stF(s_ - 2)
        ident = t["cst"][:, C_ID:C_ID + 128]
        for sub in range(nsub):
            pt = self.bank(1, 4, 6)
            for c in range(4):
                P.tr(pt[:, c * 128:(c + 1) * 128], oacc[:, sub * 512 + c * 128:sub * 512 + (c + 1) * 128], ident)
            cs = slice(q0 + sub * 128, q0 + (sub + 1) * 128)
            P.tt(uT[:, :, cs], pt.rearrange("p (c q) -> p c q", c=4), sg[:, :, cs], ALU.mult)


def make_consts():
    c = np.zeros((128, NC), np.float32)
    j = np.arange(128)[:, None]
    s = np.arange(128)[None, :]
    c[:, C_ID:C_ID + 128] = (j == s)
    c[:, C_U:C_U + 128] = (j > s)
    c[:, C_TRI:C_TRI + 128] = (j <= s)
    for rep in range(4):
        c[:, C_NEG + rep * 128:C_NEG + (rep + 1) * 128] = np.where(s < j, -30000.0, 0.0)
    c[:, C_CM:C_CM + 128] = (j < s)
    c[:, C_ONE:C_ONE + 128] = 1.0
    c[:, C_VALP] = 1.0
    return c


def fm(v, nch):
    return np.ascontiguousarray(np.asarray(v, np.float32).reshape(nch, 128).T)


def host_prep(inp, cfg):
    L = cfg.DEPTH
    f32 = np.float32
    vecs = np.zeros((L, 128, NV), f32)
    vrep = np.zeros((L, 128, 16), f32)
    wab = np.zeros((L, 8, 128, 128), f32)
    for l in range(L):
        v = vecs[l]
        v[:, V_GPRE:V_GPRE + 8] = fm(inp["norm_pre"][l], 8)
        for k in range(4):
            v[:, V_LCW + k * 4:V_LCW + k * 4 + 4] = fm(inp["lru_conv_w"][l, k], 4)
            v[:, V_SCW + k * 8:V_SCW + k * 8 + 8] = fm(inp["ssd_conv_w"][l, k], 8)
        v[:, V_LCB:V_LCB + 4] = fm(inp["lru_conv_b"][l], 4)
        v[:, V_LBA:V_LBA + 4] = fm(inp["lru_ba"][l], 4)
        v[:, V_LBX:V_LBX + 4] = fm(inp["lru_bx"][l], 4)
        v[:, V_LLAM:V_LLAM + 4] = fm(inp["lru_lambda"][l], 4)
        v[:, V_SCB:V_SCB + 8] = fm(inp["ssd_conv_b"][l], 8)
        v[:, V_SD:V_SD + 4] = fm(np.repeat(np.asarray(inp["ssd_d"][l]), 64), 4)
        v[:, V_SNORM:V_SNORM + 4] = fm(inp["ssd_norm"][l], 4)
        for k in range(31):
            v[:, V_CCW + k * 4:V_CCW + k * 4 + 4] = fm(inp["cf_conv_w"][l, k], 4)
        v[:, V_CCB:V_CCB + 4] = fm(inp["cf_conv_b"][l], 4)
        v[:, V_CLG:V_CLG + 4] = fm(inp["cf_ln_g"][l], 4)
        v[:, V_CLB:V_CLB + 4] = fm(inp["cf_ln_b"][l], 4)
        vrep[l, :, 0:8] = np.asarray(inp["ssd_dt_bias"][l])[None, :]
        vrep[l, :, 8:16] = np.asarray(inp["ssd_a_log"][l])[None, :]
        for which, nm in enumerate(("lru_wa", "lru_wx")):
            w = np.asarray(inp[nm][l])
            for c in range(4):
                wab[l, which * 4 + c, 0:64, 0:64] = w[2 * c]
                wab[l, which * 4 + c, 64:128, 64:128] = w[2 * c + 1]
    consts = make_consts()
    consts[:cfg.SVALID, C_VALS] = 1.0
    shared = dict(w_in=np.ascontiguousarray(inp["w_in"], f32), w_down=np.ascontiguousarray(inp["w_down"], f32),
                  w_out=np.ascontiguousarray(inp["w_out"], f32), wab=wab, vecs=vecs, vrep=vrep,
                  gpost=np.ascontiguousarray(inp["norm_post"], f32), consts=consts)
    return shared


def sample_inputs(inp, cfg, s):
    L = cfg.DEPTH
    f32 = np.float32
    SV = cfg.SVALID
    xs = np.zeros((128, D), f32)
    xs[:SV] = inp["x_sample"][s]
    svec = np.zeros((L, 128, NSV), f32)
    sssd = np.zeros((L, 128, 256), f32)
    pk = np.zeros((L, 4, 128, cfg.PAST), f32)
    for l in range(L):
        svec[l, :, SV_H0:SV_H0 + 4] = fm(inp["state_lru_h"][l, s], 4)
        lc = np.asarray(inp["state_lru_conv"][l, s])
        svec[l, :, SV_LH:SV_LH + 12] = lc.T.reshape(4, 128, 3).transpose(1, 0, 2).reshape(128, 12)
        sc = np.asarray(inp["state_ssd_conv"][l, s])
        svec[l, :, SV_SH:SV_SH + 24] = sc.T.reshape(8, 128, 3).transpose(1, 0, 2).reshape(128, 24)
        cc = np.asarray(inp["state_cf_conv"][l, s])
        svec[l, :, SV_CH:SV_CH + 120] = cc.T.reshape(4, 128, 30).transpose(1, 0, 2).reshape(128, 120)
        st = np.asarray(inp["state_ssd"][l, s])
        st = st.reshape(2, 2, 2, 64, 64)
        sssd[l] = st.transpose(1, 4, 0, 2, 3).reshape(128, 256)
        k = np.asarray(inp["cache_sb_k"][l, s]).reshape(cfg.PAST, 512)
        pk[l] = k.T.reshape(4, 128, cfg.PAST)
    pv = np.ascontiguousarray(np.asarray(inp["cache_sb_v"][:, s]).reshape(L, cfg.PAST, 512), f32)
    return dict(xs=xs, svec=svec, sssd=sssd, pk=pk, pv=pv)


def unpack_states(res, sfx, L):
    lruh = res["lruh_" + sfx].transpose(0, 2, 1).reshape(L, 512)
    lruc = res["lruc_" + sfx].reshape(L, 128, 4, 3).transpose(0, 3, 2, 1).reshape(L, 3, 512)
    ssdc = res["ssdc_" + sfx][:, :, 0:24].reshape(L, 128, 8, 3).transpose(0, 3, 2, 1).reshape(L, 3, 1024)
    cfc = res["cfc_" + sfx].reshape(L, 128, 4, 30).transpose(0, 3, 2, 1).reshape(L, 30, 512)
    ssd = res["ssd_" + sfx].reshape(L, 2, 64, 2, 2, 64)
    ssd = ssd.transpose(0, 3, 1, 4, 5, 2).reshape(L, 8, 64, 64)
    return lruh, lruc, ssd, ssdc, cfc


_NC_CACHE = {}


def run_cfg(inp, cfg, n_cores=8):
    key = (cfg.TP, cfg.TS, cfg.DEPTH, cfg.PAST, cfg.SVALID)
    if key not in _NC_CACHE:
        _NC_CACHE[key] = Builder(cfg).build()
    nc = _NC_CACHE[key]
    shared = host_prep(inp, cfg)
    L = cfg.DEPTH
    nb = inp["x_prompt"].shape[0] if cfg.TP > 0 else 0
    nsmp = inp["x_sample"].shape[0]
    in_maps = []
    for c in range(n_cores):
        m = dict(shared)
        if cfg.TP > 0:
            m["xp"] = np.ascontiguousarray(inp["x_prompt"][(c // 2) % nb], np.float32)
        else:
            m["xp"] = np.zeros((128, D), np.float32)
        m.update(sample_inputs(inp, cfg, c % nsmp))
        in_maps.append(m)
    res = run_bass_kernel_spmd(nc, in_maps, core_ids=list(range(n_cores))).results
    SV = cfg.SVALID
    outs = {}
    if cfg.TP > 0:
        pcs = [2 * b for b in range(nb)]
        outs["y_p"] = np.stack([res[c]["yp"] for c in pcs])
        st = [unpack_states(res[c], "p", L) for c in pcs]
        for i, nm in enumerate(("lru_h_p", "lru_conv_p", "ssd_p", "ssd_conv_p", "cf_conv_p")):
            outs[nm] = np.stack([s[i] for s in st], axis=1)
        outs["k_p"] = np.stack([res[c]["k_p"].reshape(L, cfg.TP, 8, 64) for c in pcs], axis=1)
        outs["v_p"] = np.stack([res[c]["v_p"].reshape(L, cfg.TP, 8, 64) for c in pcs], axis=1)
    scs = list(range(min(nsmp, n_cores)))
    outs["y_s"] = np.stack([res[c]["ys"][:SV] for c in scs])
    st = [unpack_states(res[c], "s", L) for c in scs]
    for i, nm in enumerate(("lru_h_s", "lru_conv_s", "ssd_s", "ssd_conv_s", "cf_conv_s")):
        outs[nm] = np.stack([s[i] for s in st], axis=1)
    outs["k_s"] = np.stack([res[c]["k_s"][:, :SV].reshape(L, SV, 8, 64) for c in scs], axis=1)
    outs["v_s"] = np.stack([res[c]["v_s"][:, :SV].reshape(L, SV, 8, 64) for c in scs], axis=1)
    return outs


ORDER = ("y_p", "y_s", "lru_h_p", "lru_h_s", "lru_conv_p", "lru_conv_s", "ssd_p", "ssd_s",
         "ssd_conv_p", "ssd_conv_s", "cf_conv_p", "cf_conv_s", "k_p", "k_s", "v_p", "v_s")


def kernel(**inputs):
    inp = {k: np.asarray(v) for k, v in inputs.items()}
    cfg = Cfg(TP=inp["x_prompt"].shape[1], TS=1024, DEPTH=inp["w_in"].shape[0],
              PAST=inp["cache_sb_k"].shape[2], SVALID=inp["x_sample"].shape[1])
    outs = run_cfg(inp, cfg, 8)
    return tuple(np.ascontiguousarray(outs[k], dtype=np.float32) for k in ORDER)
```

```python
import contextlib
import numpy as np
import ml_dtypes
import concourse.bass as bass
import concourse.mybir as mybir
from concourse.bass_utils import run_bass_kernel_spmd

F32 = mybir.dt.float32
BF16 = mybir.dt.bfloat16
AF = mybir.ActivationFunctionType
ALU = mybir.AluOpType

CENG = ("pe", "act", "dve", "pool")
NSLOT = 24

D = 1024
DIN = 10248
EPS = 1e-6
COL = dict(lru_x=0, lru_g=512, ssd_z=1024, ssd_x=1536, ssd_bc=2048, ssd_dt=2560, cf_a=2568, cf_b=3080,
           cf_g=3592, q=4104, k=4616, v=5128, sb_g=5640, merge=6152)
V_GPRE, V_LCW, V_LCB, V_LBA, V_LBX, V_LLAM = 0, 8, 24, 28, 32, 36
V_SCW, V_SCB, V_SD, V_SNORM = 40, 72, 80, 84
V_CCW, V_CCB, V_CLG, V_CLB = 88, 212, 216, 220
NV = 224
C_ID, C_U, C_TRI, C_NEG, C_CM, C_ONE, C_VALP, C_VALS = 0, 128, 256, 384, 896, 1024, 1152, 1153
NC = 1154
SV_H0, SV_LH, SV_SH, SV_CH = 0, 4, 16, 40
NSV = 160


def _rect(ap):
    t = ap.tensor
    dims = list(ap.ap)
    if str(ap.space) == "DRAM":
        lo = ap.offset
        hi = lo
        for st, n in dims:
            if st >= 0:
                hi += st * (n - 1)
            else:
                lo += st * (n - 1)
        return ("D:" + t.name, 0, 1, lo, hi + 1)
    pst, pn = dims[0]
    if pst == 0:
        p0 = 0
        base = ap.offset
    else:
        p0 = ap.offset // pst
        base = ap.offset - p0 * pst
    lo = base
    hi = base
    for st, n in dims[1:]:
        if st >= 0:
            hi += st * (n - 1)
        else:
            lo += st * (n - 1)
    esz = mybir.dt.size(ap.dtype)
    if str(ap.space) == "PSUM":
        b0 = (lo * esz) // 2048 * 2048
        b1 = ((hi + 1) * esz + 2047) // 2048 * 2048
        q0 = p0 // 32 * 32
        q1 = (p0 + pn + 31) // 32 * 32
        return ("P:" + t.name, q0, q1, b0, b1)
    return ("S:" + t.name, p0, p0 + pn, lo * esz, (hi + 1) * esz)


class _Rec:
    __slots__ = ("p0", "p1", "f0", "f1", "w", "r")

    def __init__(self, p0, p1, f0, f1):
        self.p0, self.p1, self.f0, self.f1 = p0, p1, f0, f1
        self.w = {}
        self.r = {}


def _merge(dst, src):
    for k, v in src.items():
        if dst.get(k, -1) < v:
            dst[k] = v


class Ins:
    __slots__ = ("eng", "fn", "deps", "kind", "idx", "slot", "seq", "waits", "sig")


class Prog:
    def __init__(self, nc):
        self.nc = nc
        self.ins = []
        self.track = {}
        self.cnt = {e: 0 for e in CENG}
        self.ndma = 0
        self.slot_seq = [0] * NSLOT
        import os
        self.maxins = int(os.environ.get("KMAXINS", "100000000"))

    def _access(self, ident, reads, writes):
        deps = {}
        rr = [_rect(a) for a in reads]
        ww = [_rect(a) for a in writes]
        for key, p0, p1, f0, f1 in rr:
            for rec in self.track.get(key, ()):
                if rec.p0 < p1 and p0 < rec.p1 and rec.f0 < f1 and f0 < rec.f1:
                    _merge(deps, rec.w)
                    if key[0] == "P":
                        for k2, v2 in rec.r.items():
                            if k2 != ident[0] and deps.get(k2, -1) < v2:
                                deps[k2] = v2
        for key, p0, p1, f0, f1 in ww:
            for rec in self.track.get(key, ()):
                if rec.p0 < p1 and p0 < rec.p1 and rec.f0 < f1 and f0 < rec.f1:
                    _merge(deps, rec.w)
                    _merge(deps, rec.r)
        k, v = ident
        for key, p0, p1, f0, f1 in rr:
            lst = self.track.setdefault(key, [])
            hit = False
            for rec in lst:
                if rec.p0 < p1 and p0 < rec.p1 and rec.f0 < f1 and f0 < rec.f1:
                    if rec.r.get(k, -1) < v:
                        rec.r[k] = v
                    if rec.p0 <= p0 and p1 <= rec.p1 and rec.f0 <= f0 and f1 <= rec.f1:
                        hit = True
            if not hit:
                rec = _Rec(p0, p1, f0, f1)
                rec.r[k] = v
                lst.append(rec)
                if len(lst) > 64:
                    self._collapse(key)
        for key, p0, p1, f0, f1 in ww:
            lst = self.track.setdefault(key, [])
            keep = [rec for rec in lst
                    if not (p0 <= rec.p0 and rec.p1 <= p1 and f0 <= rec.f0 and rec.f1 <= f1)]
            rec = _Rec(p0, p1, f0, f1)
            rec.w[k] = v
            keep.append(rec)
            self.track[key] = keep
            if len(keep) > 64:
                self._collapse(key)
        return deps

    def _collapse(self, key):
        keep = self.track[key]
        big = _Rec(min(r.p0 for r in keep), max(r.p1 for r in keep),
                   min(r.f0 for r in keep), max(r.f1 for r in keep))
        for r in keep:
            _merge(big.w, r.w)
            _merge(big.r, r.r)
        self.track[key] = [big]

    def op(self, eng, fn, reads, writes):
        if len(self.ins) >= self.maxins:
            return None
        i = Ins()
        i.eng, i.fn, i.kind = eng, fn, "op"
        i.idx = self.cnt[eng]
        self.cnt[eng] += 1
        i.deps = self._access((eng, i.idx), reads, writes)
        if eng == "pe":
            i.deps.pop("pe", None)
        self.ins.append(i)
        return i

    def dma(self, queue, out, in_, **kw):
        if len(self.ins) >= self.maxins:
            return None
        i = Ins()
        i.eng, i.kind = queue, "dma"
        i.slot = self.ndma % NSLOT
        self.ndma += 1
        i.seq = self.slot_seq[i.slot]
        self.slot_seq[i.slot] += 1
        i.fn = lambda e: e.dma_start(out=out, in_=in_, **kw)
        i.deps = self._access((("d", i.slot), i.seq), [in_], [out])
        if i.seq > 0:
            k = ("d", i.slot)
            if i.deps.get(k, -1) < i.seq - 1:
                i.deps[k] = i.seq - 1
        self.ins.append(i)
        return i

    def emit(self):
        nc = self.nc
        known = {e: {} for e in list(CENG) + ["sp"]}
        clock = {}
        need_sig = set()
        for i in self.ins:
            kn = known[i.eng]
            waits = []
            for k, v in i.deps.items():
                if kn.get(k, -1) >= v:
                    continue
                waits.append((k, v))
                _merge(kn, clock[(k, v)])
                if kn.get(k, -1) < v:
                    kn[k] = v
                need_sig.add((k, v))
            i.waits = waits
            if i.kind == "op":
                clock[(i.eng, i.idx)] = dict(kn)
            else:
                clock[(("d", i.slot), i.seq)] = dict(kn)
        sigcount = {e: 0 for e in CENG}
        semval = {}
        for i in self.ins:
            if i.kind == "op":
                if (i.eng, i.idx) in need_sig:
                    sigcount[i.eng] += 1
                    i.sig = True
                    semval[(i.eng, i.idx)] = sigcount[i.eng]
                else:
                    i.sig = False
            else:
                semval[(("d", i.slot), i.seq)] = 16 * (i.seq + 1)
        per = {e: [] for e in list(CENG) + ["sp"]}
        for i in self.ins:
            per[i.eng].append(i)
        self.stats = {e: len(per[e]) for e in per}
        self.stats["nwait"] = sum(len(i.waits) for i in self.ins)
        self.stats["sig"] = dict(sigcount)
        with contextlib.ExitStack() as es:
            sems = {}
            for e in CENG:
                sems[e] = es.enter_context(nc.semaphore("c_" + e))
            for s in range(NSLOT):
                sems[("d", s)] = es.enter_context(nc.semaphore("d%d" % s))
            block = es.enter_context(nc.Block())

            def run(engname):
                def body(eobj):
                    for i in per[engname]:
                        for k, v in i.waits:
                            eobj.wait_ge(sems[k], semval[(k, v)])
                        r = i.fn(eobj)
                        if i.kind == "dma":
                            r.then_inc(sems[("d", i.slot)], 16)
                        elif i.sig:
                            r.then_inc(sems[i.eng], 1)
                    if engname == "sp":
                        for s in range(NSLOT):
                            if self.slot_seq[s] > 0:
                                eobj.wait_ge(sems[("d", s)], 16 * self.slot_seq[s])
                return body

            block.tensor(run("pe"))
            block.scalar(run("act"))
            block.vector(run("dve"))
            block.gpsimd(run("pool"))
            block.sync(run("sp"))

    def mm(self, out, lhsT, rhs, start=True, stop=True, **kw):
        return self.op("pe", lambda e: e.matmul(out, lhsT, rhs, start=start, stop=stop, **kw),
                       [lhsT, rhs], [out])

    def tr(self, out, in_, ident):
        return self.op("pe", lambda e: e.transpose(out, in_, ident), [in_, ident], [out])

    def act(self, out, in_, func, bias=None, scale=1.0, accum_out=None):
        reads = [in_]
        kw = {}
        if bias is not None:
            kw["bias"] = bias
            if not isinstance(bias, (int, float)):
                reads.append(bias)
        if not isinstance(scale, (int, float)):
            reads.append(scale)
        writes = [out]
        if accum_out is not None:
            kw["accum_out"] = accum_out
            writes.append(accum_out)
        return self.op("act", lambda e: e.activation(out=out, in_=in_, func=func, scale=scale, **kw),
                       reads, writes)

    def tt(self, out, in0, in1, op, eng="dve"):
        return self.op(eng, lambda e: e.tensor_tensor(out=out, in0=in0, in1=in1, op=op), [in0, in1], [out])

    def ts(self, out, in0, s1, s2=None, op0=ALU.mult, op1=None, eng="dve"):
        reads = [in0]
        if not isinstance(s1, (int, float)):
            reads.append(s1)
        if s2 is not None and not isinstance(s2, (int, float)):
            reads.append(s2)
        if op1 is None:
            return self.op(eng, lambda e: e.tensor_scalar(out=out, in0=in0, scalar1=s1, scalar2=None, op0=op0),
                           reads, [out])
        return self.op(eng, lambda e: e.tensor_scalar(out=out, in0=in0, scalar1=s1, scalar2=s2, op0=op0, op1=op1),
                       reads, [out])

    def stt(self, out, in0, scalar, in1, op0, op1):
        reads = [in0, in1]
        if not isinstance(scalar, (int, float)):
            reads.append(scalar)
        return self.op("dve", lambda e: e.scalar_tensor_tensor(out=out, in0=in0, scalar=scalar, in1=in1,
                                                               op0=op0, op1=op1), reads, [out])

    def copy(self, out, in_, eng="dve"):
        if eng == "act":
            return self.op("act", lambda e: e.copy(out=out, in_=in_), [in_], [out])
        return self.op(eng, lambda e: e.tensor_copy(out=out, in_=in_), [in_], [out])

    def memset(self, ap, val, eng="dve"):
        return self.op(eng, lambda e: e.memset(ap, val), [], [ap])

    def scan(self, out, d0, d1, init):
        reads = [d0, d1]
        if not isinstance(init, (int, float)):
            reads.append(init)
        return self.op("dve", lambda e: e.tensor_tensor_scan(out=out, data0=d0, data1=d1, initial=init,
                                                             op0=ALU.mult, op1=ALU.add), reads, [out])


def fap(ap, dims):
    return bass.AP(ap.tensor, ap.offset, [list(ap.ap[0])] + [list(d) for d in dims])


class Cfg:
    def __init__(self, TP=8192, TS=1024, DEPTH=4, PAST=1024, SVALID=64, stop=99):
        self.TP, self.TS, self.DEPTH, self.PAST, self.SVALID = TP, TS, DEPTH, PAST, SVALID
        self.stop = stop


class Seq:
    pass


class _Stop(Exception):
    pass


class Builder:
    def __init__(self, cfg):
        self.cfg = cfg
        self.nc = bass.Bass("TRN2", target_bir_lowering=False)
        self.bank_rr = 0
        self.wrr = 0
        self.grr = 0

    def din(self, name, shape, dt=F32):
        return self.nc.dram_tensor(name, list(shape), dt, kind="ExternalInput").ap()

    def dout(self, name, shape, dt=F32):
        return self.nc.dram_tensor(name, list(shape), dt, kind="ExternalOutput").ap()

    def bank(self, n=1, lo=0, hi=8):
        if not hasattr(self, "_brr"):
            self._brr = {}
        rr = self._brr.get((lo, hi), lo)
        if n == 2 and (rr - lo) % 2:
            rr += 1
        if rr + n > hi:
            rr = lo
        self._brr[(lo, hi)] = rr + n
        return self.psum[:, rr * 512:(rr + n) * 512]

    def stage(self, n):
        if n >= self.cfg.stop:
            raise _Stop()

    def wbuf(self):
        b = self.t["w"][self.wrr % 3]
        self.wrr += 1
        return b

    def gt(self, w=512):
        b = self.G[:, self.grr % 4, 0:w]
        self.grr += 1
        return b

    def build(self):
        cfg = self.cfg
        nc = self.nc
        L, TP, TS, PAST = cfg.DEPTH, cfg.TP, cfg.TS, cfg.PAST
        d = {}
        d["xp"] = self.din("xp", [max(TP, 128), D])
        d["xs"] = self.din("xs", [128, D])
        d["w_in"] = self.din("w_in", [L, D, DIN])
        d["w_down"] = self.din("w_down", [L, 4, 512, D])
        d["w_out"] = self.din("w_out", [L, D, D])
        d["wab"] = self.din("wab", [L, 8, 128, 128])
        d["vecs"] = self.din("vecs", [L, 128, NV])
        d["vrep"] = self.din("vrep", [L, 128, 16])
        d["gpost"] = self.din("gpost", [L, D])
        d["consts"] = self.din("consts", [128, NC])
        d["svec"] = self.din("svec", [L, 128, NSV])
        d["sssd"] = self.din("sssd", [L, 128, 256])
        d["pk"] = self.din("pk", [L, 4, 128, PAST])
        d["pv"] = self.din("pv", [L, PAST, 512])
        o = {}
        o["yp"] = self.dout("yp", [max(TP, 128), D])
        o["ys"] = self.dout("ys", [128, D])
        for sfx, T in (("p", max(TP, 128)), ("s", 128)):
            o["lruh_" + sfx] = self.dout("lruh_" + sfx, [L, 128, 4])
            o["lruc_" + sfx] = self.dout("lruc_" + sfx, [L, 128, 12])
            o["ssd_" + sfx] = self.dout("ssd_" + sfx, [L, 128, 256])
            o["ssdc_" + sfx] = self.dout("ssdc_" + sfx, [L, 128, 24])
            o["cfc_" + sfx] = self.dout("cfc_" + sfx, [L, 128, 120])
            o["k_" + sfx] = self.dout("k_" + sfx, [L, T, 512])
            o["v_" + sfx] = self.dout("v_" + sfx, [L, T, 512])
        self.d, self.o = d, o
        TSC = max(TP, 128)
        self.winb = nc.dram_tensor("winb", [L, D, DIN], BF16).ap()
        self.wdnb = nc.dram_tensor("wdnb", [L, 4, 512, D], BF16).ap()
        self.woutb = nc.dram_tensor("woutb", [L, D, D], BF16).ap()
        self.kscr = nc.dram_tensor("kscr", [4, 128, TSC], BF16).ap()
        self.vscr = nc.dram_tensor("vscr", [TSC, 512], BF16).ap()

        with contextlib.ExitStack() as es:
            P = self.P = Prog(nc)

            def sb(name, shape, dt=F32):
                return es.enter_context(nc.sbuf_tensor(name, list(shape), dt))

            self.psum = es.enter_context(nc.psum_tensor("psum", [128, 4096], F32))
            TSm = self.TSm = max(TS, 128) if TP > 0 else 128
            t = self.t = {}
            t["cst"] = sb("cst", [128, NC])
            t["identb"] = sb("identb", [128, 128], BF16)
            t["negL"] = sb("negL", [128, 128], BF16)
            t["cmb"] = sb("cmb", [128, 128], BF16)
            t["onesm"] = sb("onesm", [128, 128], BF16)
            t["negone"] = sb("negone", [128, 2], BF16)
            t["vec"] = sb("vec", [128, NV])
            t["vrep"] = sb("vrept", [128, 16])
            t["gpost"] = sb("gpostt", [128, D])
            t["cl"] = sb("cl", [128, 4])
            t["aneg"] = sb("aneg", [128, 8])
            t["sm"] = sb("sm", [128, 64])
            t["dl"] = sb("dl", [128, 16, 128], BF16)
            t["ds"] = sb("ds", [128, 32, 128], BF16)
            t["dc"] = sb("dc", [128, 31, 128], BF16)
            t["wab"] = sb("wabt", [128, 8, 128], BF16)
            t["wdt"] = sb("wdt", [128, 8, 8], BF16)
            t["w"] = [sb("wbuf%d" % i, [128, 8, 512], BF16) for i in range(3)]
            t["hT"] = sb("hT", [128, 8, TSm], BF16)
            t["xin"] = [sb("xin%d" % i, [128, D]) for i in range(2)]
            t["mixed"] = sb("mixed", [128, 8, TSm])
            t["uT"] = sb("uT", [128, 4, TSm], BF16)
            t["sg"] = sb("sg", [128, 4, TSm], BF16)
            t["QT"] = sb("QT", [128, 4, TSm], BF16)
            self.F = sb("F", [128, 8, TSm])
            self.F16 = self.F.bitcast(BF16)
            self.G = sb("G", [128, 4, 512])
            self.H = sb("H", [128, 4, TSm + 32], BF16)
            self.junk = sb("junk", [128, D], BF16)
            if TSm < 1024:
                self.Xs = sb("Xs", [128, 3072])
                self.Xa = sb("Xa", [128, 8192])
                self.Xh = sb("Xh", [128, 4, 1024], BF16)
            t["wab32"] = sb("wab32", [128, 8, 128])
            t["wdt32"] = sb("wdt32", [128, 8, 8])
            t["lhist"] = sb("lhist", [128, 4, 3], BF16)
            t["shist"] = sb("shist", [128, 8, 3], BF16)
            t["chist"] = sb("chist", [128, 4, 30], BF16)
            t["hst"] = sb("hst", [128, 4])
            t["S"] = sb("S", [128, 256])
            t["S16"] = sb("S16", [128, 256], BF16)
            t["tail"] = sb("tail", [128, 160])
            t["svec"] = sb("svect", [128, NSV])
            t["dts"] = sb("dts", [128, 4, 64])
            t["xdt"] = sb("xdt", [128, 512], BF16)
            t["xdt2"] = sb("xdt2", [128, 512], BF16)
            t["btm"] = sb("btm", [128, 256], BF16)
            t["cdec"] = sb("cdec", [128, 512], BF16)
            t["mt"] = sb("mt", [128, 1024], BF16)
            t["v16"] = [sb("v16_%d" % i, [128, 512], BF16) for i in range(2)]
            t["acc"] = sb("acc", [128, 32])
            t["eacc"] = sb("eacc", [128, 32])
            t["eacc2"] = sb("eacc2", [128, 32])

            self.setup_consts()
            self.convert_weights()
            seqp = Seq()
            seqp.name, seqp.T, seqp.TS, seqp.QW = "p", TP, TS, min(512, TS)
            seqp.xin, seqp.y, seqp.nvalid, seqp.past, seqp.valcol = d["xp"], o["yp"], TP, 0, C_VALP
            seqs = Seq()
            seqs.name, seqs.T, seqs.TS, seqs.QW = "s", 128, 128, 128
            seqs.xin, seqs.y, seqs.nvalid, seqs.past, seqs.valcol = d["xs"], o["ys"], cfg.SVALID, PAST, C_VALS
            try:
                for l in range(L):
                    self.layer_setup(l)
                    if TP > 0:
                        self.run_seq(seqp, l)
                    self.run_seq(seqs, l)
            except _Stop:
                pass
            P.emit()
        return nc

    def f(self, i, w=None):
        w = self.TSm if w is None else w
        return self.F[:, i, 0:w]

    def h(self, i, w=None):
        w = self.TSm + 32 if w is None else w
        return self.H[:, i, 0:w]

    def ssd_scr(self, off, n):
        if self.TSm >= 1024:
            return bass.AP(self.F, 4 * self.TSm + off, [[8 * self.TSm, 128], [1, n]])
        return self.Xs[:, off:off + n]

    def att_scr(self, off, n):
        if self.TSm >= 1024:
            return bass.AP(self.F, off, [[8 * self.TSm, 128], [1, n]])
        return self.Xa[:, off:off + n]

    def att_h(self, i):
        if self.TSm >= 1024:
            return self.H[:, i, 0:1024]
        return self.Xh[:, i, :]

    def mixb(self, oc, o0, w):
        TSm = self.TSm
        return bass.AP(self.F16, 8 * TSm + oc * TSm + o0, [[16 * TSm, 128], [1, w]])

    def stage32(self, i):
        if self.TSm >= 1024:
            return bass.AP(self.t["mixed"], i * 4096, [[8 * self.TSm, 128], [1, 4096]])
        return self.Xa[:, i * 4096:(i + 1) * 4096]

    def convert_weights(self):
        P, d = self.P, self.d
        L = self.cfg.DEPTH
        n = 0
        jobs = []
        for l in range(L):
            c0 = 0
            while c0 < DIN:
                nc_ = min(512, DIN - c0)
                jobs.append((d["w_in"][l, :, c0:c0 + nc_].rearrange("(kc p) n -> p kc n", p=128),
                             self.winb[l, :, c0:c0 + nc_].rearrange("(kc p) n -> p kc n", p=128), 8, nc_))
                c0 += nc_
            for b in range(4):
                jobs.append((d["w_down"][l, b].rearrange("(kc p) n -> p kc n", p=128),
                             self.wdnb[l, b].rearrange("(kc p) n -> p kc n", p=128), 4, 1024))
            for hf in range(2):
                jobs.append((d["w_out"][l, :, hf * 512:(hf + 1) * 512].rearrange("(kc p) n -> p kc n", p=128),
                             self.woutb[l, :, hf * 512:(hf + 1) * 512].rearrange("(kc p) n -> p kc n", p=128), 8, 512))
        for (src, dst, a, b) in jobs:
            s32 = self.stage32(n % 2)
            s32v = fap(s32[:, 0:1], [[b, a], [1, b]])
            wb = self.t["w"][n % 3]
            wbv = fap(wb[:, 0, 0:1], [[b, a], [1, b]])
            P.dma("sp", s32v, src)
            P.copy(wbv, s32v, eng=("dve", "act", "pool")[n % 3])
            P.dma("sp", dst, wbv)
            n += 1

    def setup_consts(self):
        P, t = self.P, self.t
        cst = t["cst"]
        P.dma("sp", cst[:], self.d["consts"])
        P.copy(t["identb"][:], cst[:, C_ID:C_ID + 128])
        P.tt(self.G[:, 0, 0:128], cst[:, C_U:C_U + 128], cst[:, C_ID:C_ID + 128], ALU.add)
        P.ts(t["negL"][:], self.G[:, 0, 0:128], -1.0)
        P.copy(t["cmb"][:], cst[:, C_CM:C_CM + 128])
        P.ts(t["onesm"][:], cst[:, C_ONE:C_ONE + 128], 1.0 / 512.0)
        P.memset(t["negone"][:], -1.0)
        P.memset(t["tail"][:], 0.0)

    def layer_setup(self, l):
        P, t, d = self.P, self.t, self.d
        vec = t["vec"]
        P.dma("sp", vec[:], d["vecs"][l])
        P.dma("sp", t["vrep"][:], d["vrep"][l])
        gp = d["gpost"]
        P.dma("sp", t["gpost"][:], bass.AP(gp.tensor, l * D, [[0, 128], [1, D]]))
        P.dma("sp", t["wab32"][:], d["wab"][l].rearrange("i p m -> p i m"))
        P.copy(t["wab"][:], t["wab32"][:])
        P.dma("sp", t["wdt32"][:],
              d["w_in"][l, :, COL["ssd_dt"]:COL["ssd_dt"] + 8].rearrange("(kc p) n -> p kc n", p=128))
        P.copy(t["wdt"][:], t["wdt32"][:])
        sm = t["sm"]
        lam = vec[:, V_LLAM:V_LLAM + 4]
        P.act(sm[:, 0:4], lam, AF.Abs)
        P.act(sm[:, 4:8], sm[:, 0:4], AF.Exp, scale=-1.0)
        P.act(sm[:, 8:12], sm[:, 4:8], AF.Ln, bias=1.0)
        P.ts(sm[:, 12:16], lam, -1.0, 0.0, op0=ALU.mult, op1=ALU.max)
        P.tt(sm[:, 16:20], sm[:, 8:12], sm[:, 12:16], ALU.add)
        P.ts(t["cl"][:], sm[:, 16:20], -8.0)
        P.act(sm[:, 24:32], t["vrep"][:, 8:16], AF.Exp)
        P.ts(t["aneg"][:], sm[:, 24:32], -1.0)
        idf = t["cst"][:, C_ID:C_ID + 128]
        for i in range(16):
            P.ts(t["dl"][:, i, :], idf, vec[:, V_LCW + i:V_LCW + i + 1], eng="pool" if i % 2 else "dve")
        for i in range(32):
            P.ts(t["ds"][:, i, :], idf, vec[:, V_SCW + i:V_SCW + i + 1], eng="pool" if i % 2 else "dve")

    def load_w(self, l, c0, ncol=512):
        buf = self.wbuf()
        src = self.winb[l, :, c0:c0 + ncol].rearrange("(kc p) n -> p kc n", p=128)
        self.P.dma("sp", buf[:, :, 0:ncol], src)
        return buf

    def proj(self, wb, wc0, o0, w):
        P, t = self.P, self.t
        ps = self.bank()[:, 0:w]
        for k in range(8):
            P.mm(ps, wb[:, k, wc0:wc0 + 128], t["hT"][:, k, o0:o0 + w], start=(k == 0), stop=(k == 7))
        return ps

    def ttiles(self, TS):
        w = min(512, TS)
        return [(o0, w) for o0 in range(0, TS, w)]

    def rsqrt_rep(self, out, ps):
        P = self.P
        P.act(out, ps, AF.Ln, bias=EPS)
        P.act(out, out, AF.Exp, scale=-0.5)

    def run_seq(self, sq, l):
        P, t, d, o = self.P, self.t, self.d, self.o
        TS = sq.TS
        nst = sq.T // TS
        if sq.name == "p":
            for nm in ("lhist", "shist", "chist", "hst", "S", "S16"):
                P.memset(t[nm][:], 0.0, eng="pool")
        else:
            sv = t["svec"]
            P.dma("sp", sv[:], d["svec"][l])
            P.dma("sp", t["S"][:], d["sssd"][l])
            P.copy(t["S16"][:], t["S"][:])
            P.copy(t["hst"][:], sv[:, SV_H0:SV_H0 + 4])
            P.copy(t["lhist"][:], sv[:, SV_LH:SV_LH + 12].rearrange("p (c j) -> p c j", c=4))
            P.copy(t["shist"][:], sv[:, SV_SH:SV_SH + 24].rearrange("p (c j) -> p c j", c=8))
            P.copy(t["chist"][:], sv[:, SV_CH:SV_CH + 120].rearrange("p (c j) -> p c j", c=4))
        for st in range(nst):
            self.supertile(sq, l, st, last=(st == nst - 1))
        sfx = sq.name
        P.dma("sp", o["lruh_" + sfx][l], t["hst"][:])
        P.dma("sp", o["ssd_" + sfx][l], t["S"][:])
        P.dma("sp", o["lruc_" + sfx][l], t["tail"][:, 0:12])
        P.dma("sp", o["ssdc_" + sfx][l], t["tail"][:, 16:40])
        P.dma("sp", o["cfc_" + sfx][l], t["tail"][:, 40:160])

    def merge(self, sq, l, b, TS):
        P, t, d = self.P, self.t, self.d
        wd = self.wbuf()
        wdv = fap(wd[:], [[1024, 4], [1, 1024]])
        P.dma("sp", wdv, self.wdnb[l, b].rearrange("(kc p) n -> p kc n", p=128))
        for half in range(2):
            wm = self.load_w(l, COL["merge"] + b * 1024 + half * 512)
            for ocl in range(4):
                oc = half * 4 + ocl
                for (o0, w) in self.ttiles(TS):
                    py = self.bank()[:, 0:w]
                    for k in range(4):
                        P.mm(py, bass.AP(wd, k * 1024 + oc * 128, [[4096, 128], [1, 128]]),
                             t["uT"][:, k, o0:o0 + w], start=(k == 0), stop=(k == 3))
                    pg = self.proj(wm, ocl * 128, o0, w)
                    g1 = self.gt(w)
                    P.act(g1, pg, AF.Sigmoid)
                    if b == 0:
                        P.tt(t["mixed"][:, oc, o0:o0 + w], py, g1, ALU.mult)
                    else:
                        g2 = self.gt(w)
                        P.tt(g2, py, g1, ALU.mult)
                        dst = self.mixb(oc, o0, w) if b == 3 else t["mixed"][:, oc, o0:o0 + w]
                        P.tt(dst, t["mixed"][:, oc, o0:o0 + w], g2, ALU.add, eng="pool")

    def supertile(self, sq, l, st, last):
        P, t, d, o = self.P, self.t, self.d, self.o
        TS = sq.TS
        nblk = TS // 128
        t0 = st * TS
        tts = self.ttiles(TS)
        vec, hT, sm, sg, uT = t["vec"], t["hT"], t["sm"], t["sg"], t["uT"]
        f, h = self.f, self.h
        nv = min(sq.nvalid - t0, TS)
        xsrc = sq.xin if l == 0 else sq.y
        cst = t["cst"]
        ident = cst[:, C_ID:C_ID + 128]
        sfx = sq.name

        def in_tile(c0, c1, o0, w):
            return o0 <= c0 and c1 <= o0 + w

        xn = fap(self.G[:, 0, :], [[1, 1024]])
        for b in range(nblk):
            xt = t["xin"][b % 2]
            P.dma("sp", xt[:], xsrc[t0 + b * 128:t0 + (b + 1) * 128, :])
            ssq = sm[:, 32:33]
            P.act(self.junk[:], xt[:], AF.Square, accum_out=ssq)
            rs = sm[:, 34:35]
            P.act(rs, ssq, AF.Ln, bias=EPS, scale=1.0 / D)
            P.act(rs, rs, AF.Exp, scale=-0.5)
            P.ts(xn, xt[:], rs)
            for half in range(2):
                ps = self.bank()
                for j in range(4):
                    jj = half * 4 + j
                    P.tr(ps[:, j * 128:(j + 1) * 128], xn[:, jj * 128:(jj + 1) * 128], ident)
                gp = fap(vec[:, V_GPRE + half * 4:V_GPRE + half * 4 + 4], [[1, 4], [0, 128]])
                P.tt(hT[:, half * 4:half * 4 + 4, b * 128:(b + 1) * 128],
                     ps.rearrange("p (j q) -> p j q", j=4), gp, ALU.mult)

        self.stage(1)
        wb = self.load_w(l, COL["lru_g"])
        for c in range(4):
            for (o0, w) in tts:
                ps = self.proj(wb, c * 128, o0, w)
                P.act(sg[:, c, o0:o0 + w], ps, AF.Silu)
        self.stage(2)
        wb = self.load_w(l, COL["lru_x"])
        for c in range(4):
            lxh = h(c % 2)
            P.copy(lxh[:, 0:3], t["lhist"][:, c, :])
            for (o0, w) in tts:
                ps = self.proj(wb, c * 128, o0, w)
                P.copy(lxh[:, 3 + o0:3 + o0 + w], ps, eng="act")
                if last and in_tile(nv - 3, nv, o0, w):
                    P.copy(t["tail"][:, c * 3:c * 3 + 3], ps[:, nv - 3 - o0:nv - o0])
            P.copy(t["lhist"][:, c, :], lxh[:, TS:TS + 3])
            self.stage(2.1)
            xc, xc16 = f(0, TS), h(2, TS)
            for (o0, w) in tts:
                psc = self.bank()[:, 0:w]
                for k in range(4):
                    P.mm(psc, t["dl"][:, k * 4 + c, :], lxh[:, o0 + k:o0 + k + w], start=(k == 0), stop=(k == 3))
                P.act(xc[:, o0:o0 + w], psc, AF.Identity, bias=vec[:, V_LCB + c:V_LCB + c + 1])
                P.copy(xc16[:, o0:o0 + w], xc[:, o0:o0 + w])
            self.stage(2.2)
            r_, i_ = f(1, TS), f(2, TS)
            for (o0, w) in tts:
                pa = self.bank()[:, 0:w]
                P.mm(pa, t["wab"][:, c, :], xc16[:, o0:o0 + w])
                P.act(r_[:, o0:o0 + w], pa, AF.Sigmoid, bias=vec[:, V_LBA + c:V_LBA + c + 1])
                px = self.bank()[:, 0:w]
                P.mm(px, t["wab"][:, 4 + c, :], xc16[:, o0:o0 + w])
                P.act(i_[:, o0:o0 + w], px, AF.Sigmoid, bias=vec[:, V_LBX + c:V_LBX + c + 1])
            self.stage(2.3)
            a_, th, m_ = f(3, TS), f(4, TS), f(5, TS)
            clc = t["cl"][:, c:c + 1]
            P.act(a_, r_, AF.Exp, scale=clc)
            P.act(th, r_, AF.Tanh, scale=clc)
            self.stage(2.4)
            P.tt(m_, a_, a_, ALU.mult)
            P.stt(m_, m_, 1.0, th, ALU.add, ALU.mult)
            self.stage(2.5)
            P.act(m_, m_, AF.Sqrt, scale=-1.0)
            P.tt(i_, i_, xc, ALU.mult)
            P.tt(i_, i_, m_, ALU.mult)
            self.stage(2.6)
            hh = f(1, TS)
            P.scan(hh, a_, i_, t["hst"][:, c:c + 1])
            P.copy(t["hst"][:, c:c + 1], hh[:, nv - 1:nv])
            self.stage(2.7)
            P.tt(uT[:, c, 0:TS], hh, sg[:, c, 0:TS], ALU.mult)
        self.stage(3)
        self.merge(sq, l, 0, TS)

        self.stage(4)
        wb = self.load_w(l, COL["ssd_z"])
        for c in range(4):
            for (o0, w) in tts:
                ps = self.proj(wb, c * 128, o0, w)
                P.act(sg[:, c, o0:o0 + w], ps, AF.Silu)
        bc = t["QT"]
        for grp in range(2):
            wb = self.load_w(l, COL["ssd_x"] + grp * 512)
            for c in range(4):
                c8 = grp * 4 + c
                sxh = h(c8 % 2)
                P.copy(sxh[:, 0:3], t["shist"][:, c8, :])
                for (o0, w) in tts:
                    ps = self.proj(wb, c * 128, o0, w)
                    P.copy(sxh[:, 3 + o0:3 + o0 + w], ps, eng="act")
                    if last and in_tile(nv - 3, nv, o0, w):
                        P.copy(t["tail"][:, 16 + c8 * 3:16 + c8 * 3 + 3], ps[:, nv - 3 - o0:nv - o0])
                P.copy(t["shist"][:, c8, :], sxh[:, TS:TS + 3])
                for (o0, w) in tts:
                    psc = self.bank()[:, 0:w]
                    for k in range(4):
                        P.mm(psc, t["ds"][:, k * 8 + c8, :], sxh[:, o0 + k:o0 + k + w], start=(k == 0), stop=(k == 3))
                    dst = f(c)[:, o0:o0 + w] if grp == 0 else bc[:, c, o0:o0 + w]
                    P.act(dst, psc, AF.Silu, bias=vec[:, V_SCB + c8:V_SCB + c8 + 1])
        self.stage(5)
        dts = t["dts"]
        pd = self.bank()
        for b in range(nblk):
            for k in range(8):
                P.mm(pd[:, b * 8:b * 8 + 8], hT[:, k, b * 128:(b + 1) * 128], t["wdt"][:, k, :],
                     start=(k == 0), stop=(k == 7))
        nb8 = nblk * 8
        dx, dabs, dln, dt_, adt = dts[:, 0, 0:nb8], dts[:, 1, 0:nb8], dts[:, 2, 0:nb8], dts[:, 3, 0:nb8], dts[:, 1, 0:nb8]
        P.tt(dx.rearrange("p (b h) -> p b h", h=8), pd[:, 0:nb8].rearrange("p (b h) -> p b h", h=8),
             fap(t["vrep"][:, 0:8], [[0, nblk], [1, 8]]), ALU.add)
        P.act(dabs, dx, AF.Abs)
        P.act(dln, dabs, AF.Exp, scale=-1.0)
        P.act(dln, dln, AF.Ln, bias=1.0)
        P.ts(dabs, dx, 0.0, op0=ALU.max)
        P.tt(dt_, dabs, dln, ALU.add)
        P.ts(dt_, dt_, cst[:, sq.valcol:sq.valcol + 1])
        P.tt(adt.rearrange("p (b h) -> p b h", h=8), dt_.rearrange("p (b h) -> p b h", h=8),
             fap(t["aneg"][:], [[0, nblk], [1, 8]]), ALU.mult)
        self.stage(6)
        tri = cst[:, C_TRI:C_TRI + 128]
        Umat = cst[:, C_U:C_U + 128]
        ones = cst[:, C_ONE:C_ONE + 128]
        S, S16 = t["S"], t["S16"]
        for b in range(nblk):
            bs = slice(b * 128, (b + 1) * 128)
            dtb = dts[:, 3, b * 8:b * 8 + 8]
            adtb = dts[:, 1, b * 8:b * 8 + 8]
            psx = self.bank()
            for c in range(4):
                P.tr(psx[:, c * 128:(c + 1) * 128], f(c)[:, bs], ident)
            psx3 = psx.rearrange("p (h q) -> p h q", h=8)
            P.tt(t["xdt"][:].rearrange("p (h q) -> p h q", h=8), psx3, fap(dtb, [[1, 8], [0, 64]]), ALU.mult)
            pds = self.bank()
            P.mm(pds[:, 0:8], Umat, adtb)
            P.mm(pds[:, 8:16], ones, adtb)
            P.act(sm[:, 40:56], pds[:, 0:16], AF.Exp)
            P.tt(sm[:, 56:64], sm[:, 40:48], dtb, ALU.mult)
            P.tt(t["xdt2"][:].rearrange("p (h q) -> p h q", h=8), psx3, fap(sm[:, 56:64], [[1, 8], [0, 64]]), ALU.mult)
            pbt = self.bank()
            for cb in range(2):
                P.mm(pbt[:, cb * 128:(cb + 1) * 128], bc[:, cb, bs], t["identb"][:])
            P.copy(t["btm"][:], pbt[:, 0:256], eng="act")
            pcb = self.bank(2)
            for g_ in range(4):
                cc, gl = g_ // 2, g_ % 2
                P.mm(pcb[:, gl * 512 + cc * 128:gl * 512 + (cc + 1) * 128], bc[gl * 64:(gl + 1) * 64, cc, bs],
                     bc[gl * 64:(gl + 1) * 64, 2 + cc, bs])
            rall = self.ssd_scr(0, 1024)
            dec = self.ssd_scr(1024, 1024)
            P.tt(rall.rearrange("p (h q) -> p h q", h=8), fap(tri, [[0, 8], [1, 128]]),
                 fap(adtb, [[1, 8], [0, 128]]), ALU.mult)
            pseg = self.bank(2)
            for hf in range(2):
                P.mm(pseg[:, hf * 512:(hf + 1) * 512], Umat, rall[:, hf * 512:(hf + 1) * 512], start=True, stop=False)
                P.mm(pseg[:, hf * 512:(hf + 1) * 512], ident,
                     cst[:, C_NEG:C_NEG + 512], start=False, stop=True)
            P.act(dec, pseg, AF.Exp)
            for cc in range(2):
                P.tt(t["mt"][:, cc * 512:(cc + 1) * 512].rearrange("p (g r q) -> p g r q", g=2, r=2),
                     dec[:, cc * 512:(cc + 1) * 512].rearrange("p (g r q) -> p g r q", g=2, r=2),
                     fap(pcb[:, cc * 128:cc * 128 + 1], [[512, 2], [0, 2], [1, 128]]), ALU.mult)
            ea = self.ssd_scr(2048, 1024)
            for gl in range(2):
                pe_ = self.bank()
                rsel = bass.AP(rall.tensor, rall.offset + 2 * gl * 128, [list(rall.ap[0]), [512, 2], [128, 2], [1, 128]])
                P.mm(pe_, ones, rsel)
                hs = slice(gl * 64, (gl + 1) * 64)
                eav = ea[hs, gl * 512:(gl + 1) * 512]
                P.act(eav, pe_[hs, :], AF.Exp)
                P.tt(t["cdec"][hs, :].rearrange("p (c r q) -> p c r q", c=2, r=2),
                     eav.rearrange("p (c r q) -> p c r q", c=2, r=2),
                     fap(bc[hs, 2, bs], [[self.TSm, 2], [0, 2], [1, 128]]), ALU.mult)
            py = self.bank(2)
            for c in range(4):
                cc, gl = c // 2, c % 2
                for r in range(2):
                    hh_ = 2 * c + r
                    outp = py[r * 64:(r + 1) * 64, gl * 512 + cc * 128:gl * 512 + (cc + 1) * 128]
                    P.mm(outp, t["xdt"][:, hh_ * 64:(hh_ + 1) * 64], t["mt"][:, hh_ * 128:(hh_ + 1) * 128],
                         start=True, stop=False)
                    P.mm(outp, S16[gl * 64:(gl + 1) * 64, (cc * 2 + r) * 64:(cc * 2 + r + 1) * 64],
                         t["cdec"][gl * 64:(gl + 1) * 64, (cc * 2 + r) * 128:(cc * 2 + r + 1) * 128],
                         start=False, stop=True)
            for c in range(4):
                cc, gl = c // 2, c % 2
                P.stt(f(c)[:, bs], f(c)[:, bs], vec[:, V_SD + c:V_SD + c + 1],
                      py[:, gl * 512 + cc * 128:gl * 512 + (cc + 1) * 128], ALU.mult, ALU.add)
            pst = self.bank()
            for g_ in range(4):
                cc, gl = g_ // 2, g_ % 2
                P.mm(pst[gl * 64:(gl + 1) * 64, cc * 128:(cc + 1) * 128], t["btm"][:, g_ * 64:(g_ + 1) * 64],
                     t["xdt2"][:, g_ * 128:(g_ + 1) * 128])
            for gl in range(2):
                hs = slice(gl * 64, (gl + 1) * 64)
                sv4 = S[hs, :].rearrange("p (c r q) -> p c r q", c=2, r=2)
                P.tt(sv4, sv4, fap(sm[hs, 48 + 2 * gl:48 + 2 * gl + 1], [[4, 2], [1, 2], [0, 64]]), ALU.mult)
            P.tt(S[:], S[:], pst[:, 0:256], ALU.add)
            P.copy(S16[:], S[:])
        self.stage(7)
        for c in range(4):
            P.tt(f(c, TS), f(c, TS), sg[:, c, 0:TS], ALU.mult)
            P.act(h(c, TS), f(c, TS), AF.Square)
        for (o0, w) in tts:
            pss = self.bank()[:, 0:w]
            for c in range(4):
                P.mm(pss, t["onesm"][:], h(c)[:, o0:o0 + w], start=(c == 0), stop=(c == 3))
            rstd = self.gt(w)
            self.rsqrt_rep(rstd, pss)
            for c in range(4):
                P.stt(uT[:, c, o0:o0 + w], f(c)[:, o0:o0 + w], vec[:, V_SNORM + c:V_SNORM + c + 1], rstd,
                      ALU.mult, ALU.mult)
        self.merge(sq, l, 1, TS)

        self.stage(8)
        wb = self.load_w(l, COL["cf_g"])
        for c in range(4):
            for (o0, w) in tts:
                ps = self.proj(wb, c * 128, o0, w)
                P.act(sg[:, c, o0:o0 + w], ps, AF.Silu)
        wbb = self.load_w(l, COL["cf_b"])
        wba = self.load_w(l, COL["cf_a"])
        idf = ident
        for c in range(4):
            for k in range(31):
                P.ts(t["dc"][:, k, :], idf, vec[:, V_CCW + k * 4 + c:V_CCW + k * 4 + c + 1],
                     eng="pool" if k % 2 else "dve")
            gh = h(c % 2)
            P.copy(gh[:, 0:30], t["chist"][:, c, :])
            for (o0, w) in tts:
                pb = self.proj(wbb, c * 128, o0, w)
                sig = self.gt(w)
                P.act(sig, pb, AF.Sigmoid)
                pa = self.proj(wba, c * 128, o0, w)
                P.tt(gh[:, 30 + o0:30 + o0 + w], pa, sig, ALU.mult)
                if last and in_tile(nv - 30, nv, o0, w):
                    P.tt(t["tail"][:, 40 + c * 30:40 + c * 30 + 30], pa[:, nv - 30 - o0:nv - o0],
                         sig[:, nv - 30 - o0:nv - o0], ALU.mult)
            P.copy(t["chist"][:, c, :], gh[:, TS:TS + 30])
            for (o0, w) in tts:
                psc = self.bank()[:, 0:w]
                for k in range(31):
                    P.mm(psc, t["dc"][:, k, :], gh[:, o0 + k:o0 + k + w], start=(k == 0), stop=(k == 30))
                P.act(f(c)[:, o0:o0 + w], psc, AF.Identity, bias=vec[:, V_CCB + c:V_CCB + c + 1])
        for c in range(4):
            P.copy(h(c, TS), f(c, TS), eng="pool" if c % 2 else "dve")
        for (o0, w) in tts:
            pm = self.bank()[:, 0:w]
            for c in range(4):
                P.mm(pm, t["onesm"][:], h(c)[:, o0:o0 + w], start=(c == 0), stop=(c == 3))
            for c in range(4):
                P.tt(f(c)[:, o0:o0 + w], f(c)[:, o0:o0 + w], pm, ALU.subtract)
        for c in range(4):
            P.act(h(c, TS), f(c, TS), AF.Square)
        for (o0, w) in tts:
            pv_ = self.bank()[:, 0:w]
            for c in range(4):
                P.mm(pv_, t["onesm"][:], h(c)[:, o0:o0 + w], start=(c == 0), stop=(c == 3))
            rstd = self.gt(w)
            self.rsqrt_rep(rstd, pv_)
            for c in range(4):
                P.tt(f(c)[:, o0:o0 + w], f(c)[:, o0:o0 + w], rstd, ALU.mult)
        for c in range(4):
            P.act(f(4 + c % 2, TS), f(c, TS), AF.Silu, scale=vec[:, V_CLG + c:V_CLG + c + 1],
                  bias=vec[:, V_CLB + c:V_CLB + c + 1])
            P.tt(uT[:, c, 0:TS], f(4 + c % 2, TS), sg[:, c, 0:TS], ALU.mult)
        self.merge(sq, l, 2, TS)

        self.stage(9)
        QT = t["QT"]
        wb = self.load_w(l, COL["q"])
        for c in range(4):
            for (o0, w) in tts:
                ps = self.proj(wb, c * 128, o0, w)
                P.act(QT[:, c, o0:o0 + w], ps, AF.Copy, scale=0.125)
        wb = self.load_w(l, COL["k"])
        for c in range(4):
            kt = h(c, TS)
            for (o0, w) in tts:
                ps = self.proj(wb, c * 128, o0, w)
                P.copy(kt[:, o0:o0 + w], ps, eng="act" if c % 2 else "dve")
            P.dma("sp", self.kscr[c, :, t0:t0 + TS], kt)
        for b in range(nblk):
            pk = self.bank()
            for k in range(8):
                P.mm(pk, hT[:, k, b * 128:(b + 1) * 128], wb[:, k, :], start=(k == 0), stop=(k == 7))
            kv = self.gt(512)
            P.copy(kv, pk, eng="act")
            P.dma("sp", o["k_" + sfx][l, t0 + b * 128:t0 + (b + 1) * 128, :], kv)
        wb = self.load_w(l, COL["v"])
        for b in range(nblk):
            pv_ = self.bank()
            for k in range(8):
                P.mm(pv_, hT[:, k, b * 128:(b + 1) * 128], wb[:, k, :], start=(k == 0), stop=(k == 7))
            kv = self.gt(512)
            P.copy(kv, pv_, eng="act")
            P.dma("sp", o["v_" + sfx][l, t0 + b * 128:t0 + (b + 1) * 128, :], kv)
            P.copy(t["v16"][b % 2][:], pv_)
            P.dma("sp", self.vscr[t0 + b * 128:t0 + (b + 1) * 128, :], t["v16"][b % 2][:])
        wb = self.load_w(l, COL["sb_g"])
        for c in range(4):
            for (o0, w) in tts:
                ps = self.proj(wb, c * 128, o0, w)
                P.act(sg[:, c, o0:o0 + w], ps, AF.Silu)

        self.stage(10)
        QW = sq.QW
        nsub = QW // 128
        for qt in range(TS // QW):
            self.attention(sq, l, t0, qt)
        self.stage(11)
        self.merge(sq, l, 3, TS)

        self.stage(12)
        wo = []
        for half in range(2):
            buf = self.wbuf()
            P.dma("sp", buf[:], self.woutb[l, :, half * 512:(half + 1) * 512].rearrange("(kc p) n -> p kc n", p=128))
            wo.append(buf)
        ot = fap(self.G[:, 0, :], [[1, 1024]])
        for b in range(nblk):
            xt = t["xin"][b % 2]
            P.dma("sp", xt[:], xsrc[t0 + b * 128:t0 + (b + 1) * 128, :])
            for half in range(2):
                po = self.bank()
                for k in range(8):
                    P.mm(po, self.mixb(k, b * 128, 128), wo[half][:, k, :], start=(k == 0), stop=(k == 7))
                P.copy(ot[:, half * 512:(half + 1) * 512], po, eng="act")
            ssq = sm[:, 36:37]
            P.act(self.junk[:], ot, AF.Square, accum_out=ssq)
            rs = sm[:, 38:39]
            P.act(rs, ssq, AF.Ln, bias=EPS, scale=1.0 / D)
            P.act(rs, rs, AF.Exp, scale=-0.5)
            P.stt(ot, ot, rs, t["gpost"][:], ALU.mult, ALU.mult)
            P.tt(ot, ot, xt[:], ALU.add)
            P.dma("sp", sq.y[t0 + b * 128:t0 + (b + 1) * 128, :], ot)

    def attention(self, sq, l, t0, qt):
        P, t, d = self.P, self.t, self.d
        QW = sq.QW
        nsub = QW // 128
        QT, sg, uT = t["QT"], t["sg"], t["uT"]
        q0 = qt * QW
        gq0 = t0 + q0
        acc = t["acc"]
        eaccs = [t["eacc"], t["eacc2"]]
        oacc = self.att_scr(2048, nsub * 512)
        tmpo = self.G[:, 3, :]
        P.memset(acc[:, 0:nsub * 8], 0.0)
        P.memset(eaccs[0][:, 0:nsub * 8], 1.0)
        P.memset(oacc, 0.0, eng="pool")
        spans = []
        if sq.past == 0:
            assert QW == 512
            nsp = gq0 // 512 + 1
            for sp_ in range(nsp - 1, -1, -1):
                spans.append(("cur", sp_ * 512, 512, sp_ == nsp - 1))
        else:
            spans.append(("cur", 0, QW, True))
            for sp_ in range(sq.past // 512 - 1, -1, -1):
                spans.append(("past", sp_ * 512, 512, False))

        kvbufs = {}

        def load_span(si):
            if si >= len(spans) or si in kvbufs:
                return
            kind, k0, klen, diag = spans[si]
            kvb = self.wbuf()
            nkb = klen // 128
            if kind == "cur":
                P.dma("sp", kvb[:, 0:4, 0:klen], self.kscr[:, :, k0:k0 + klen].rearrange("c p n -> p c n"))
                P.dma("sp", kvb[:, 4:4 + nkb, :], self.vscr[k0:k0 + klen, :].rearrange("(j p) n -> p j n", p=128))
            else:
                stg = self.att_scr(4096, 4096)
                sk = fap(stg[:, 0:1], [[512, 4], [1, klen]])
                svv = fap(stg[:, 2048:2049], [[512, nkb], [1, 512]])
                P.dma("sp", sk, d["pk"][l, :, :, k0:k0 + klen].rearrange("c p n -> p c n"))
                P.dma("sp", svv, d["pv"][l, k0:k0 + klen, :].rearrange("(j p) n -> p j n", p=128))
                P.copy(kvb[:, 0:4, 0:klen], sk)
                P.copy(kvb[:, 4:4 + nkb, :], svv, eng="act")
            kvbufs[si] = kvb

        items = []
        blk = 0
        for si, (kind, k0, klen, diag) in enumerate(spans):
            nkb = klen // 128
            for j in range(nkb - 1, -1, -1):
                for hp in range(4):
                    items.append(dict(si=si, j=j, hp=hp, diag=diag, c0=(j * 128 if diag else 0), blk=blk,
                                      first=(hp == 0), last=(hp == 3), sfirst=(j == nkb - 1 and hp == 0)))
                blk += 1
        n = len(items)
        paccs = {}

        def bufs(i):
            b = i % 2
            pz = self.psum[:, b * 1024:(b + 1) * 1024]
            et = self.att_scr(b * 1024, 1024)
            spt = self.att_h(b)
            wt = self.att_h(2 + b)
            po = self.psum[:, (4 + b) * 512:(5 + b) * 512]
            return pz, et, spt, wt, po

        def views(it, pz, et, spt, wt):
            c0 = it["c0"]
            wq = QW - c0
            return (fap(pz[:, c0:c0 + 1], [[512, 2], [1, wq]]), fap(et[:, c0:c0 + 1], [[QW, 2], [1, wq]]),
                    fap(spt[:, c0:c0 + 1], [[QW, 2], [1, wq]]), fap(wt[:, c0:c0 + 1], [[QW, 2], [1, wq]]))

        def stA(i):
            it = items[i]
            if it["sfirst"]:
                load_span(it["si"])
                load_span(it["si"] + 1)
            kvb = kvbufs[it["si"]]
            pz, et, spt, wt, po = bufs(i)
            c0, j, hp = it["c0"], it["j"], it["hp"]
            for r in range(2):
                rs_ = slice(r * 64, (r + 1) * 64)
                P.mm(pz[:, r * 512 + c0:r * 512 + QW], kvb[rs_, hp, j * 128:(j + 1) * 128],
                     QT[rs_, hp, q0 + c0:q0 + QW])

        def stB1(i):
            it = items[i]
            pz, et, spt, wt, po = bufs(i)
            pzv, etv, spv, wtv = views(it, pz, et, spt, wt)
            P.act(etv, pzv, AF.Exp)

        def stB(i):
            it = items[i]
            pz, et, spt, wt, po = bufs(i)
            pzv, etv, spv, wtv = views(it, pz, et, spt, wt)
            c0 = it["c0"]
            P.act(spv, etv, AF.Ln, bias=1.0)
            if it["diag"]:
                dv = fap(spt[:, c0:c0 + 1], [[QW, 2], [1, 128]])
                P.tt(dv, dv, fap(t["cmb"][:, 0:1], [[0, 2], [1, 128]]), ALU.mult)

        def stC(i):
            it = items[i]
            pz, et, spt, wt, po = bufs(i)
            c0, hp = it["c0"], it["hp"]
            sub0 = c0 // 128
            for r in range(2):
                P.mm(pz[:, r * 512 + c0:r * 512 + QW], t["negL"][:], spt[:, r * QW + c0:(r + 1) * QW],
                     start=False, stop=True, skip_group_check=True)
            pacc = self.psum[:, (6 + it["blk"] % 2) * 512:(7 + it["blk"] % 2) * 512]
            for r in range(2):
                hd = 2 * hp + r
                for sub in range(sub0, nsub):
                    qs = slice(r * QW + sub * 128, r * QW + (sub + 1) * 128)
                    P.mm(pacc[:, sub * 8 + hd:sub * 8 + hd + 1], spt[:, qs], t["negone"][:, 0:1])

        def stD(i):
            it = items[i]
            pz, et, spt, wt, po = bufs(i)
            pzv, etv, spv, wtv = views(it, pz, et, spt, wt)
            c0 = it["c0"]
            P.act(wtv, pzv, AF.Exp)
            if it["diag"]:
                dv = fap(wt[:, c0:c0 + 1], [[QW, 2], [1, 128]])
                P.tt(dv, dv, fap(t["cmb"][:, 0:1], [[0, 2], [1, 128]]), ALU.mult)
            if it["last"]:
                sub0 = c0 // 128
                pacc = self.psum[:, (6 + it["blk"] % 2) * 512:(7 + it["blk"] % 2) * 512]
                a_ = acc[:, sub0 * 8:nsub * 8]
                P.tt(a_, a_, pacc[:, sub0 * 8:nsub * 8], ALU.add)
                P.act(eaccs[(it["blk"] + 1) % 2][:, 0:nsub * 8], acc[:, 0:nsub * 8], AF.Exp)

        def stE(i):
            it = items[i]
            kvb = kvbufs[it["si"]]
            pz, et, spt, wt, po = bufs(i)
            c0, j, hp = it["c0"], it["j"], it["hp"]
            sub0 = c0 // 128
            for r in range(2):
                hd = 2 * hp + r
                for sub in range(sub0, nsub):
                    qs = slice(r * QW + sub * 128, r * QW + (sub + 1) * 128)
                    P.mm(po[:, (sub * 2 + r) * 64:(sub * 2 + r + 1) * 64], wt[:, qs],
                         kvb[:, 4 + j, hd * 64:(hd + 1) * 64])

        def stF(i):
            it = items[i]
            pz, et, spt, wt, po = bufs(i)
            c0, hp = it["c0"], it["hp"]
            sub0 = c0 // 128
            ns = nsub - sub0
            eacc = eaccs[it["blk"] % 2]
            pov = fap(po[:, sub0 * 128:sub0 * 128 + 1], [[128, ns], [64, 2], [1, 64]])
            eav = fap(eacc[:, sub0 * 8 + 2 * hp:sub0 * 8 + 2 * hp + 1], [[8, ns], [1, 2], [0, 64]])
            tv = fap(tmpo[:, 0:1], [[128, ns], [64, 2], [1, 64]])
            P.tt(tv, pov, eav, ALU.mult)
            ov = fap(oacc[:, sub0 * 512 + 2 * hp * 64:sub0 * 512 + 2 * hp * 64 + 1], [[512, ns], [64, 2], [1, 64]])
            P.tt(ov, ov, tv, ALU.add, eng="pool")

        for s_ in range(n + 2):
            if s_ < n:
                stA(s_)
                stB1(s_)
            if 0 <= s_ - 1 < n:
                stC(s_ - 1)
                stD(s_ - 1)
            if s_ < n:
                stB(s_)
            if 0 <= s_ - 2 < n:
                stE(s_ - 2)
                stF(s_ - 2)
        ident = t["cst"][:, C_ID:C_ID + 128]
        for sub in range(nsub):
            pt = self.bank(1, 4, 6)
            for c in range(4):
                P.tr(pt[:, c * 128:(c + 1) * 128], oacc[:, sub * 512 + c * 128:sub * 512 + (c + 1) * 128], ident)
            cs = slice(q0 + sub * 128, q0 + (sub + 1) * 128)
            P.tt(uT[:, :, cs], pt.rearrange("p (c q) -> p c q", c=4), sg[:, :, cs], ALU.mult)


def make_consts():
    c = np.zeros((128, NC), np.float32)
    j = np.arange(128)[:, None]
    s = np.arange(128)[None, :]
    c[:, C_ID:C_ID + 128] = (j == s)
    c[:, C_U:C_U + 128] = (j > s)
    c[:, C_TRI:C_TRI + 128] = (j <= s)
    for rep in range(4):
        c[:, C_NEG + rep * 128:C_NEG + (rep + 1) * 128] = np.where(s < j, -30000.0, 0.0)
    c[:, C_CM:C_CM + 128] = (j < s)
    c[:, C_ONE:C_ONE + 128] = 1.0
    c[:, C_VALP] = 1.0
    return c


def fm(v, nch):
    return np.ascontiguousarray(np.asarray(v, np.float32).reshape(nch, 128).T)


def host_prep(inp, cfg):
    L = cfg.DEPTH
    f32 = np.float32
    vecs = np.zeros((L, 128, NV), f32)
    vrep = np.zeros((L, 128, 16), f32)
    wab = np.zeros((L, 8, 128, 128), f32)
    for l in range(L):
        v = vecs[l]
        v[:, V_GPRE:V_GPRE + 8] = fm(inp["norm_pre"][l], 8)
        for k in range(4):
            v[:, V_LCW + k * 4:V_LCW + k * 4 + 4] = fm(inp["lru_conv_w"][l, k], 4)
            v[:, V_SCW + k * 8:V_SCW + k * 8 + 8] = fm(inp["ssd_conv_w"][l, k], 8)
        v[:, V_LCB:V_LCB + 4] = fm(inp["lru_conv_b"][l], 4)
        v[:, V_LBA:V_LBA + 4] = fm(inp["lru_ba"][l], 4)
        v[:, V_LBX:V_LBX + 4] = fm(inp["lru_bx"][l], 4)
        v[:, V_LLAM:V_LLAM + 4] = fm(inp["lru_lambda"][l], 4)
        v[:, V_SCB:V_SCB + 8] = fm(inp["ssd_conv_b"][l], 8)
        v[:, V_SD:V_SD + 4] = fm(np.repeat(np.asarray(inp["ssd_d"][l]), 64), 4)
        v[:, V_SNORM:V_SNORM + 4] = fm(inp["ssd_norm"][l], 4)
        for k in range(31):
            v[:, V_CCW + k * 4:V_CCW + k * 4 + 4] = fm(inp["cf_conv_w"][l, k], 4)
        v[:, V_CCB:V_CCB + 4] = fm(inp["cf_conv_b"][l], 4)
        v[:, V_CLG:V_CLG + 4] = fm(inp["cf_ln_g"][l], 4)
        v[:, V_CLB:V_CLB + 4] = fm(inp["cf_ln_b"][l], 4)
        vrep[l, :, 0:8] = np.asarray(inp["ssd_dt_bias"][l])[None, :]
        vrep[l, :, 8:16] = np.asarray(inp["ssd_a_log"][l])[None, :]
        for which, nm in enumerate(("lru_wa", "lru_wx")):
            w = np.asarray(inp[nm][l])
            for c in range(4):
                wab[l, which * 4 + c, 0:64, 0:64] = w[2 * c]
                wab[l, which * 4 + c, 64:128, 64:128] = w[2 * c + 1]
    consts = make_consts()
    consts[:cfg.SVALID, C_VALS] = 1.0
    shared = dict(w_in=np.ascontiguousarray(inp["w_in"], f32), w_down=np.ascontiguousarray(inp["w_down"], f32),
                  w_out=np.ascontiguousarray(inp["w_out"], f32), wab=wab, vecs=vecs, vrep=vrep,
                  gpost=np.ascontiguousarray(inp["norm_post"], f32), consts=consts)
    return shared


def sample_inputs(inp, cfg, s):
    L = cfg.DEPTH
    f32 = np.float32
    SV = cfg.SVALID
    xs = np.zeros((128, D), f32)
    xs[:SV] = inp["x_sample"][s]
    svec = np.zeros((L, 128, NSV), f32)
    sssd = np.zeros((L, 128, 256), f32)
    pk = np.zeros((L, 4, 128, cfg.PAST), f32)
    for l in range(L):
        svec[l, :, SV_H0:SV_H0 + 4] = fm(inp["state_lru_h"][l, s], 4)
        lc = np.asarray(inp["state_lru_conv"][l, s])
        svec[l, :, SV_LH:SV_LH + 12] = lc.T.reshape(4, 128, 3).transpose(1, 0, 2).reshape(128, 12)
        sc = np.asarray(inp["state_ssd_conv"][l, s])
        svec[l, :, SV_SH:SV_SH + 24] = sc.T.reshape(8, 128, 3).transpose(1, 0, 2).reshape(128, 24)
        cc = np.asarray(inp["state_cf_conv"][l, s])
        svec[l, :, SV_CH:SV_CH + 120] = cc.T.reshape(4, 128, 30).transpose(1, 0, 2).reshape(128, 120)
        st = np.asarray(inp["state_ssd"][l, s])
        st = st.reshape(2, 2, 2, 64, 64)
        sssd[l] = st.transpose(1, 4, 0, 2, 3).reshape(128, 256)
        k = np.asarray(inp["cache_sb_k"][l, s]).reshape(cfg.PAST, 512)
        pk[l] = k.T.reshape(4, 128, cfg.PAST)
    pv = np.ascontiguousarray(np.asarray(inp["cache_sb_v"][:, s]).reshape(L, cfg.PAST, 512), f32)
    return dict(xs=xs, svec=svec, sssd=sssd, pk=pk, pv=pv)


def unpack_states(res, sfx, L):
    lruh = res["lruh_" + sfx].transpose(0, 2, 1).reshape(L, 512)
    lruc = res["lruc_" + sfx].reshape(L, 128, 4, 3).transpose(0, 3, 2, 1).reshape(L, 3, 512)
    ssdc = res["ssdc_" + sfx][:, :, 0:24].reshape(L, 128, 8, 3).transpose(0, 3, 2, 1).reshape(L, 3, 1024)
    cfc = res["cfc_" + sfx].reshape(L, 128, 4, 30).transpose(0, 3, 2, 1).reshape(L, 30, 512)
    ssd = res["ssd_" + sfx].reshape(L, 2, 64, 2, 2, 64)
    ssd = ssd.transpose(0, 3, 1, 4, 5, 2).reshape(L, 8, 64, 64)
    return lruh, lruc, ssd, ssdc, cfc


_NC_CACHE = {}


def run_cfg(inp, cfg, n_cores=8):
    key = (cfg.TP, cfg.TS, cfg.DEPTH, cfg.PAST, cfg.SVALID)
    if key not in _NC_CACHE:
        _NC_CACHE[key] = Builder(cfg).build()
    nc = _NC_CACHE[key]
    shared = host_prep(inp, cfg)
    L = cfg.DEPTH
    nb = inp["x_prompt"].shape[0] if cfg.TP > 0 else 0
    nsmp = inp["x_sample"].shape[0]
    in_maps = []
    for c in range(n_cores):
        m = dict(shared)
        if cfg.TP > 0:
            m["xp"] = np.ascontiguousarray(inp["x_prompt"][(c // 2) % nb], np.float32)
        else:
            m["xp"] = np.zeros((128, D), np.float32)
        m.update(sample_inputs(inp, cfg, c % nsmp))
        in_maps.append(m)
    res = run_bass_kernel_spmd(nc, in_maps, core_ids=list(range(n_cores))).results
    SV = cfg.SVALID
    outs = {}
    if cfg.TP > 0:
        pcs = [2 * b for b in range(nb)]
        outs["y_p"] = np.stack([res[c]["yp"] for c in pcs])
        st = [unpack_states(res[c], "p", L) for c in pcs]
        for i, nm in enumerate(("lru_h_p", "lru_conv_p", "ssd_p", "ssd_conv_p", "cf_conv_p")):
            outs[nm] = np.stack([s[i] for s in st], axis=1)
        outs["k_p"] = np.stack([res[c]["k_p"].reshape(L, cfg.TP, 8, 64) for c in pcs], axis=1)
        outs["v_p"] = np.stack([res[c]["v_p"].reshape(L, cfg.TP, 8, 64) for c in pcs], axis=1)
    scs = list(range(min(nsmp, n_cores)))
    outs["y_s"] = np.stack([res[c]["ys"][:SV] for c in scs])
    st = [unpack_states(res[c], "s", L) for c in scs]
    for i, nm in enumerate(("lru_h_s", "lru_conv_s", "ssd_s", "ssd_conv_s", "cf_conv_s")):
        outs[nm] = np.stack([s[i] for s in st], axis=1)
    outs["k_s"] = np.stack([res[c]["k_s"][:, :SV].reshape(L, SV, 8, 64) for c in scs], axis=1)
    outs["v_s"] = np.stack([res[c]["v_s"][:, :SV].reshape(L, SV, 8, 64) for c in scs], axis=1)
    return outs


ORDER = ("y_p", "y_s", "lru_h_p", "lru_h_s", "lru_conv_p", "lru_conv_s", "ssd_p", "ssd_s",
         "ssd_conv_p", "ssd_conv_s", "cf_conv_p", "cf_conv_s", "k_p", "k_s", "v_p", "v_s")


def kernel(**inputs):
    inp = {k: np.asarray(v) for k, v in inputs.items()}
    cfg = Cfg(TP=inp["x_prompt"].shape[1], TS=1024, DEPTH=inp["w_in"].shape[0],
              PAST=inp["cache_sb_k"].shape[2], SVALID=inp["x_sample"].shape[1])
    outs = run_cfg(inp, cfg, 8)
    return tuple(np.ascontiguousarray(outs[k], dtype=np.float32) for k in ORDER)
```

```python
import contextlib
import numpy as np
import ml_dtypes
import concourse.bass as bass
import concourse.mybir as mybir
from concourse.bass_utils import run_bass_kernel_spmd

F32 = mybir.dt.float32
BF16 = mybir.dt.bfloat16
AF = mybir.ActivationFunctionType
ALU = mybir.AluOpType

CENG = ("pe", "act", "dve", "pool")
NSLOT = 24

D = 1024
DIN = 10248
EPS = 1e-6
COL = dict(lru_x=0, lru_g=512, ssd_z=1024, ssd_x=1536, ssd_bc=2048, ssd_dt=2560, cf_a=2568, cf_b=3080,
           cf_g=3592, q=4104, k=4616, v=5128, sb_g=5640, merge=6152)
V_GPRE, V_LCW, V_LCB, V_LBA, V_LBX, V_LLAM = 0, 8, 24, 28, 32, 36
V_SCW, V_SCB, V_SD, V_SNORM = 40, 72, 80, 84
V_CCW, V_CCB, V_CLG, V_CLB = 88, 212, 216, 220
NV = 224
C_ID, C_U, C_TRI, C_NEG, C_CM, C_ONE, C_VALP, C_VALS = 0, 128, 256, 384, 896, 1024, 1152, 1153
NC = 1154
SV_H0, SV_LH, SV_SH, SV_CH = 0, 4, 16, 40
NSV = 160


def _rect(ap):
    t = ap.tensor
    dims = list(ap.ap)
    if str(ap.space) == "DRAM":
        lo = ap.offset
        hi = lo
        for st, n in dims:
            if st >= 0:
                hi += st * (n - 1)
            else:
                lo += st * (n - 1)
        return ("D:" + t.name, 0, 1, lo, hi + 1)
    pst, pn = dims[0]
    if pst == 0:
        p0 = 0
        base = ap.offset
    else:
        p0 = ap.offset // pst
        base = ap.offset - p0 * pst
    lo = base
    hi = base
    for st, n in dims[1:]:
        if st >= 0:
            hi += st * (n - 1)
        else:
            lo += st * (n - 1)
    esz = mybir.dt.size(ap.dtype)
    if str(ap.space) == "PSUM":
        b0 = (lo * esz) // 2048 * 2048
        b1 = ((hi + 1) * esz + 2047) // 2048 * 2048
        q0 = p0 // 32 * 32
        q1 = (p0 + pn + 31) // 32 * 32
        return ("P:" + t.name, q0, q1, b0, b1)
    return ("S:" + t.name, p0, p0 + pn, lo * esz, (hi + 1) * esz)


class _Rec:
    __slots__ = ("p0", "p1", "f0", "f1", "w", "r")

    def __init__(self, p0, p1, f0, f1):
        self.p0, self.p1, self.f0, self.f1 = p0, p1, f0, f1
        self.w = {}
        self.r = {}


def _merge(dst, src):
    for k, v in src.items():
        if dst.get(k, -1) < v:
            dst[k] = v


class Ins:
    __slots__ = ("eng", "fn", "deps", "kind", "idx", "slot", "seq", "waits", "sig")


class Prog:
    def __init__(self, nc):
        self.nc = nc
        self.ins = []
        self.track = {}
        self.cnt = {e: 0 for e in CENG}
        self.ndma = 0
        self.slot_seq = [0] * NSLOT
        import os
        self.maxins = int(os.environ.get("KMAXINS", "100000000"))

    def _access(self, ident, reads, writes):
        deps = {}
        rr = [_rect(a) for a in reads]
        ww = [_rect(a) for a in writes]
        for key, p0, p1, f0, f1 in rr:
            for rec in self.track.get(key, ()):
                if rec.p0 < p1 and p0 < rec.p1 and rec.f0 < f1 and f0 < rec.f1:
                    _merge(deps, rec.w)
                    if key[0] == "P":
                        for k2, v2 in rec.r.items():
                            if k2 != ident[0] and deps.get(k2, -1) < v2:
                                deps[k2] = v2
        for key, p0, p1, f0, f1 in ww:
            for rec in self.track.get(key, ()):
                if rec.p0 < p1 and p0 < rec.p1 and rec.f0 < f1 and f0 < rec.f1:
                    _merge(deps, rec.w)
                    _merge(deps, rec.r)
        k, v = ident
        for key, p0, p1, f0, f1 in rr:
            lst = self.track.setdefault(key, [])
            hit = False
            for rec in lst:
                if rec.p0 < p1 and p0 < rec.p1 and rec.f0 < f1 and f0 < rec.f1:
                    if rec.r.get(k, -1) < v:
                        rec.r[k] = v
                    if rec.p0 <= p0 and p1 <= rec.p1 and rec.f0 <= f0 and f1 <= rec.f1:
                        hit = True
            if not hit:
                rec = _Rec(p0, p1, f0, f1)
                rec.r[k] = v
                lst.append(rec)
                if len(lst) > 64:
                    self._collapse(key)
        for key, p0, p1, f0, f1 in ww:
            lst = self.track.setdefault(key, [])
            keep = [rec for rec in lst
                    if not (p0 <= rec.p0 and rec.p1 <= p1 and f0 <= rec.f0 and rec.f1 <= f1)]
            rec = _Rec(p0, p1, f0, f1)
            rec.w[k] = v
            keep.append(rec)
            self.track[key] = keep
            if len(keep) > 64:
                self._collapse(key)
        return deps

    def _collapse(self, key):
        keep = self.track[key]
        big = _Rec(min(r.p0 for r in keep), max(r.p1 for r in keep),
                   min(r.f0 for r in keep), max(r.f1 for r in keep))
        for r in keep:
            _merge(big.w, r.w)
            _merge(big.r, r.r)
        self.track[key] = [big]

    def op(self, eng, fn, reads, writes):
        if len(self.ins) >= self.maxins:
            return None
        i = Ins()
        i.eng, i.fn, i.kind = eng, fn, "op"
        i.idx = self.cnt[eng]
        self.cnt[eng] += 1
        i.deps = self._access((eng, i.idx), reads, writes)
        if eng == "pe":
            i.deps.pop("pe", None)
        self.ins.append(i)
        return i

    def dma(self, queue, out, in_, **kw):
        if len(self.ins) >= self.maxins:
            return None
        i = Ins()
        i.eng, i.kind = queue, "dma"
        i.slot = self.ndma % NSLOT
        self.ndma += 1
        i.seq = self.slot_seq[i.slot]
        self.slot_seq[i.slot] += 1
        i.fn = lambda e: e.dma_start(out=out, in_=in_, **kw)
        i.deps = self._access((("d", i.slot), i.seq), [in_], [out])
        if i.seq > 0:
            k = ("d", i.slot)
            if i.deps.get(k, -1) < i.seq - 1:
                i.deps[k] = i.seq - 1
        self.ins.append(i)
        return i

    def emit(self):
        nc = self.nc
        known = {e: {} for e in list(CENG) + ["sp"]}
        clock = {}
        need_sig = set()
        for i in self.ins:
            kn = known[i.eng]
            waits = []
            for k, v in i.deps.items():
                if kn.get(k, -1) >= v:
                    continue
                waits.append((k, v))
                _merge(kn, clock[(k, v)])
                if kn.get(k, -1) < v:
                    kn[k] = v
                need_sig.add((k, v))
            i.waits = waits
            if i.kind == "op":
                clock[(i.eng, i.idx)] = dict(kn)
            else:
                clock[(("d", i.slot), i.seq)] = dict(kn)
        sigcount = {e: 0 for e in CENG}
        semval = {}
        for i in self.ins:
            if i.kind == "op":
                if (i.eng, i.idx) in need_sig:
                    sigcount[i.eng] += 1
                    i.sig = True
                    semval[(i.eng, i.idx)] = sigcount[i.eng]
                else:
                    i.sig = False
            else:
                semval[(("d", i.slot), i.seq)] = 16 * (i.seq + 1)
        per = {e: [] for e in list(CENG) + ["sp"]}
        for i in self.ins:
            per[i.eng].append(i)
        self.stats = {e: len(per[e]) for e in per}
        self.stats["nwait"] = sum(len(i.waits) for i in self.ins)
        self.stats["sig"] = dict(sigcount)
        with contextlib.ExitStack() as es:
            sems = {}
            for e in CENG:
                sems[e] = es.enter_context(nc.semaphore("c_" + e))
            for s in range(NSLOT):
                sems[("d", s)] = es.enter_context(nc.semaphore("d%d" % s))
            block = es.enter_context(nc.Block())

            def run(engname):
                def body(eobj):
                    for i in per[engname]:
                        for k, v in i.waits:
                            eobj.wait_ge(sems[k], semval[(k, v)])
                        r = i.fn(eobj)
                        if i.kind == "dma":
                            r.then_inc(sems[("d", i.slot)], 16)
                        elif i.sig:
                            r.then_inc(sems[i.eng], 1)
                    if engname == "sp":
                        for s in range(NSLOT):
                            if self.slot_seq[s] > 0:
                                eobj.wait_ge(sems[("d", s)], 16 * self.slot_seq[s])
                return body

            block.tensor(run("pe"))
            block.scalar(run("act"))
            block.vector(run("dve"))
            block.gpsimd(run("pool"))
            block.sync(run("sp"))

    def mm(self, out, lhsT, rhs, start=True, stop=True, **kw):
        return self.op("pe", lambda e: e.matmul(out, lhsT, rhs, start=start, stop=stop, **kw),
                       [lhsT, rhs], [out])

    def tr(self, out, in_, ident):
        return self.op("pe", lambda e: e.transpose(out, in_, ident), [in_, ident], [out])

    def act(self, out, in_, func, bias=None, scale=1.0, accum_out=None):
        reads = [in_]
        kw = {}
        if bias is not None:
            kw["bias"] = bias
            if not isinstance(bias, (int, float)):
                reads.append(bias)
        if not isinstance(scale, (int, float)):
            reads.append(scale)
        writes = [out]
        if accum_out is not None:
            kw["accum_out"] = accum_out
            writes.append(accum_out)
        return self.op("act", lambda e: e.activation(out=out, in_=in_, func=func, scale=scale, **kw),
                       reads, writes)

    def tt(self, out, in0, in1, op, eng="dve"):
        return self.op(eng, lambda e: e.tensor_tensor(out=out, in0=in0, in1=in1, op=op), [in0, in1], [out])

    def ts(self, out, in0, s1, s2=None, op0=ALU.mult, op1=None, eng="dve"):
        reads = [in0]
        if not isinstance(s1, (int, float)):
            reads.append(s1)
        if s2 is not None and not isinstance(s2, (int, float)):
            reads.append(s2)
        if op1 is None:
            return self.op(eng, lambda e: e.tensor_scalar(out=out, in0=in0, scalar1=s1, scalar2=None, op0=op0),
                           reads, [out])
        return self.op(eng, lambda e: e.tensor_scalar(out=out, in0=in0, scalar1=s1, scalar2=s2, op0=op0, op1=op1),
                       reads, [out])

    def stt(self, out, in0, scalar, in1, op0, op1):
        reads = [in0, in1]
        if not isinstance(scalar, (int, float)):
            reads.append(scalar)
        return self.op("dve", lambda e: e.scalar_tensor_tensor(out=out, in0=in0, scalar=scalar, in1=in1,
                                                               op0=op0, op1=op1), reads, [out])

    def copy(self, out, in_, eng="dve"):
        if eng == "act":
            return self.op("act", lambda e: e.copy(out=out, in_=in_), [in_], [out])
        return self.op(eng, lambda e: e.tensor_copy(out=out, in_=in_), [in_], [out])

    def memset(self, ap, val, eng="dve"):
        return self.op(eng, lambda e: e.memset(ap, val), [], [ap])

    def scan(self, out, d0, d1, init):
        reads = [d0, d1]
        if not isinstance(init, (int, float)):
            reads.append(init)
        return self.op("dve", lambda e: e.tensor_tensor_scan(out=out, data0=d0, data1=d1, initial=init,
                                                             op0=ALU.mult, op1=ALU.add), reads, [out])


def fap(ap, dims):
    return bass.AP(ap.tensor, ap.offset, [list(ap.ap[0])] + [list(d) for d in dims])


class Cfg:
    def __init__(self, TP=8192, TS=1024, DEPTH=4, PAST=1024, SVALID=64, stop=99):
        self.TP, self.TS, self.DEPTH, self.PAST, self.SVALID = TP, TS, DEPTH, PAST, SVALID
        self.stop = stop


class Seq:
    pass


class _Stop(Exception):
    pass


class Builder:
    def __init__(self, cfg):
        self.cfg = cfg
        self.nc = bass.Bass("TRN2", target_bir_lowering=False)
        self.bank_rr = 0
        self.wrr = 0
        self.grr = 0

    def din(self, name, shape, dt=F32):
        return self.nc.dram_tensor(name, list(shape), dt, kind="ExternalInput").ap()

    def dout(self, name, shape, dt=F32):
        return self.nc.dram_tensor(name, list(shape), dt, kind="ExternalOutput").ap()

    def bank(self, n=1, lo=0, hi=8):
        if not hasattr(self, "_brr"):
            self._brr = {}
        rr = self._brr.get((lo, hi), lo)
        if n == 2 and (rr - lo) % 2:
            rr += 1
        if rr + n > hi:
            rr = lo
        self._brr[(lo, hi)] = rr + n
        return self.psum[:, rr * 512:(rr + n) * 512]

    def stage(self, n):
        if n >= self.cfg.stop:
            raise _Stop()

    def wbuf(self):
        b = self.t["w"][self.wrr % 3]
        self.wrr += 1
        return b

    def gt(self, w=512):
        b = self.G[:, self.grr % 4, 0:w]
        self.grr += 1
        return b

    def build(self):
        cfg = self.cfg
        nc = self.nc
        L, TP, TS, PAST = cfg.DEPTH, cfg.TP, cfg.TS, cfg.PAST
        d = {}
        d["xp"] = self.din("xp", [max(TP, 128), D])
        d["xs"] = self.din("xs", [128, D])
        d["w_in"] = self.din("w_in", [L, D, DIN])
        d["w_down"] = self.din("w_down", [L, 4, 512, D])
        d["w_out"] = self.din("w_out", [L, D, D])
        d["wab"] = self.din("wab", [L, 8, 128, 128])
        d["vecs"] = self.din("vecs", [L, 128, NV])
        d["vrep"] = self.din("vrep", [L, 128, 16])
        d["gpost"] = self.din("gpost", [L, D])
        d["consts"] = self.din("consts", [128, NC])
        d["svec"] = self.din("svec", [L, 128, NSV])
        d["sssd"] = self.din("sssd", [L, 128, 256])
        d["pk"] = self.din("pk", [L, 4, 128, PAST])
        d["pv"] = self.din("pv", [L, PAST, 512])
        o = {}
        o["yp"] = self.dout("yp", [max(TP, 128), D])
        o["ys"] = self.dout("ys", [128, D])
        for sfx, T in (("p", max(TP, 128)), ("s", 128)):
            o["lruh_" + sfx] = self.dout("lruh_" + sfx, [L, 128, 4])
            o["lruc_" + sfx] = self.dout("lruc_" + sfx, [L, 128, 12])
            o["ssd_" + sfx] = self.dout("ssd_" + sfx, [L, 128, 256])
            o["ssdc_" + sfx] = self.dout("ssdc_" + sfx, [L, 128, 24])
            o["cfc_" + sfx] = self.dout("cfc_" + sfx, [L, 128, 120])
            o["k_" + sfx] = self.dout("k_" + sfx, [L, T, 512])
            o["v_" + sfx] = self.dout("v_" + sfx, [L, T, 512])
        self.d, self.o = d, o
        TSC = max(TP, 128)
        self.winb = nc.dram_tensor("winb", [L, D, DIN], BF16).ap()
        self.wdnb = nc.dram_tensor("wdnb", [L, 4, 512, D], BF16).ap()
        self.woutb = nc.dram_tensor("woutb", [L, D, D], BF16).ap()
        self.kscr = nc.dram_tensor("kscr", [4, 128, TSC], BF16).ap()
        self.vscr = nc.dram_tensor("vscr", [TSC, 512], BF16).ap()

        with contextlib.ExitStack() as es:
            P = self.P = Prog(nc)

            def sb(name, shape, dt=F32):
                return es.enter_context(nc.sbuf_tensor(name, list(shape), dt))

            self.psum = es.enter_context(nc.psum_tensor("psum", [128, 4096], F32))
            TSm = self.TSm = max(TS, 128) if TP > 0 else 128
            t = self.t = {}
            t["cst"] = sb("cst", [128, NC])
            t["identb"] = sb("identb", [128, 128], BF16)
            t["negL"] = sb("negL", [128, 128], BF16)
            t["cmb"] = sb("cmb", [128, 128], BF16)
            t["onesm"] = sb("onesm", [128, 128], BF16)
            t["negone"] = sb("negone", [128, 2], BF16)
            t["vec"] = sb("vec", [128, NV])
            t["vrep"] = sb("vrept", [128, 16])
            t["gpost"] = sb("gpostt", [128, D])
            t["cl"] = sb("cl", [128, 4])
            t["aneg"] = sb("aneg", [128, 8])
            t["sm"] = sb("sm", [128, 64])
            t["dl"] = sb("dl", [128, 16, 128], BF16)
            t["ds"] = sb("ds", [128, 32, 128], BF16)
            t["dc"] = sb("dc", [128, 31, 128], BF16)
            t["wab"] = sb("wabt", [128, 8, 128], BF16)
            t["wdt"] = sb("wdt", [128, 8, 8], BF16)
            t["w"] = [sb("wbuf%d" % i, [128, 8, 512], BF16) for i in range(3)]
            t["hT"] = sb("hT", [128, 8, TSm], BF16)
            t["xin"] = [sb("xin%d" % i, [128, D]) for i in range(2)]
            t["mixed"] = sb("mixed", [128, 8, TSm])
            t["uT"] = sb("uT", [128, 4, TSm], BF16)
            t["sg"] = sb("sg", [128, 4, TSm], BF16)
            t["QT"] = sb("QT", [128, 4, TSm], BF16)
            self.F = sb("F", [128, 8, TSm])
            self.F16 = self.F.bitcast(BF16)
            self.G = sb("G", [128, 4, 512])
            self.H = sb("H", [128, 4, TSm + 32], BF16)
            self.junk = sb("junk", [128, D], BF16)
            if TSm < 1024:
                self.Xs = sb("Xs", [128, 3072])
                self.Xa = sb("Xa", [128, 8192])
                self.Xh = sb("Xh", [128, 4, 1024], BF16)
            t["wab32"] = sb("wab32", [128, 8, 128])
            t["wdt32"] = sb("wdt32", [128, 8, 8])
            t["lhist"] = sb("lhist", [128, 4, 3], BF16)
            t["shist"] = sb("shist", [128, 8, 3], BF16)
            t["chist"] = sb("chist", [128, 4, 30], BF16)
            t["hst"] = sb("hst", [128, 4])
            t["S"] = sb("S", [128, 256])
            t["S16"] = sb("S16", [128, 256], BF16)
            t["tail"] = sb("tail", [128, 160])
            t["svec"] = sb("svect", [128, NSV])
            t["dts"] = sb("dts", [128, 4, 64])
            t["xdt"] = sb("xdt", [128, 512], BF16)
            t["xdt2"] = sb("xdt2", [128, 512], BF16)
            t["btm"] = sb("btm", [128, 256], BF16)
            t["cdec"] = sb("cdec", [128, 512], BF16)
            t["mt"] = sb("mt", [128, 1024], BF16)
            t["v16"] = [sb("v16_%d" % i, [128, 512], BF16) for i in range(2)]
            t["acc"] = sb("acc", [128, 32])
            t["eacc"] = sb("eacc", [128, 32])
            t["eacc2"] = sb("eacc2", [128, 32])

            self.setup_consts()
            self.convert_weights()
            seqp = Seq()
            seqp.name, seqp.T, seqp.TS, seqp.QW = "p", TP, TS, min(512, TS)
            seqp.xin, seqp.y, seqp.nvalid, seqp.past, seqp.valcol = d["xp"], o["yp"], TP, 0, C_VALP
            seqs = Seq()
            seqs.name, seqs.T, seqs.TS, seqs.QW = "s", 128, 128, 128
            seqs.xin, seqs.y, seqs.nvalid, seqs.past, seqs.valcol = d["xs"], o["ys"], cfg.SVALID, PAST, C_VALS
            try:
                for l in range(L):
                    self.layer_setup(l)
                    if TP > 0:
                        self.run_seq(seqp, l)
                    self.run_seq(seqs, l)
            except _Stop:
                pass
            P.emit()
        return nc

    def f(self, i, w=None):
        w = self.TSm if w is None else w
        return self.F[:, i, 0:w]

    def h(self, i, w=None):
        w = self.TSm + 32 if w is None else w
        return self.H[:, i, 0:w]

    def ssd_scr(self, off, n):
        if self.TSm >= 1024:
            return bass.AP(self.F, 4 * self.TSm + off, [[8 * self.TSm, 128], [1, n]])
        return self.Xs[:, off:off + n]

    def att_scr(self, off, n):
        if self.TSm >= 1024:
            return bass.AP(self.F, off, [[8 * self.TSm, 128], [1, n]])
        return self.Xa[:, off:off + n]

    def att_h(self, i):
        if self.TSm >= 1024:
            return self.H[:, i, 0:1024]
        return self.Xh[:, i, :]

    def mixb(self, oc, o0, w):
        TSm = self.TSm
        return bass.AP(self.F16, 8 * TSm + oc * TSm + o0, [[16 * TSm, 128], [1, w]])

    def stage32(self, i):
        if self.TSm >= 1024:
            return bass.AP(self.t["mixed"], i * 4096, [[8 * self.TSm, 128], [1, 4096]])
        return self.Xa[:, i * 4096:(i + 1) * 4096]

    def convert_weights(self):
        P, d = self.P, self.d
        L = self.cfg.DEPTH
        n = 0
        jobs = []
        for l in range(L):
            c0 = 0
            while c0 < DIN:
                nc_ = min(512, DIN - c0)
                jobs.append((d["w_in"][l, :, c0:c0 + nc_].rearrange("(kc p) n -> p kc n", p=128),
                             self.winb[l, :, c0:c0 + nc_].rearrange("(kc p) n -> p kc n", p=128), 8, nc_))
                c0 += nc_
            for b in range(4):
                jobs.append((d["w_down"][l, b].rearrange("(kc p) n -> p kc n", p=128),
                             self.wdnb[l, b].rearrange("(kc p) n -> p kc n", p=128), 4, 1024))
            for hf in range(2):
                jobs.append((d["w_out"][l, :, hf * 512:(hf + 1) * 512].rearrange("(kc p) n -> p kc n", p=128),
                             self.woutb[l, :, hf * 512:(hf + 1) * 512].rearrange("(kc p) n -> p kc n", p=128), 8, 512))
        for (src, dst, a, b) in jobs:
            s32 = self.stage32(n % 2)
            s32v = fap(s32[:, 0:1], [[b, a], [1, b]])
            wb = self.t["w"][n % 3]
            wbv = fap(wb[:, 0, 0:1], [[b, a], [1, b]])
            P.dma("sp", s32v, src)
            P.copy(wbv, s32v, eng=("dve", "act", "pool")[n % 3])
            P.dma("sp", dst, wbv)
            n += 1

    def setup_consts(self):
        P, t = self.P, self.t
        cst = t["cst"]
        P.dma("sp", cst[:], self.d["consts"])
        P.copy(t["identb"][:], cst[:, C_ID:C_ID + 128])
        P.tt(self.G[:, 0, 0:128], cst[:, C_U:C_U + 128], cst[:, C_ID:C_ID + 128], ALU.add)
        P.ts(t["negL"][:], self.G[:, 0, 0:128], -1.0)
        P.copy(t["cmb"][:], cst[:, C_CM:C_CM + 128])
        P.ts(t["onesm"][:], cst[:, C_ONE:C_ONE + 128], 1.0 / 512.0)
        P.memset(t["negone"][:], -1.0)
        P.memset(t["tail"][:], 0.0)

    def layer_setup(self, l):
        P, t, d = self.P, self.t, self.d
        vec = t["vec"]
        P.dma("sp", vec[:], d["vecs"][l])
        P.dma("sp", t["vrep"][:], d["vrep"][l])
        gp = d["gpost"]
        P.dma("sp", t["gpost"][:], bass.AP(gp.tensor, l * D, [[0, 128], [1, D]]))
        P.dma("sp", t["wab32"][:], d["wab"][l].rearrange("i p m -> p i m"))
        P.copy(t["wab"][:], t["wab32"][:])
        P.dma("sp", t["wdt32"][:],
              d["w_in"][l, :, COL["ssd_dt"]:COL["ssd_dt"] + 8].rearrange("(kc p) n -> p kc n", p=128))
        P.copy(t["wdt"][:], t["wdt32"][:])
        sm = t["sm"]
        lam = vec[:, V_LLAM:V_LLAM + 4]
        P.act(sm[:, 0:4], lam, AF.Abs)
        P.act(sm[:, 4:8], sm[:, 0:4], AF.Exp, scale=-1.0)
        P.act(sm[:, 8:12], sm[:, 4:8], AF.Ln, bias=1.0)
        P.ts(sm[:, 12:16], lam, -1.0, 0.0, op0=ALU.mult, op1=ALU.max)
        P.tt(sm[:, 16:20], sm[:, 8:12], sm[:, 12:16], ALU.add)
        P.ts(t["cl"][:], sm[:, 16:20], -8.0)
        P.act(sm[:, 24:32], t["vrep"][:, 8:16], AF.Exp)
        P.ts(t["aneg"][:], sm[:, 24:32], -1.0)
        idf = t["cst"][:, C_ID:C_ID + 128]
        for i in range(16):
            P.ts(t["dl"][:, i, :], idf, vec[:, V_LCW + i:V_LCW + i + 1], eng="pool" if i % 2 else "dve")
        for i in range(32):
            P.ts(t["ds"][:, i, :], idf, vec[:, V_SCW + i:V_SCW + i + 1], eng="pool" if i % 2 else "dve")

    def load_w(self, l, c0, ncol=512):
        buf = self.wbuf()
        src = self.winb[l, :, c0:c0 + ncol].rearrange("(kc p) n -> p kc n", p=128)
        self.P.dma("sp", buf[:, :, 0:ncol], src)
        return buf

    def proj(self, wb, wc0, o0, w):
        P, t = self.P, self.t
        ps = self.bank()[:, 0:w]
        for k in range(8):
            P.mm(ps, wb[:, k, wc0:wc0 + 128], t["hT"][:, k, o0:o0 + w], start=(k == 0), stop=(k == 7))
        return ps

    def ttiles(self, TS):
        w = min(512, TS)
        return [(o0, w) for o0 in range(0, TS, w)]

    def rsqrt_rep(self, out, ps):
        P = self.P
        P.act(out, ps, AF.Ln, bias=EPS)
        P.act(out, out, AF.Exp, scale=-0.5)

    def run_seq(self, sq, l):
        P, t, d, o = self.P, self.t, self.d, self.o
        TS = sq.TS
        nst = sq.T // TS
        if sq.name == "p":
            for nm in ("lhist", "shist", "chist", "hst", "S", "S16"):
                P.memset(t[nm][:], 0.0, eng="pool")
        else:
            sv = t["svec"]
            P.dma("sp", sv[:], d["svec"][l])
            P.dma("sp", t["S"][:], d["sssd"][l])
            P.copy(t["S16"][:], t["S"][:])
            P.copy(t["hst"][:], sv[:, SV_H0:SV_H0 + 4])
            P.copy(t["lhist"][:], sv[:, SV_LH:SV_LH + 12].rearrange("p (c j) -> p c j", c=4))
            P.copy(t["shist"][:], sv[:, SV_SH:SV_SH + 24].rearrange("p (c j) -> p c j", c=8))
            P.copy(t["chist"][:], sv[:, SV_CH:SV_CH + 120].rearrange("p (c j) -> p c j", c=4))
        for st in range(nst):
            self.supertile(sq, l, st, last=(st == nst - 1))
        sfx = sq.name
        P.dma("sp", o["lruh_" + sfx][l], t["hst"][:])
        P.dma("sp", o["ssd_" + sfx][l], t["S"][:])
        P.dma("sp", o["lruc_" + sfx][l], t["tail"][:, 0:12])
        P.dma("sp", o["ssdc_" + sfx][l], t["tail"][:, 16:40])
        P.dma("sp", o["cfc_" + sfx][l], t["tail"][:, 40:160])

    def merge(self, sq, l, b, TS):
        P, t, d = self.P, self.t, self.d
        wd = self.wbuf()
        wdv = fap(wd[:], [[1024, 4], [1, 1024]])
        P.dma("sp", wdv, self.wdnb[l, b].rearrange("(kc p) n -> p kc n", p=128))
        for half in range(2):
            wm = self.load_w(l, COL["merge"] + b * 1024 + half * 512)
            for ocl in range(4):
                oc = half * 4 + ocl
                for (o0, w) in self.ttiles(TS):
                    py = self.bank()[:, 0:w]
                    for k in range(4):
                        P.mm(py, bass.AP(wd, k * 1024 + oc * 128, [[4096, 128], [1, 128]]),
                             t["uT"][:, k, o0:o0 + w], start=(k == 0), stop=(k == 3))
                    pg = self.proj(wm, ocl * 128, o0, w)
                    g1 = self.gt(w)
                    P.act(g1, pg, AF.Sigmoid)
                    if b == 0:
                        P.tt(t["mixed"][:, oc, o0:o0 + w], py, g1, ALU.mult)
                    else:
                        g2 = self.gt(w)
                        P.tt(g2, py, g1, ALU.mult)
                        dst = self.mixb(oc, o0, w) if b == 3 else t["mixed"][:, oc, o0:o0 + w]
                        P.tt(dst, t["mixed"][:, oc, o0:o0 + w], g2, ALU.add, eng="pool")

    def supertile(self, sq, l, st, last):
        P, t, d, o = self.P, self.t, self.d, self.o
        TS = sq.TS
        nblk = TS // 128
        t0 = st * TS
        tts = self.ttiles(TS)
        vec, hT, sm, sg, uT = t["vec"], t["hT"], t["sm"], t["sg"], t["uT"]
        f, h = self.f, self.h
        nv = min(sq.nvalid - t0, TS)
        xsrc = sq.xin if l == 0 else sq.y
        cst = t["cst"]
        ident = cst[:, C_ID:C_ID + 128]
        sfx = sq.name

        def in_tile(c0, c1, o0, w):
            return o0 <= c0 and c1 <= o0 + w

        xn = fap(self.G[:, 0, :], [[1, 1024]])
        for b in range(nblk):
            xt = t["xin"][b % 2]
            P.dma("sp", xt[:], xsrc[t0 + b * 128:t0 + (b + 1) * 128, :])
            ssq = sm[:, 32:33]
            P.act(self.junk[:], xt[:], AF.Square, accum_out=ssq)
            rs = sm[:, 34:35]
            P.act(rs, ssq, AF.Ln, bias=EPS, scale=1.0 / D)
            P.act(rs, rs, AF.Exp, scale=-0.5)
            P.ts(xn, xt[:], rs)
            for half in range(2):
                ps = self.bank()
                for j in range(4):
                    jj = half * 4 + j
                    P.tr(ps[:, j * 128:(j + 1) * 128], xn[:, jj * 128:(jj + 1) * 128], ident)
                gp = fap(vec[:, V_GPRE + half * 4:V_GPRE + half * 4 + 4], [[1, 4], [0, 128]])
                P.tt(hT[:, half * 4:half * 4 + 4, b * 128:(b + 1) * 128],
                     ps.rearrange("p (j q) -> p j q", j=4), gp, ALU.mult)

        self.stage(1)
        wb = self.load_w(l, COL["lru_g"])
        for c in range(4):
            for (o0, w) in tts:
                ps = self.proj(wb, c * 128, o0, w)
                P.act(sg[:, c, o0:o0 + w], ps, AF.Silu)
        self.stage(2)
        wb = self.load_w(l, COL["lru_x"])
        for c in range(4):
            lxh = h(c % 2)
            P.copy(lxh[:, 0:3], t["lhist"][:, c, :])
            for (o0, w) in tts:
                ps = self.proj(wb, c * 128, o0, w)
                P.copy(lxh[:, 3 + o0:3 + o0 + w], ps, eng="act")
                if last and in_tile(nv - 3, nv, o0, w):
                    P.copy(t["tail"][:, c * 3:c * 3 + 3], ps[:, nv - 3 - o0:nv - o0])
            P.copy(t["lhist"][:, c, :], lxh[:, TS:TS + 3])
            self.stage(2.1)
            xc, xc16 = f(0, TS), h(2, TS)
            for (o0, w) in tts:
                psc = self.bank()[:, 0:w]
                for k in range(4):
                    P.mm(psc, t["dl"][:, k * 4 + c, :], lxh[:, o0 + k:o0 + k + w], start=(k == 0), stop=(k == 3))
                P.act(xc[:, o0:o0 + w], psc, AF.Identity, bias=vec[:, V_LCB + c:V_LCB + c + 1])
                P.copy(xc16[:, o0:o0 + w], xc[:, o0:o0 + w])
            self.stage(2.2)
            r_, i_ = f(1, TS), f(2, TS)
            for (o0, w) in tts:
                pa = self.bank()[:, 0:w]
                P.mm(pa, t["wab"][:, c, :], xc16[:, o0:o0 + w])
                P.act(r_[:, o0:o0 + w], pa, AF.Sigmoid, bias=vec[:, V_LBA + c:V_LBA + c + 1])
                px = self.bank()[:, 0:w]
                P.mm(px, t["wab"][:, 4 + c, :], xc16[:, o0:o0 + w])
                P.act(i_[:, o0:o0 + w], px, AF.Sigmoid, bias=vec[:, V_LBX + c:V_LBX + c + 1])
            self.stage(2.3)
            a_, th, m_ = f(3, TS), f(4, TS), f(5, TS)
            clc = t["cl"][:, c:c + 1]
            P.act(a_, r_, AF.Exp, scale=clc)
            P.act(th, r_, AF.Tanh, scale=clc)
            self.stage(2.4)
            P.tt(m_, a_, a_, ALU.mult)
            P.stt(m_, m_, 1.0, th, ALU.add, ALU.mult)
            self.stage(2.5)
            P.act(m_, m_, AF.Sqrt, scale=-1.0)
            P.tt(i_, i_, xc, ALU.mult)
            P.tt(i_, i_, m_, ALU.mult)
            self.stage(2.6)
            hh = f(1, TS)
            P.scan(hh, a_, i_, t["hst"][:, c:c + 1])
            P.copy(t["hst"][:, c:c + 1], hh[:, nv - 1:nv])
            self.stage(2.7)
            P.tt(uT[:, c, 0:TS], hh, sg[:, c, 0:TS], ALU.mult)
        self.stage(3)
        self.merge(sq, l, 0, TS)

        self.stage(4)
        wb = self.load_w(l, COL["ssd_z"])
        for c in range(4):
            for (o0, w) in tts:
                ps = self.proj(wb, c * 128, o0, w)
                P.act(sg[:, c, o0:o0 + w], ps, AF.Silu)
        bc = t["QT"]
        for grp in range(2):
            wb = self.load_w(l, COL["ssd_x"] + grp * 512)
            for c in range(4):
                c8 = grp * 4 + c
                sxh = h(c8 % 2)
                P.copy(sxh[:, 0:3], t["shist"][:, c8, :])
                for (o0, w) in tts:
                    ps = self.proj(wb, c * 128, o0, w)
                    P.copy(sxh[:, 3 + o0:3 + o0 + w], ps, eng="act")
                    if last and in_tile(nv - 3, nv, o0, w):
                        P.copy(t["tail"][:, 16 + c8 * 3:16 + c8 * 3 + 3], ps[:, nv - 3 - o0:nv - o0])
                P.copy(t["shist"][:, c8, :], sxh[:, TS:TS + 3])
                for (o0, w) in tts:
                    psc = self.bank()[:, 0:w]
                    for k in range(4):
                        P.mm(psc, t["ds"][:, k * 8 + c8, :], sxh[:, o0 + k:o0 + k + w], start=(k == 0), stop=(k == 3))
                    dst = f(c)[:, o0:o0 + w] if grp == 0 else bc[:, c, o0:o0 + w]
                    P.act(dst, psc, AF.Silu, bias=vec[:, V_SCB + c8:V_SCB + c8 + 1])
        self.stage(5)
        dts = t["dts"]
        pd = self.bank()
        for b in range(nblk):
            for k in range(8):
                P.mm(pd[:, b * 8:b * 8 + 8], hT[:, k, b * 128:(b + 1) * 128], t["wdt"][:, k, :],
                     start=(k == 0), stop=(k == 7))
        nb8 = nblk * 8
        dx, dabs, dln, dt_, adt = dts[:, 0, 0:nb8], dts[:, 1, 0:nb8], dts[:, 2, 0:nb8], dts[:, 3, 0:nb8], dts[:, 1, 0:nb8]
        P.tt(dx.rearrange("p (b h) -> p b h", h=8), pd[:, 0:nb8].rearrange("p (b h) -> p b h", h=8),
             fap(t["vrep"][:, 0:8], [[0, nblk], [1, 8]]), ALU.add)
        P.act(dabs, dx, AF.Abs)
        P.act(dln, dabs, AF.Exp, scale=-1.0)
        P.act(dln, dln, AF.Ln, bias=1.0)
        P.ts(dabs, dx, 0.0, op0=ALU.max)
        P.tt(dt_, dabs, dln, ALU.add)
        P.ts(dt_, dt_, cst[:, sq.valcol:sq.valcol + 1])
        P.tt(adt.rearrange("p (b h) -> p b h", h=8), dt_.rearrange("p (b h) -> p b h", h=8),
             fap(t["aneg"][:], [[0, nblk], [1, 8]]), ALU.mult)
        self.stage(6)
        tri = cst[:, C_TRI:C_TRI + 128]
        Umat = cst[:, C_U:C_U + 128]
        ones = cst[:, C_ONE:C_ONE + 128]
        S, S16 = t["S"], t["S16"]
        for b in range(nblk):
            bs = slice(b * 128, (b + 1) * 128)
            dtb = dts[:, 3, b * 8:b * 8 + 8]
            adtb = dts[:, 1, b * 8:b * 8 + 8]
            psx = self.bank()
            for c in range(4):
                P.tr(psx[:, c * 128:(c + 1) * 128], f(c)[:, bs], ident)
            psx3 = psx.rearrange("p (h q) -> p h q", h=8)
            P.tt(t["xdt"][:].rearrange("p (h q) -> p h q", h=8), psx3, fap(dtb, [[1, 8], [0, 64]]), ALU.mult)
            pds = self.bank()
            P.mm(pds[:, 0:8], Umat, adtb)
            P.mm(pds[:, 8:16], ones, adtb)
            P.act(sm[:, 40:56], pds[:, 0:16], AF.Exp)
            P.tt(sm[:, 56:64], sm[:, 40:48], dtb, ALU.mult)
            P.tt(t["xdt2"][:].rearrange("p (h q) -> p h q", h=8), psx3, fap(sm[:, 56:64], [[1, 8], [0, 64]]), ALU.mult)
            pbt = self.bank()
            for cb in range(2):
                P.mm(pbt[:, cb * 128:(cb + 1) * 128], bc[:, cb, bs], t["identb"][:])
            P.copy(t["btm"][:], pbt[:, 0:256], eng="act")
            pcb = self.bank(2)
            for g_ in range(4):
                cc, gl = g_ // 2, g_ % 2
                P.mm(pcb[:, gl * 512 + cc * 128:gl * 512 + (cc + 1) * 128], bc[gl * 64:(gl + 1) * 64, cc, bs],
                     bc[gl * 64:(gl + 1) * 64, 2 + cc, bs])
            rall = self.ssd_scr(0, 1024)
            dec = self.ssd_scr(1024, 1024)
            P.tt(rall.rearrange("p (h q) -> p h q", h=8), fap(tri, [[0, 8], [1, 128]]),
                 fap(adtb, [[1, 8], [0, 128]]), ALU.mult)
            pseg = self.bank(2)
            for hf in range(2):
                P.mm(pseg[:, hf * 512:(hf + 1) * 512], Umat, rall[:, hf * 512:(hf + 1) * 512], start=True, stop=False)
                P.mm(pseg[:, hf * 512:(hf + 1) * 512], ident,
                     cst[:, C_NEG:C_NEG + 512], start=False, stop=True)
            P.act(dec, pseg, AF.Exp)
            for cc in range(2):
                P.tt(t["mt"][:, cc * 512:(cc + 1) * 512].rearrange("p (g r q) -> p g r q", g=2, r=2),
                     dec[:, cc * 512:(cc + 1) * 512].rearrange("p (g r q) -> p g r q", g=2, r=2),
                     fap(pcb[:, cc * 128:cc * 128 + 1], [[512, 2], [0, 2], [1, 128]]), ALU.mult)
            ea = self.ssd_scr(2048, 1024)
            for gl in range(2):
                pe_ = self.bank()
                rsel = bass.AP(rall.tensor, rall.offset + 2 * gl * 128, [list(rall.ap[0]), [512, 2], [128, 2], [1, 128]])
                P.mm(pe_, ones, rsel)
                hs = slice(gl * 64, (gl + 1) * 64)
                eav = ea[hs, gl * 512:(gl + 1) * 512]
                P.act(eav, pe_[hs, :], AF.Exp)
                P.tt(t["cdec"][hs, :].rearrange("p (c r q) -> p c r q", c=2, r=2),
                     eav.rearrange("p (c r q) -> p c r q", c=2, r=2),
                     fap(bc[hs, 2, bs], [[self.TSm, 2], [0, 2], [1, 128]]), ALU.mult)
            py = self.bank(2)
            for c in range(4):
                cc, gl = c // 2, c % 2
                for r in range(2):
                    hh_ = 2 * c + r
                    outp = py[r * 64:(r + 1) * 64, gl * 512 + cc * 128:gl * 512 + (cc + 1) * 128]
                    P.mm(outp, t["xdt"][:, hh_ * 64:(hh_ + 1) * 64], t["mt"][:, hh_ * 128:(hh_ + 1) * 128],
                         start=True, stop=False)
                    P.mm(outp, S16[gl * 64:(gl + 1) * 64, (cc * 2 + r) * 64:(cc * 2 + r + 1) * 64],
                         t["cdec"][gl * 64:(gl + 1) * 64, (cc * 2 + r) * 128:(cc * 2 + r + 1) * 128],
                         start=False, stop=True)
            for c in range(4):
                cc, gl = c // 2, c % 2
                P.stt(f(c)[:, bs], f(c)[:, bs], vec[:, V_SD + c:V_SD + c + 1],
                      py[:, gl * 512 + cc * 128:gl * 512 + (cc + 1) * 128], ALU.mult, ALU.add)
            pst = self.bank()
            for g_ in range(4):
                cc, gl = g_ // 2, g_ % 2
                P.mm(pst[gl * 64:(gl + 1) * 64, cc * 128:(cc + 1) * 128], t["btm"][:, g_ * 64:(g_ + 1) * 64],
                     t["xdt2"][:, g_ * 128:(g_ + 1) * 128])
            for gl in range(2):
                hs = slice(gl * 64, (gl + 1) * 64)
                sv4 = S[hs, :].rearrange("p (c r q) -> p c r q", c=2, r=2)
                P.tt(sv4, sv4, fap(sm[hs, 48 + 2 * gl:48 + 2 * gl + 1], [[4, 2], [1, 2], [0, 64]]), ALU.mult)
            P.tt(S[:], S[:], pst[:, 0:256], ALU.add)
            P.copy(S16[:], S[:])
        self.stage(7)
        for c in range(4):
            P.tt(f(c, TS), f(c, TS), sg[:, c, 0:TS], ALU.mult)
            P.act(h(c, TS), f(c, TS), AF.Square)
        for (o0, w) in tts:
            pss = self.bank()[:, 0:w]
            for c in range(4):
                P.mm(pss, t["onesm"][:], h(c)[:, o0:o0 + w], start=(c == 0), stop=(c == 3))
            rstd = self.gt(w)
            self.rsqrt_rep(rstd, pss)
            for c in range(4):
                P.stt(uT[:, c, o0:o0 + w], f(c)[:, o0:o0 + w], vec[:, V_SNORM + c:V_SNORM + c + 1], rstd,
                      ALU.mult, ALU.mult)
        self.merge(sq, l, 1, TS)

        self.stage(8)
        wb = self.load_w(l, COL["cf_g"])
        for c in range(4):
            for (o0, w) in tts:
                ps = self.proj(wb, c * 128, o0, w)
                P.act(sg[:, c, o0:o0 + w], ps, AF.Silu)
        wbb = self.load_w(l, COL["cf_b"])
        wba = self.load_w(l, COL["cf_a"])
        idf = ident
        for c in range(4):
            for k in range(31):
                P.ts(t["dc"][:, k, :], idf, vec[:, V_CCW + k * 4 + c:V_CCW + k * 4 + c + 1],
                     eng="pool" if k % 2 else "dve")
            gh = h(c % 2)
            P.copy(gh[:, 0:30], t["chist"][:, c, :])
            for (o0, w) in tts:
                pb = self.proj(wbb, c * 128, o0, w)
                sig = self.gt(w)
                P.act(sig, pb, AF.Sigmoid)
                pa = self.proj(wba, c * 128, o0, w)
                P.tt(gh[:, 30 + o0:30 + o0 + w], pa, sig, ALU.mult)
                if last and in_tile(nv - 30, nv, o0, w):
                    P.tt(t["tail"][:, 40 + c * 30:40 + c * 30 + 30], pa[:, nv - 30 - o0:nv - o0],
                         sig[:, nv - 30 - o0:nv - o0], ALU.mult)
            P.copy(t["chist"][:, c, :], gh[:, TS:TS + 30])
            for (o0, w) in tts:
                psc = self.bank()[:, 0:w]
                for k in range(31):
                    P.mm(psc, t["dc"][:, k, :], gh[:, o0 + k:o0 + k + w], start=(k == 0), stop=(k == 30))
                P.act(f(c)[:, o0:o0 + w], psc, AF.Identity, bias=vec[:, V_CCB + c:V_CCB + c + 1])
        for c in range(4):
            P.copy(h(c, TS), f(c, TS), eng="pool" if c % 2 else "dve")
        for (o0, w) in tts:
            pm = self.bank()[:, 0:w]
            for c in range(4):
                P.mm(pm, t["onesm"][:], h(c)[:, o0:o0 + w], start=(c == 0), stop=(c == 3))
            for c in range(4):
                P.tt(f(c)[:, o0:o0 + w], f(c)[:, o0:o0 + w], pm, ALU.subtract)
        for c in range(4):
            P.act(h(c, TS), f(c, TS), AF.Square)
        for (o0, w) in tts:
            pv_ = self.bank()[:, 0:w]
            for c in range(4):
                P.mm(pv_, t["onesm"][:], h(c)[:, o0:o0 + w], start=(c == 0), stop=(c == 3))
            rstd = self.gt(w)
            self.rsqrt_rep(rstd, pv_)
            for c in range(4):
                P.tt(f(c)[:, o0:o0 + w], f(c)[:, o0:o0 + w], rstd, ALU.mult)
        for c in range(4):
            P.act(f(4 + c % 2, TS), f(c, TS), AF.Silu, scale=vec[:, V_CLG + c:V_CLG + c + 1],
                  bias=vec[:, V_CLB + c:V_CLB + c + 1])
            P.tt(uT[:, c, 0:TS], f(4 + c % 2, TS), sg[:, c, 0:TS], ALU.mult)
        self.merge(sq, l, 2, TS)

        self.stage(9)
        QT = t["QT"]
        wb = self.load_w(l, COL["q"])
        for c in range(4):
            for (o0, w) in tts:
                ps = self.proj(wb, c * 128, o0, w)
                P.act(QT[:, c, o0:o0 + w], ps, AF.Copy, scale=0.125)
        wb = self.load_w(l, COL["k"])
        for c in range(4):
            kt = h(c, TS)
            for (o0, w) in tts:
                ps = self.proj(wb, c * 128, o0, w)
                P.copy(kt[:, o0:o0 + w], ps, eng="act" if c % 2 else "dve")
            P.dma("sp", self.kscr[c, :, t0:t0 + TS], kt)
        for b in range(nblk):
            pk = self.bank()
            for k in range(8):
                P.mm(pk, hT[:, k, b * 128:(b + 1) * 128], wb[:, k, :], start=(k == 0), stop=(k == 7))
            kv = self.gt(512)
            P.copy(kv, pk, eng="act")
            P.dma("sp", o["k_" + sfx][l, t0 + b * 128:t0 + (b + 1) * 128, :], kv)
        wb = self.load_w(l, COL["v"])
        for b in range(nblk):
            pv_ = self.bank()
            for k in range(8):
                P.mm(pv_, hT[:, k, b * 128:(b + 1) * 128], wb[:, k, :], start=(k == 0), stop=(k == 7))
            kv = self.gt(512)
            P.copy(kv, pv_, eng="act")
            P.dma("sp", o["v_" + sfx][l, t0 + b * 128:t0 + (b + 1) * 128, :], kv)
            P.copy(t["v16"][b % 2][:], pv_)
            P.dma("sp", self.vscr[t0 + b * 128:t0 + (b + 1) * 128, :], t["v16"][b % 2][:])
        wb = self.load_w(l, COL["sb_g"])
        for c in range(4):
            for (o0, w) in tts:
                ps = self.proj(wb, c * 128, o0, w)
                P.act(sg[:, c, o0:o0 + w], ps, AF.Silu)

        self.stage(10)
        QW = sq.QW
        nsub = QW // 128
        for qt in range(TS // QW):
            self.attention(sq, l, t0, qt)
        self.stage(11)
        self.merge(sq, l, 3, TS)

        self.stage(12)
        wo = []
        for half in range(2):
            buf = self.wbuf()
            P.dma("sp", buf[:], self.woutb[l, :, half * 512:(half + 1) * 512].rearrange("(kc p) n -> p kc n", p=128))
            wo.append(buf)
        ot = fap(self.G[:, 0, :], [[1, 1024]])
        for b in range(nblk):
            xt = t["xin"][b % 2]
            P.dma("sp", xt[:], xsrc[t0 + b * 128:t0 + (b + 1) * 128, :])
            for half in range(2):
                po = self.bank()
                for k in range(8):
                    P.mm(po, self.mixb(k, b * 128, 128), wo[half][:, k, :], start=(k == 0), stop=(k == 7))
                P.copy(ot[:, half * 512:(half + 1) * 512], po, eng="act")
            ssq = sm[:, 36:37]
            P.act(self.junk[:], ot, AF.Square, accum_out=ssq)
            rs = sm[:, 38:39]
            P.act(rs, ssq, AF.Ln, bias=EPS, scale=1.0 / D)
            P.act(rs, rs, AF.Exp, scale=-0.5)
            P.stt(ot, ot, rs, t["gpost"][:], ALU.mult, ALU.mult)
            P.tt(ot, ot, xt[:], ALU.add)
            P.dma("sp", sq.y[t0 + b * 128:t0 + (b + 1) * 128, :], ot)

    def attention(self, sq, l, t0, qt):
        P, t, d = self.P, self.t, self.d
        QW = sq.QW
        nsub = QW // 128
        QT, sg, uT = t["QT"], t["sg"], t["uT"]
        q0 = qt * QW
        gq0 = t0 + q0
        acc = t["acc"]
        eaccs = [t["eacc"], t["eacc2"]]
        oacc = self.att_scr(2048, nsub * 512)
        tmpo = self.G[:, 3, :]
        P.memset(acc[:, 0:nsub * 8], 0.0)
        P.memset(eaccs[0][:, 0:nsub * 8], 1.0)
        P.memset(oacc, 0.0, eng="pool")
        spans = []
        if sq.past == 0:
            assert QW == 512
            nsp = gq0 // 512 + 1
            for sp_ in range(nsp - 1, -1, -1):
                spans.append(("cur", sp_ * 512, 512, sp_ == nsp - 1))
        else:
            spans.append(("cur", 0, QW, True))
            for sp_ in range(sq.past // 512 - 1, -1, -1):
                spans.append(("past", sp_ * 512, 512, False))

        kvbufs = {}

        def load_span(si):
            if si >= len(spans) or si in kvbufs:
                return
            kind, k0, klen, diag = spans[si]
            kvb = self.wbuf()
            nkb = klen // 128
            if kind == "cur":
                P.dma("sp", kvb[:, 0:4, 0:klen], self.kscr[:, :, k0:k0 + klen].rearrange("c p n -> p c n"))
                P.dma("sp", kvb[:, 4:4 + nkb, :], self.vscr[k0:k0 + klen, :].rearrange("(j p) n -> p j n", p=128))
            else:
                stg = self.att_scr(4096, 4096)
                sk = fap(stg[:, 0:1], [[512, 4], [1, klen]])
                svv = fap(stg[:, 2048:2049], [[512, nkb], [1, 512]])
                P.dma("sp", sk, d["pk"][l, :, :, k0:k0 + klen].rearrange("c p n -> p c n"))
                P.dma("sp", svv, d["pv"][l, k0:k0 + klen, :].rearrange("(j p) n -> p j n", p=128))
                P.copy(kvb[:, 0:4, 0:klen], sk)
                P.copy(kvb[:, 4:4 + nkb, :], svv, eng="act")
            kvbufs[si] = kvb

        items = []
        blk = 0
        for si, (kind, k0, klen, diag) in enumerate(spans):
            nkb = klen // 128
            for j in range(nkb - 1, -1, -1):
                for hp in range(4):
                    items.append(dict(si=si, j=j, hp=hp, diag=diag, c0=(j * 128 if diag else 0), blk=blk,
                                      first=(hp == 0), last=(hp == 3), sfirst=(j == nkb - 1 and hp == 0)))
                blk += 1
        n = len(items)
        paccs = {}

        def bufs(i):
            b = i % 2
            pz = self.psum[:, b * 1024:(b + 1) * 1024]
            et = self.att_scr(b * 1024, 1024)
            spt = self.att_h(b)
            wt = self.att_h(2 + b)
            po = self.psum[:, (4 + b) * 512:(5 + b) * 512]
            return pz, et, spt, wt, po

        def views(it, pz, et, spt, wt):
            c0 = it["c0"]
            wq = QW - c0
            return (fap(pz[:, c0:c0 + 1], [[512, 2], [1, wq]]), fap(et[:, c0:c0 + 1], [[QW, 2], [1, wq]]),
                    fap(spt[:, c0:c0 + 1], [[QW, 2], [1, wq]]), fap(wt[:, c0:c0 + 1], [[QW, 2], [1, wq]]))

        def stA(i):
            it = items[i]
            if it["sfirst"]:
                load_span(it["si"])
                load_span(it["si"] + 1)
            kvb = kvbufs[it["si"]]
            pz, et, spt, wt, po = bufs(i)
            c0, j, hp = it["c0"], it["j"], it["hp"]
            for r in range(2):
                rs_ = slice(r * 64, (r + 1) * 64)
                P.mm(pz[:, r * 512 + c0:r * 512 + QW], kvb[rs_, hp, j * 128:(j + 1) * 128],
                     QT[rs_, hp, q0 + c0:q0 + QW])

        def stB1(i):
            it = items[i]
            pz, et, spt, wt, po = bufs(i)
            pzv, etv, spv, wtv = views(it, pz, et, spt, wt)
            P.act(etv, pzv, AF.Exp)

        def stB(i):
            it = items[i]
            pz, et, spt, wt, po = bufs(i)
            pzv, etv, spv, wtv = views(it, pz, et, spt, wt)
            c0 = it["c0"]
            P.act(spv, etv, AF.Ln, bias=1.0)
            if it["diag"]:
                dv = fap(spt[:, c0:c0 + 1], [[QW, 2], [1, 128]])
                P.tt(dv, dv, fap(t["cmb"][:, 0:1], [[0, 2], [1, 128]]), ALU.mult)

        def stC(i):
            it = items[i]
            pz, et, spt, wt, po = bufs(i)
            c0, hp = it["c0"], it["hp"]
            sub0 = c0 // 128
            for r in range(2):
                P.mm(pz[:, r * 512 + c0:r * 512 + QW], t["negL"][:], spt[:, r * QW + c0:(r + 1) * QW],
                     start=False, stop=True, skip_group_check=True)
            pacc = self.psum[:, (6 + it["blk"] % 2) * 512:(7 + it["blk"] % 2) * 512]
            for r in range(2):
                hd = 2 * hp + r
                for sub in range(sub0, nsub):
                    qs = slice(r * QW + sub * 128, r * QW + (sub + 1) * 128)
                    P.mm(pacc[:, sub * 8 + hd:sub * 8 + hd + 1], spt[:, qs], t["negone"][:, 0:1])

        def stD(i):
            it = items[i]
            pz, et, spt, wt, po = bufs(i)
            pzv, etv, spv, wtv = views(it, pz, et, spt, wt)
            c0 = it["c0"]
            P.act(wtv, pzv, AF.Exp)
            if it["diag"]:
                dv = fap(wt[:, c0:c0 + 1], [[QW, 2], [1, 128]])
                P.tt(dv, dv, fap(t["cmb"][:, 0:1], [[0, 2], [1, 128]]), ALU.mult)
            if it["last"]:
                sub0 = c0 // 128
                pacc = self.psum[:, (6 + it["blk"] % 2) * 512:(7 + it["blk"] % 2) * 512]
                a_ = acc[:, sub0 * 8:nsub * 8]
                P.tt(a_, a_, pacc[:, sub0 * 8:nsub * 8], ALU.add)
                P.act(eaccs[(it["blk"] + 1) % 2][:, 0:nsub * 8], acc[:, 0:nsub * 8], AF.Exp)

        def stE(i):
            it = items[i]
            kvb = kvbufs[it["si"]]
            pz, et, spt, wt, po = bufs(i)
            c0, j, hp = it["c0"], it["j"], it["hp"]
            sub0 = c0 // 128
            for r in range(2):
                hd = 2 * hp + r
                for sub in range(sub0, nsub):
                    qs = slice(r * QW + sub * 128, r * QW + (sub + 1) * 128)
                    P.mm(po[:, (sub * 2 + r) * 64:(sub * 2 + r + 1) * 64], wt[:, qs],
                         kvb[:, 4 + j, hd * 64:(hd + 1) * 64])

        def stF(i):
            it = items[i]
            pz, et, spt, wt, po = bufs(i)
            c0, hp = it["c0"], it["hp"]
            sub0 = c0 // 128
            ns = nsub - sub0
            eacc = eaccs[it["blk"] % 2]
            pov = fap(po[:, sub0 * 128:sub0 * 128 + 1], [[128, ns], [64, 2], [1, 64]])
            eav = fap(eacc[:, sub0 * 8 + 2 * hp:sub0 * 8 + 2 * hp + 1], [[8, ns], [1, 2], [0, 64]])
            tv = fap(tmpo[:, 0:1], [[128, ns], [64, 2], [1, 64]])
            P.tt(tv, pov, eav, ALU.mult)
            ov = fap(oacc[:, sub0 * 512 + 2 * hp * 64:sub0 * 512 + 2 * hp * 64 + 1], [[512, ns], [64, 2], [1, 64]])
            P.tt(ov, ov, tv, ALU.add, eng="pool")

        for s_ in range(n + 2):
            if s_ < n:
                stA(s_)
                stB1(s_)
            if 0 <= s_ - 1 < n:
                stC(s_ - 1)
                stD(s_ - 1)
            if s_ < n:
                stB(s_)
            if 0 <= s_ - 2 < n:
                stE(s_ - 2)
                stF(s_ - 2)
        ident = t["cst"][:, C_ID:C_ID + 128]
        for sub in range(nsub):
            pt = self.bank(1, 4, 6)
            for c in range(4):
                P.tr(pt[:, c * 128:(c + 1) * 128], oacc[:, sub * 512 + c * 128:sub * 512 + (c + 1) * 128], ident)
            cs = slice(q0 + sub * 128, q0 + (sub + 1) * 128)
            P.tt(uT[:, :, cs], pt.rearrange("p (c q) -> p c q", c=4), sg[:, :, cs], ALU.mult)


def make_consts():
    c = np.zeros((128, NC), np.float32)
    j = np.arange(128)[:, None]
    s = np.arange(128)[None, :]
    c[:, C_ID:C_ID + 128] = (j == s)
    c[:, C_U:C_U + 128] = (j > s)
    c[:, C_TRI:C_TRI + 128] = (j <= s)
    for rep in range(4):
        c[:, C_NEG + rep * 128:C_NEG + (rep + 1) * 128] = np.where(s < j, -30000.0, 0.0)
    c[:, C_CM:C_CM + 128] = (j < s)
    c[:, C_ONE:C_ONE + 128] = 1.0
    c[:, C_VALP] = 1.0
    return c


def fm(v, nch):
    return np.ascontiguousarray(np.asarray(v, np.float32).reshape(nch, 128).T)


def host_prep(inp, cfg):
    L = cfg.DEPTH
    f32 = np.float32
    vecs = np.zeros((L, 128, NV), f32)
    vrep = np.zeros((L, 128, 16), f32)
    wab = np.zeros((L, 8, 128, 128), f32)
    for l in range(L):
        v = vecs[l]
        v[:, V_GPRE:V_GPRE + 8] = fm(inp["norm_pre"][l], 8)
        for k in range(4):
            v[:, V_LCW + k * 4:V_LCW + k * 4 + 4] = fm(inp["lru_conv_w"][l, k], 4)
            v[:, V_SCW + k * 8:V_SCW + k * 8 + 8] = fm(inp["ssd_conv_w"][l, k], 8)
        v[:, V_LCB:V_LCB + 4] = fm(inp["lru_conv_b"][l], 4)
        v[:, V_LBA:V_LBA + 4] = fm(inp["lru_ba"][l], 4)
        v[:, V_LBX:V_LBX + 4] = fm(inp["lru_bx"][l], 4)
        v[:, V_LLAM:V_LLAM + 4] = fm(inp["lru_lambda"][l], 4)
        v[:, V_SCB:V_SCB + 8] = fm(inp["ssd_conv_b"][l], 8)
        v[:, V_SD:V_SD + 4] = fm(np.repeat(np.asarray(inp["ssd_d"][l]), 64), 4)
        v[:, V_SNORM:V_SNORM + 4] = fm(inp["ssd_norm"][l], 4)
        for k in range(31):
            v[:, V_CCW + k * 4:V_CCW + k * 4 + 4] = fm(inp["cf_conv_w"][l, k], 4)
        v[:, V_CCB:V_CCB + 4] = fm(inp["cf_conv_b"][l], 4)
        v[:, V_CLG:V_CLG + 4] = fm(inp["cf_ln_g"][l], 4)
        v[:, V_CLB:V_CLB + 4] = fm(inp["cf_ln_b"][l], 4)
        vrep[l, :, 0:8] = np.asarray(inp["ssd_dt_bias"][l])[None, :]
        vrep[l, :, 8:16] = np.asarray(inp["ssd_a_log"][l])[None, :]
        for which, nm in enumerate(("lru_wa", "lru_wx")):
            w = np.asarray(inp[nm][l])
            for c in range(4):
                wab[l, which * 4 + c, 0:64, 0:64] = w[2 * c]
                wab[l, which * 4 + c, 64:128, 64:128] = w[2 * c + 1]
    consts = make_consts()
    consts[:cfg.SVALID, C_VALS] = 1.0
    shared = dict(w_in=np.ascontiguousarray(inp["w_in"], f32), w_down=np.ascontiguousarray(inp["w_down"], f32),
                  w_out=np.ascontiguousarray(inp["w_out"], f32), wab=wab, vecs=vecs, vrep=vrep,
                  gpost=np.ascontiguousarray(inp["norm_post"], f32), consts=consts)
    return shared


def sample_inputs(inp, cfg, s):
    L = cfg.DEPTH
    f32 = np.float32
    SV = cfg.SVALID
    xs = np.zeros((128, D), f32)
    xs[:SV] = inp["x_sample"][s]
    svec = np.zeros((L, 128, NSV), f32)
    sssd = np.zeros((L, 128, 256), f32)
    pk = np.zeros((L, 4, 128, cfg.PAST), f32)
    for l in range(L):
        svec[l, :, SV_H0:SV_H0 + 4] = fm(inp["state_lru_h"][l, s], 4)
        lc = np.asarray(inp["state_lru_conv"][l, s])
        svec[l, :, SV_LH:SV_LH + 12] = lc.T.reshape(4, 128, 3).transpose(1, 0, 2).reshape(128, 12)
        sc = np.asarray(inp["state_ssd_conv"][l, s])
        svec[l, :, SV_SH:SV_SH + 24] = sc.T.reshape(8, 128, 3).transpose(1, 0, 2).reshape(128, 24)
        cc = np.asarray(inp["state_cf_conv"][l, s])
        svec[l, :, SV_CH:SV_CH + 120] = cc.T.reshape(4, 128, 30).transpose(1, 0, 2).reshape(128, 120)
        st = np.asarray(inp["state_ssd"][l, s])
        st = st.reshape(2, 2, 2, 64, 64)
        sssd[l] = st.transpose(1, 4, 0, 2, 3).reshape(128, 256)
        k = np.asarray(inp["cache_sb_k"][l, s]).reshape(cfg.PAST, 512)
        pk[l] = k.T.reshape(4, 128, cfg.PAST)
    pv = np.ascontiguousarray(np.asarray(inp["cache_sb_v"][:, s]).reshape(L, cfg.PAST, 512), f32)
    return dict(xs=xs, svec=svec, sssd=sssd, pk=pk, pv=pv)


def unpack_states(res, sfx, L):
    lruh = res["lruh_" + sfx].transpose(0, 2, 1).reshape(L, 512)
    lruc = res["lruc_" + sfx].reshape(L, 128, 4, 3).transpose(0, 3, 2, 1).reshape(L, 3, 512)
    ssdc = res["ssdc_" + sfx][:, :, 0:24].reshape(L, 128, 8, 3).transpose(0, 3, 2, 1).reshape(L, 3, 1024)
    cfc = res["cfc_" + sfx].reshape(L, 128, 4, 30).transpose(0, 3, 2, 1).reshape(L, 30, 512)
    ssd = res["ssd_" + sfx].reshape(L, 2, 64, 2, 2, 64)
    ssd = ssd.transpose(0, 3, 1, 4, 5, 2).reshape(L, 8, 64, 64)
    return lruh, lruc, ssd, ssdc, cfc


_NC_CACHE = {}


def run_cfg(inp, cfg, n_cores=8):
    key = (cfg.TP, cfg.TS, cfg.DEPTH, cfg.PAST, cfg.SVALID)
    if key not in _NC_CACHE:
        _NC_CACHE[key] = Builder(cfg).build()
    nc = _NC_CACHE[key]
    shared = host_prep(inp, cfg)
    L = cfg.DEPTH
    nb = inp["x_prompt"].shape[0] if cfg.TP > 0 else 0
    nsmp = inp["x_sample"].shape[0]
    in_maps = []
    for c in range(n_cores):
        m = dict(shared)
        if cfg.TP > 0 and c % 2 == 0:
            m["xp"] = np.ascontiguousarray(inp["x_prompt"][(c // 2) % nb], np.float32)
        elif cfg.TP > 0:
            m["xp"] = np.zeros((cfg.TP, D), np.float32)
        else:
            m["xp"] = np.zeros((128, D), np.float32)
        m.update(sample_inputs(inp, cfg, c % nsmp))
        in_maps.append(m)
    res = run_bass_kernel_spmd(nc, in_maps, core_ids=list(range(n_cores))).results
    SV = cfg.SVALID
    outs = {}
    if cfg.TP > 0:
        pcs = [2 * b for b in range(nb)]
        outs["y_p"] = np.stack([res[c]["yp"] for c in pcs])
        st = [unpack_states(res[c], "p", L) for c in pcs]
        for i, nm in enumerate(("lru_h_p", "lru_conv_p", "ssd_p", "ssd_conv_p", "cf_conv_p")):
            outs[nm] = np.stack([s[i] for s in st], axis=1)
        outs["k_p"] = np.stack([res[c]["k_p"].reshape(L, cfg.TP, 8, 64) for c in pcs], axis=1)
        outs["v_p"] = np.stack([res[c]["v_p"].reshape(L, cfg.TP, 8, 64) for c in pcs], axis=1)
    scs = list(range(min(nsmp, n_cores)))
    outs["y_s"] = np.stack([res[c]["ys"][:SV] for c in scs])
    st = [unpack_states(res[c], "s", L) for c in scs]
    for i, nm in enumerate(("lru_h_s", "lru_conv_s", "ssd_s", "ssd_conv_s", "cf_conv_s")):
        outs[nm] = np.stack([s[i] for s in st], axis=1)
    outs["k_s"] = np.stack([res[c]["k_s"][:, :SV].reshape(L, SV, 8, 64) for c in scs], axis=1)
    outs["v_s"] = np.stack([res[c]["v_s"][:, :SV].reshape(L, SV, 8, 64) for c in scs], axis=1)
    return outs


ORDER = ("y_p", "y_s", "lru_h_p", "lru_h_s", "lru_conv_p", "lru_conv_s", "ssd_p", "ssd_s",
         "ssd_conv_p", "ssd_conv_s", "cf_conv_p", "cf_conv_s", "k_p", "k_s", "v_p", "v_s")


def kernel(**inputs):
    inp = {k: np.asarray(v) for k, v in inputs.items()}
    cfg = Cfg(TP=inp["x_prompt"].shape[1], TS=1024, DEPTH=inp["w_in"].shape[0],
              PAST=inp["cache_sb_k"].shape[2], SVALID=inp["x_sample"].shape[1])
    outs = run_cfg(inp, cfg, 8)
    return tuple(np.ascontiguousarray(outs[k], dtype=np.float32) for k in ORDER)
```

```python
import contextlib
import numpy as np
import ml_dtypes
import concourse.bass as bass
import concourse.mybir as mybir
from concourse.bass_utils import run_bass_kernel_spmd

F32 = mybir.dt.float32
BF16 = mybir.dt.bfloat16
AF = mybir.ActivationFunctionType
ALU = mybir.AluOpType

CENG = ("pe", "act", "dve", "pool")
NSLOT = 24

D = 1024
DIN = 10248
EPS = 1e-6
COL = dict(lru_x=0, lru_g=512, ssd_z=1024, ssd_x=1536, ssd_bc=2048, ssd_dt=2560, cf_a=2568, cf_b=3080,
           cf_g=3592, q=4104, k=4616, v=5128, sb_g=5640, merge=6152)
V_GPRE, V_LCW, V_LCB, V_LBA, V_LBX, V_LLAM = 0, 8, 24, 28, 32, 36
V_SCW, V_SCB, V_SD, V_SNORM = 40, 72, 80, 84
V_CCW, V_CCB, V_CLG, V_CLB = 88, 212, 216, 220
NV = 224
C_ID, C_U, C_TRI, C_NEG, C_CM, C_ONE, C_VALP, C_VALS = 0, 128, 256, 384, 896, 1024, 1152, 1153
NC = 1154
SV_H0, SV_LH, SV_SH, SV_CH = 0, 4, 16, 40
NSV = 160


def _rect(ap):
    t = ap.tensor
    dims = list(ap.ap)
    if str(ap.space) == "DRAM":
        lo = ap.offset
        hi = lo
        for st, n in dims:
            if st >= 0:
                hi += st * (n - 1)
            else:
                lo += st * (n - 1)
        return ("D:" + t.name, 0, 1, lo, hi + 1)
    pst, pn = dims[0]
    if pst == 0:
        p0 = 0
        base = ap.offset
    else:
        p0 = ap.offset // pst
        base = ap.offset - p0 * pst
    lo = base
    hi = base
    for st, n in dims[1:]:
        if st >= 0:
            hi += st * (n - 1)
        else:
            lo += st * (n - 1)
    esz = mybir.dt.size(ap.dtype)
    if str(ap.space) == "PSUM":
        b0 = (lo * esz) // 2048 * 2048
        b1 = ((hi + 1) * esz + 2047) // 2048 * 2048
        q0 = p0 // 32 * 32
        q1 = (p0 + pn + 31) // 32 * 32
        return ("P:" + t.name, q0, q1, b0, b1)
    return ("S:" + t.name, p0, p0 + pn, lo * esz, (hi + 1) * esz)


class _Rec:
    __slots__ = ("p0", "p1", "f0", "f1", "w", "r")

    def __init__(self, p0, p1, f0, f1):
        self.p0, self.p1, self.f0, self.f1 = p0, p1, f0, f1
        self.w = {}
        self.r = {}


def _merge(dst, src):
    for k, v in src.items():
        if dst.get(k, -1) < v:
            dst[k] = v


class Ins:
    __slots__ = ("eng", "fn", "deps", "kind", "idx", "slot", "seq", "waits", "sig")


class Prog:
    def __init__(self, nc):
        self.nc = nc
        self.ins = []
        self.track = {}
        self.cnt = {e: 0 for e in CENG}
        self.ndma = 0
        self.slot_seq = [0] * NSLOT
        import os
        self.maxins = int(os.environ.get("KMAXINS", "100000000"))

    def _access(self, ident, reads, writes):
        deps = {}
        rr = [_rect(a) for a in reads]
        ww = [_rect(a) for a in writes]
        for key, p0, p1, f0, f1 in rr:
            for rec in self.track.get(key, ()):
                if rec.p0 < p1 and p0 < rec.p1 and rec.f0 < f1 and f0 < rec.f1:
                    _merge(deps, rec.w)
                    if key[0] == "P":
                        for k2, v2 in rec.r.items():
                            if k2 != ident[0] and deps.get(k2, -1) < v2:
                                deps[k2] = v2
        for key, p0, p1, f0, f1 in ww:
            for rec in self.track.get(key, ()):
                if rec.p0 < p1 and p0 < rec.p1 and rec.f0 < f1 and f0 < rec.f1:
                    _merge(deps, rec.w)
                    _merge(deps, rec.r)
        k, v = ident
        for key, p0, p1, f0, f1 in rr:
            lst = self.track.setdefault(key, [])
            hit = False
            for rec in lst:
                if rec.p0 < p1 and p0 < rec.p1 and rec.f0 < f1 and f0 < rec.f1:
                    if rec.r.get(k, -1) < v:
                        rec.r[k] = v
                    if rec.p0 <= p0 and p1 <= rec.p1 and rec.f0 <= f0 and f1 <= rec.f1:
                        hit = True
            if not hit:
                rec = _Rec(p0, p1, f0, f1)
                rec.r[k] = v
                lst.append(rec)
                if len(lst) > 64:
                    self._collapse(key)
        for key, p0, p1, f0, f1 in ww:
            lst = self.track.setdefault(key, [])
            keep = [rec for rec in lst
                    if not (p0 <= rec.p0 and rec.p1 <= p1 and f0 <= rec.f0 and rec.f1 <= f1)]
            rec = _Rec(p0, p1, f0, f1)
            rec.w[k] = v
            keep.append(rec)
            self.track[key] = keep
            if len(keep) > 64:
                self._collapse(key)
        return deps

    def _collapse(self, key):
        keep = self.track[key]
        big = _Rec(min(r.p0 for r in keep), max(r.p1 for r in keep),
                   min(r.f0 for r in keep), max(r.f1 for r in keep))
        for r in keep:
            _merge(big.w, r.w)
            _merge(big.r, r.r)
        self.track[key] = [big]

    def op(self, eng, fn, reads, writes):
        if len(self.ins) >= self.maxins:
            return None
        i = Ins()
        i.eng, i.fn, i.kind = eng, fn, "op"
        i.idx = self.cnt[eng]
        self.cnt[eng] += 1
        i.deps = self._access((eng, i.idx), reads, writes)
        if eng == "pe":
            i.deps.pop("pe", None)
        self.ins.append(i)
        return i

    def dma(self, queue, out, in_, **kw):
        if len(self.ins) >= self.maxins:
            return None
        i = Ins()
        i.eng, i.kind = queue, "dma"
        i.slot = self.ndma % NSLOT
        self.ndma += 1
        i.seq = self.slot_seq[i.slot]
        self.slot_seq[i.slot] += 1
        i.fn = lambda e: e.dma_start(out=out, in_=in_, **kw)
        i.deps = self._access((("d", i.slot), i.seq), [in_], [out])
        if i.seq > 0:
            k = ("d", i.slot)
            if i.deps.get(k, -1) < i.seq - 1:
                i.deps[k] = i.seq - 1
        self.ins.append(i)
        return i

    def emit(self):
        nc = self.nc
        known = {e: {} for e in list(CENG) + ["sp"]}
        clock = {}
        need_sig = set()
        for i in self.ins:
            kn = known[i.eng]
            waits = []
            for k, v in i.deps.items():
                if kn.get(k, -1) >= v:
                    continue
                waits.append((k, v))
                _merge(kn, clock[(k, v)])
                if kn.get(k, -1) < v:
                    kn[k] = v
                need_sig.add((k, v))
            i.waits = waits
            if i.kind == "op":
                clock[(i.eng, i.idx)] = dict(kn)
            else:
                clock[(("d", i.slot), i.seq)] = dict(kn)
        sigcount = {e: 0 for e in CENG}
        semval = {}
        for i in self.ins:
            if i.kind == "op":
                if (i.eng, i.idx) in need_sig:
                    sigcount[i.eng] += 1
                    i.sig = True
                    semval[(i.eng, i.idx)] = sigcount[i.eng]
                else:
                    i.sig = False
            else:
                semval[(("d", i.slot), i.seq)] = 16 * (i.seq + 1)
        per = {e: [] for e in list(CENG) + ["sp"]}
        for i in self.ins:
            per[i.eng].append(i)
        self.stats = {e: len(per[e]) for e in per}
        self.stats["nwait"] = sum(len(i.waits) for i in self.ins)
        self.stats["sig"] = dict(sigcount)
        with contextlib.ExitStack() as es:
            sems = {}
            for e in CENG:
                sems[e] = es.enter_context(nc.semaphore("c_" + e))
            for s in range(NSLOT):
                sems[("d", s)] = es.enter_context(nc.semaphore("d%d" % s))
            block = es.enter_context(nc.Block())

            def run(engname):
                def body(eobj):
                    for i in per[engname]:
                        for k, v in i.waits:
                            eobj.wait_ge(sems[k], semval[(k, v)])
                        r = i.fn(eobj)
                        if i.kind == "dma":
                            r.then_inc(sems[("d", i.slot)], 16)
                        elif i.sig:
                            r.then_inc(sems[i.eng], 1)
                    if engname == "sp":
                        for s in range(NSLOT):
                            if self.slot_seq[s] > 0:
                                eobj.wait_ge(sems[("d", s)], 16 * self.slot_seq[s])
                return body

            block.tensor(run("pe"))
            block.scalar(run("act"))
            block.vector(run("dve"))
            block.gpsimd(run("pool"))
            block.sync(run("sp"))

    def mm(self, out, lhsT, rhs, start=True, stop=True, **kw):
        return self.op("pe", lambda e: e.matmul(out, lhsT, rhs, start=start, stop=stop, **kw),
                       [lhsT, rhs], [out])

    def tr(self, out, in_, ident):
        return self.op("pe", lambda e: e.transpose(out, in_, ident), [in_, ident], [out])

    def act(self, out, in_, func, bias=None, scale=1.0, accum_out=None):
        reads = [in_]
        kw = {}
        if bias is not None:
            kw["bias"] = bias
            if not isinstance(bias, (int, float)):
                reads.append(bias)
        if not isinstance(scale, (int, float)):
            reads.append(scale)
        writes = [out]
        if accum_out is not None:
            kw["accum_out"] = accum_out
            writes.append(accum_out)
        return self.op("act", lambda e: e.activation(out=out, in_=in_, func=func, scale=scale, **kw),
                       reads, writes)

    def tt(self, out, in0, in1, op, eng="dve"):
        return self.op(eng, lambda e: e.tensor_tensor(out=out, in0=in0, in1=in1, op=op), [in0, in1], [out])

    def ts(self, out, in0, s1, s2=None, op0=ALU.mult, op1=None, eng="dve"):
        reads = [in0]
        if not isinstance(s1, (int, float)):
            reads.append(s1)
        if s2 is not None and not isinstance(s2, (int, float)):
            reads.append(s2)
        if op1 is None:
            return self.op(eng, lambda e: e.tensor_scalar(out=out, in0=in0, scalar1=s1, scalar2=None, op0=op0),
                           reads, [out])
        return self.op(eng, lambda e: e.tensor_scalar(out=out, in0=in0, scalar1=s1, scalar2=s2, op0=op0, op1=op1),
                       reads, [out])

    def stt(self, out, in0, scalar, in1, op0, op1):
        reads = [in0, in1]
        if not isinstance(scalar, (int, float)):
            reads.append(scalar)
        return self.op("dve", lambda e: e.scalar_tensor_tensor(out=out, in0=in0, scalar=scalar, in1=in1,
                                                               op0=op0, op1=op1), reads, [out])

    def copy(self, out, in_, eng="dve"):
        if eng == "act":
            return self.op("act", lambda e: e.copy(out=out, in_=in_), [in_], [out])
        return self.op(eng, lambda e: e.tensor_copy(out=out, in_=in_), [in_], [out])

    def memset(self, ap, val, eng="dve"):
        return self.op(eng, lambda e: e.memset(ap, val), [], [ap])

    def scan(self, out, d0, d1, init):
        reads = [d0, d1]
        if not isinstance(init, (int, float)):
            reads.append(init)
        return self.op("dve", lambda e: e.tensor_tensor_scan(out=out, data0=d0, data1=d1, initial=init,
                                                             op0=ALU.mult, op1=ALU.add), reads, [out])


def fap(ap, dims):
    return bass.AP(ap.tensor, ap.offset, [list(ap.ap[0])] + [list(d) for d in dims])


class Cfg:
    def __init__(self, TP=8192, TS=1024, DEPTH=4, PAST=1024, SVALID=64, stop=99):
        self.TP, self.TS, self.DEPTH, self.PAST, self.SVALID = TP, TS, DEPTH, PAST, SVALID
        self.stop = stop


class Seq:
    pass


class _Stop(Exception):
    pass


class Builder:
    def __init__(self, cfg):
        self.cfg = cfg
        self.nc = bass.Bass("TRN2", target_bir_lowering=False)
        self.bank_rr = 0
        self.wrr = 0
        self.grr = 0

    def din(self, name, shape, dt=F32):
        return self.nc.dram_tensor(name, list(shape), dt, kind="ExternalInput").ap()

    def dout(self, name, shape, dt=F32):
        return self.nc.dram_tensor(name, list(shape), dt, kind="ExternalOutput").ap()

    def bank(self, n=1, lo=0, hi=8):
        if not hasattr(self, "_brr"):
            self._brr = {}
        rr = self._brr.get((lo, hi), lo)
        if n == 2 and (rr - lo) % 2:
            rr += 1
        if rr + n > hi:
            rr = lo
        self._brr[(lo, hi)] = rr + n
        return self.psum[:, rr * 512:(rr + n) * 512]

    def stage(self, n):
        if n >= self.cfg.stop:
            raise _Stop()

    def wbuf(self):
        b = self.t["w"][self.wrr % 3]
        self.wrr += 1
        return b

    def gt(self, w=512):
        b = self.G[:, self.grr % 4, 0:w]
        self.grr += 1
        return b

    def build(self):
        cfg = self.cfg
        nc = self.nc
        L, TP, TS, PAST = cfg.DEPTH, cfg.TP, cfg.TS, cfg.PAST
        d = {}
        d["xp"] = self.din("xp", [max(TP, 128), D])
        d["xs"] = self.din("xs", [128, D])
        d["w_in"] = self.din("w_in", [L, D, DIN])
        d["w_down"] = self.din("w_down", [L, 4, 512, D])
        d["w_out"] = self.din("w_out", [L, D, D])
        d["wab"] = self.din("wab", [L, 8, 128, 128])
        d["vecs"] = self.din("vecs", [L, 128, NV])
        d["vrep"] = self.din("vrep", [L, 128, 16])
        d["gpost"] = self.din("gpost", [L, D])
        d["consts"] = self.din("consts", [128, NC])
        d["svec"] = self.din("svec", [L, 128, NSV])
        d["sssd"] = self.din("sssd", [L, 128, 256])
        d["pk"] = self.din("pk", [L, 4, 128, PAST])
        d["pv"] = self.din("pv", [L, PAST, 512])
        o = {}
        o["yp"] = self.dout("yp", [max(TP, 128), D])
        o["ys"] = self.dout("ys", [128, D])
        for sfx, T in (("p", max(TP, 128)), ("s", 128)):
            o["lruh_" + sfx] = self.dout("lruh_" + sfx, [L, 128, 4])
            o["lruc_" + sfx] = self.dout("lruc_" + sfx, [L, 128, 12])
            o["ssd_" + sfx] = self.dout("ssd_" + sfx, [L, 128, 256])
            o["ssdc_" + sfx] = self.dout("ssdc_" + sfx, [L, 128, 24])
            o["cfc_" + sfx] = self.dout("cfc_" + sfx, [L, 128, 120])
            o["k_" + sfx] = self.dout("k_" + sfx, [L, T, 512])
            o["v_" + sfx] = self.dout("v_" + sfx, [L, T, 512])
        self.d, self.o = d, o
        TSC = max(TP, 128)
        self.winb = nc.dram_tensor("winb", [L, D, DIN], BF16).ap()
        self.wdnb = nc.dram_tensor("wdnb", [L, 4, 512, D], BF16).ap()
        self.woutb = nc.dram_tensor("woutb", [L, D, D], BF16).ap()
        self.kscr = nc.dram_tensor("kscr", [4, 128, TSC], BF16).ap()
        self.vscr = nc.dram_tensor("vscr", [TSC, 512], BF16).ap()

        with contextlib.ExitStack() as es:
            P = self.P = Prog(nc)

            def sb(name, shape, dt=F32):
                return es.enter_context(nc.sbuf_tensor(name, list(shape), dt))

            self.psum = es.enter_context(nc.psum_tensor("psum", [128, 4096], F32))
            TSm = self.TSm = max(TS, 128) if TP > 0 else 128
            t = self.t = {}
            t["cst"] = sb("cst", [128, NC])
            t["identb"] = sb("identb", [128, 128], BF16)
            t["negL"] = sb("negL", [128, 128], BF16)
            t["cmb"] = sb("cmb", [128, 128], BF16)
            t["onesm"] = sb("onesm", [128, 128], BF16)
            t["negone"] = sb("negone", [128, 2], BF16)
            t["negb"] = sb("negb", [128, 512], BF16)
            t["vec"] = sb("vec", [128, NV])
            t["vrep"] = sb("vrept", [128, 16])
            t["gpost"] = sb("gpostt", [128, D])
            t["cl"] = sb("cl", [128, 4])
            t["aneg"] = sb("aneg", [128, 8])
            t["sm"] = sb("sm", [128, 64])
            t["dl"] = sb("dl", [128, 16, 128], BF16)
            t["ds"] = sb("ds", [128, 32, 128], BF16)
            t["dc"] = sb("dc", [128, 31, 128], BF16)
            t["wab"] = sb("wabt", [128, 8, 128], BF16)
            t["wdt"] = sb("wdt", [128, 8, 8], BF16)
            t["w"] = [sb("wbuf%d" % i, [128, 8, 512], BF16) for i in range(3)]
            t["hT"] = sb("hT", [128, 8, TSm], BF16)
            t["xin"] = [sb("xin%d" % i, [128, D]) for i in range(2)]
            t["mixed"] = sb("mixed", [128, 8, TSm])
            t["uT"] = sb("uT", [128, 4, TSm], BF16)
            t["sg"] = sb("sg", [128, 4, TSm], BF16)
            t["QT"] = sb("QT", [128, 4, TSm], BF16)
            self.F = sb("F", [128, 8, TSm])
            self.F16 = self.F.bitcast(BF16)
            self.G = sb("G", [128, 4, 512])
            self.H = sb("H", [128, 4, TSm + 32], BF16)
            self.junk = sb("junk", [128, D], BF16)
            if TSm < 1024:
                self.Xs = sb("Xs", [128, 3072])
                self.Xa = sb("Xa", [128, 8192])
                self.Xh = sb("Xh", [128, 4, 1024], BF16)
            t["wab32"] = sb("wab32", [128, 8, 128])
            t["wdt32"] = sb("wdt32", [128, 8, 8])
            t["lhist"] = sb("lhist", [128, 4, 3], BF16)
            t["shist"] = sb("shist", [128, 8, 3], BF16)
            t["chist"] = sb("chist", [128, 4, 30], BF16)
            t["hst"] = sb("hst", [128, 4])
            t["S"] = sb("S", [128, 256])
            t["S16"] = sb("S16", [128, 256], BF16)
            t["tail"] = sb("tail", [128, 160])
            t["svec"] = sb("svect", [128, NSV])
            t["dts"] = sb("dts", [128, 4, 64])
            t["xdt"] = sb("xdt", [128, 512], BF16)
            t["xdt2"] = sb("xdt2", [128, 512], BF16)
            t["btm"] = sb("btm", [128, 256], BF16)
            t["cdec"] = sb("cdec", [128, 512], BF16)
            t["mt"] = sb("mt", [128, 1024], BF16)
            t["v16"] = [sb("v16_%d" % i, [128, 512], BF16) for i in range(2)]
            t["acc"] = sb("acc", [128, 32])
            t["eacc"] = sb("eacc", [128, 32])
            t["eacc2"] = sb("eacc2", [128, 32])

            self.setup_consts()
            self.convert_weights()
            seqp = Seq()
            seqp.name, seqp.T, seqp.TS, seqp.QW = "p", TP, TS, min(512, TS)
            seqp.xin, seqp.y, seqp.nvalid, seqp.past, seqp.valcol = d["xp"], o["yp"], TP, 0, C_VALP
            seqs = Seq()
            seqs.name, seqs.T, seqs.TS, seqs.QW = "s", 128, 128, 128
            seqs.xin, seqs.y, seqs.nvalid, seqs.past, seqs.valcol = d["xs"], o["ys"], cfg.SVALID, PAST, C_VALS
            try:
                for l in range(L):
                    self.layer_setup(l)
                    if TP > 0:
                        self.run_seq(seqp, l)
                    self.run_seq(seqs, l)
            except _Stop:
                pass
            P.emit()
        return nc

    def f(self, i, w=None):
        w = self.TSm if w is None else w
        return self.F[:, i, 0:w]

    def h(self, i, w=None):
        w = self.TSm + 32 if w is None else w
        return self.H[:, i, 0:w]

    def ssd_scr(self, off, n):
        if self.TSm >= 1024:
            return bass.AP(self.F, 4 * self.TSm + off, [[8 * self.TSm, 128], [1, n]])
        return self.Xs[:, off:off + n]

    def att_scr(self, off, n):
        if self.TSm >= 1024:
            return bass.AP(self.F, off, [[8 * self.TSm, 128], [1, n]])
        return self.Xa[:, off:off + n]

    def att_h(self, i):
        if self.TSm >= 1024:
            return self.H[:, i, 0:1024]
        return self.Xh[:, i, :]

    def mixb(self, oc, o0, w):
        TSm = self.TSm
        return bass.AP(self.F16, 8 * TSm + oc * TSm + o0, [[16 * TSm, 128], [1, w]])

    def stage32(self, i):
        if self.TSm >= 1024:
            return bass.AP(self.t["mixed"], i * 4096, [[8 * self.TSm, 128], [1, 4096]])
        return self.Xa[:, i * 4096:(i + 1) * 4096]

    def convert_weights(self):
        P, d = self.P, self.d
        L = self.cfg.DEPTH
        n = 0
        jobs = []
        for l in range(L):
            c0 = 0
            while c0 < DIN:
                nc_ = min(512, DIN - c0)
                jobs.append((d["w_in"][l, :, c0:c0 + nc_].rearrange("(kc p) n -> p kc n", p=128),
                             self.winb[l, :, c0:c0 + nc_].rearrange("(kc p) n -> p kc n", p=128), 8, nc_))
                c0 += nc_
            for b in range(4):
                jobs.append((d["w_down"][l, b].rearrange("(kc p) n -> p kc n", p=128),
                             self.wdnb[l, b].rearrange("(kc p) n -> p kc n", p=128), 4, 1024))
            for hf in range(2):
                jobs.append((d["w_out"][l, :, hf * 512:(hf + 1) * 512].rearrange("(kc p) n -> p kc n", p=128),
                             self.woutb[l, :, hf * 512:(hf + 1) * 512].rearrange("(kc p) n -> p kc n", p=128), 8, 512))
        for (src, dst, a, b) in jobs:
            s32 = self.stage32(n % 2)
            s32v = fap(s32[:, 0:1], [[b, a], [1, b]])
            wb = self.t["w"][n % 3]
            wbv = fap(wb[:, 0, 0:1], [[b, a], [1, b]])
            P.dma("sp", s32v, src)
            P.copy(wbv, s32v, eng=("dve", "act", "pool")[n % 3])
            P.dma("sp", dst, wbv)
            n += 1

    def setup_consts(self):
        P, t = self.P, self.t
        cst = t["cst"]
        P.dma("sp", cst[:], self.d["consts"])
        P.copy(t["identb"][:], cst[:, C_ID:C_ID + 128])
        P.tt(self.G[:, 0, 0:128], cst[:, C_U:C_U + 128], cst[:, C_ID:C_ID + 128], ALU.add)
        P.ts(t["negL"][:], self.G[:, 0, 0:128], -1.0)
        P.copy(t["cmb"][:], cst[:, C_CM:C_CM + 128])
        P.ts(t["onesm"][:], cst[:, C_ONE:C_ONE + 128], 1.0 / 512.0)
        P.memset(t["negone"][:], -1.0)
        P.copy(t["negb"][:], cst[:, C_NEG:C_NEG + 512])
        P.memset(t["tail"][:], 0.0)

    def layer_setup(self, l):
        P, t, d = self.P, self.t, self.d
        vec = t["vec"]
        P.dma("sp", vec[:], d["vecs"][l])
        P.dma("sp", t["vrep"][:], d["vrep"][l])
        gp = d["gpost"]
        P.dma("sp", t["gpost"][:], bass.AP(gp.tensor, l * D, [[0, 128], [1, D]]))
        P.dma("sp", t["wab32"][:], d["wab"][l].rearrange("i p m -> p i m"))
        P.copy(t["wab"][:], t["wab32"][:])
        P.dma("sp", t["wdt32"][:],
              d["w_in"][l, :, COL["ssd_dt"]:COL["ssd_dt"] + 8].rearrange("(kc p) n -> p kc n", p=128))
        P.copy(t["wdt"][:], t["wdt32"][:])
        sm = t["sm"]
        lam = vec[:, V_LLAM:V_LLAM + 4]
        P.act(sm[:, 0:4], lam, AF.Abs)
        P.act(sm[:, 4:8], sm[:, 0:4], AF.Exp, scale=-1.0)
        P.act(sm[:, 8:12], sm[:, 4:8], AF.Ln, bias=1.0)
        P.ts(sm[:, 12:16], lam, -1.0, 0.0, op0=ALU.mult, op1=ALU.max)
        P.tt(sm[:, 16:20], sm[:, 8:12], sm[:, 12:16], ALU.add)
        P.ts(t["cl"][:], sm[:, 16:20], -8.0)
        P.act(sm[:, 24:32], t["vrep"][:, 8:16], AF.Exp)
        P.ts(t["aneg"][:], sm[:, 24:32], -1.0)
        idf = t["cst"][:, C_ID:C_ID + 128]
        for i in range(16):
            P.ts(t["dl"][:, i, :], idf, vec[:, V_LCW + i:V_LCW + i + 1], eng="pool" if i % 2 else "dve")
        for i in range(32):
            P.ts(t["ds"][:, i, :], idf, vec[:, V_SCW + i:V_SCW + i + 1], eng="pool" if i % 2 else "dve")

    def load_w(self, l, c0, ncol=512):
        buf = self.wbuf()
        src = self.winb[l, :, c0:c0 + ncol].rearrange("(kc p) n -> p kc n", p=128)
        self.P.dma("sp", buf[:, :, 0:ncol], src)
        return buf

    def proj(self, wb, wc0, o0, w):
        P, t = self.P, self.t
        ps = self.bank()[:, 0:w]
        for k in range(8):
            P.mm(ps, wb[:, k, wc0:wc0 + 128], t["hT"][:, k, o0:o0 + w], start=(k == 0), stop=(k == 7))
        return ps

    def ttiles(self, TS):
        w = min(512, TS)
        return [(o0, w) for o0 in range(0, TS, w)]

    def rsqrt_rep(self, out, ps):
        P = self.P
        P.act(out, ps, AF.Ln, bias=EPS)
        P.act(out, out, AF.Exp, scale=-0.5)

    def run_seq(self, sq, l):
        P, t, d, o = self.P, self.t, self.d, self.o
        TS = sq.TS
        nst = sq.T // TS
        if sq.name == "p":
            for nm in ("lhist", "shist", "chist", "hst", "S", "S16"):
                P.memset(t[nm][:], 0.0, eng="pool")
        else:
            sv = t["svec"]
            P.dma("sp", sv[:], d["svec"][l])
            P.dma("sp", t["S"][:], d["sssd"][l])
            P.copy(t["S16"][:], t["S"][:])
            P.copy(t["hst"][:], sv[:, SV_H0:SV_H0 + 4])
            P.copy(t["lhist"][:], sv[:, SV_LH:SV_LH + 12].rearrange("p (c j) -> p c j", c=4))
            P.copy(t["shist"][:], sv[:, SV_SH:SV_SH + 24].rearrange("p (c j) -> p c j", c=8))
            P.copy(t["chist"][:], sv[:, SV_CH:SV_CH + 120].rearrange("p (c j) -> p c j", c=4))
        for st in range(nst):
            self.supertile(sq, l, st, last=(st == nst - 1))
        sfx = sq.name
        P.dma("sp", o["lruh_" + sfx][l], t["hst"][:])
        P.dma("sp", o["ssd_" + sfx][l], t["S"][:])
        P.dma("sp", o["lruc_" + sfx][l], t["tail"][:, 0:12])
        P.dma("sp", o["ssdc_" + sfx][l], t["tail"][:, 16:40])
        P.dma("sp", o["cfc_" + sfx][l], t["tail"][:, 40:160])

    def merge(self, sq, l, b, TS):
        P, t, d = self.P, self.t, self.d
        wd = self.wbuf()
        wdv = fap(wd[:], [[1024, 4], [1, 1024]])
        P.dma("sp", wdv, self.wdnb[l, b].rearrange("(kc p) n -> p kc n", p=128))
        for half in range(2):
            wm = self.load_w(l, COL["merge"] + b * 1024 + half * 512)
            for ocl in range(4):
                oc = half * 4 + ocl
                for (o0, w) in self.ttiles(TS):
                    py = self.bank()[:, 0:w]
                    for k in range(4):
                        P.mm(py, bass.AP(wd, k * 1024 + oc * 128, [[4096, 128], [1, 128]]),
                             t["uT"][:, k, o0:o0 + w], start=(k == 0), stop=(k == 3))
                    pg = self.proj(wm, ocl * 128, o0, w)
                    g1 = self.gt(w)
                    P.act(g1, pg, AF.Sigmoid)
                    if b == 0:
                        P.tt(t["mixed"][:, oc, o0:o0 + w], py, g1, ALU.mult)
                    else:
                        g2 = self.gt(w)
                        P.tt(g2, py, g1, ALU.mult)
                        dst = self.mixb(oc, o0, w) if b == 3 else t["mixed"][:, oc, o0:o0 + w]
                        P.tt(dst, t["mixed"][:, oc, o0:o0 + w], g2, ALU.add, eng="pool")

    def supertile(self, sq, l, st, last):
        P, t, d, o = self.P, self.t, self.d, self.o
        TS = sq.TS
        nblk = TS // 128
        t0 = st * TS
        tts = self.ttiles(TS)
        vec, hT, sm, sg, uT = t["vec"], t["hT"], t["sm"], t["sg"], t["uT"]
        f, h = self.f, self.h
        nv = min(sq.nvalid - t0, TS)
        xsrc = sq.xin if l == 0 else sq.y
        cst = t["cst"]
        ident = cst[:, C_ID:C_ID + 128]
        sfx = sq.name

        def in_tile(c0, c1, o0, w):
            return o0 <= c0 and c1 <= o0 + w

        xn = fap(self.G[:, 0, :], [[1, 1024]])
        for b in range(nblk):
            xt = t["xin"][b % 2]
            P.dma("sp", xt[:], xsrc[t0 + b * 128:t0 + (b + 1) * 128, :])
            ssq = sm[:, 32:33]
            P.act(self.junk[:], xt[:], AF.Square, accum_out=ssq)
            rs = sm[:, 34:35]
            P.act(rs, ssq, AF.Ln, bias=EPS, scale=1.0 / D)
            P.act(rs, rs, AF.Exp, scale=-0.5)
            P.ts(xn, xt[:], rs)
            for half in range(2):
                ps = self.bank()
                for j in range(4):
                    jj = half * 4 + j
                    P.tr(ps[:, j * 128:(j + 1) * 128], xn[:, jj * 128:(jj + 1) * 128], ident)
                gp = fap(vec[:, V_GPRE + half * 4:V_GPRE + half * 4 + 4], [[1, 4], [0, 128]])
                P.tt(hT[:, half * 4:half * 4 + 4, b * 128:(b + 1) * 128],
                     ps.rearrange("p (j q) -> p j q", j=4), gp, ALU.mult)

        self.stage(1)
        wb = self.load_w(l, COL["lru_g"])
        for c in range(4):
            for (o0, w) in tts:
                ps = self.proj(wb, c * 128, o0, w)
                P.act(sg[:, c, o0:o0 + w], ps, AF.Silu)
        self.stage(2)
        wb = self.load_w(l, COL["lru_x"])
        for c in range(4):
            lxh = h(c % 2)
            P.copy(lxh[:, 0:3], t["lhist"][:, c, :])
            for (o0, w) in tts:
                ps = self.proj(wb, c * 128, o0, w)
                P.copy(lxh[:, 3 + o0:3 + o0 + w], ps, eng="act")
                if last and in_tile(nv - 3, nv, o0, w):
                    P.copy(t["tail"][:, c * 3:c * 3 + 3], ps[:, nv - 3 - o0:nv - o0])
            P.copy(t["lhist"][:, c, :], lxh[:, TS:TS + 3])
            self.stage(2.1)
            xc, xc16 = f(0, TS), h(2, TS)
            for (o0, w) in tts:
                psc = self.bank()[:, 0:w]
                for k in range(4):
                    P.mm(psc, t["dl"][:, k * 4 + c, :], lxh[:, o0 + k:o0 + k + w], start=(k == 0), stop=(k == 3))
                P.act(xc[:, o0:o0 + w], psc, AF.Identity, bias=vec[:, V_LCB + c:V_LCB + c + 1])
                P.copy(xc16[:, o0:o0 + w], xc[:, o0:o0 + w])
            self.stage(2.2)
            r_, i_ = f(1, TS), f(2, TS)
            for (o0, w) in tts:
                pa = self.bank()[:, 0:w]
                P.mm(pa, t["wab"][:, c, :], xc16[:, o0:o0 + w])
                P.act(r_[:, o0:o0 + w], pa, AF.Sigmoid, bias=vec[:, V_LBA + c:V_LBA + c + 1])
                px = self.bank()[:, 0:w]
                P.mm(px, t["wab"][:, 4 + c, :], xc16[:, o0:o0 + w])
                P.act(i_[:, o0:o0 + w], px, AF.Sigmoid, bias=vec[:, V_LBX + c:V_LBX + c + 1])
            self.stage(2.3)
            a_, th, m_ = f(3, TS), f(4, TS), f(5, TS)
            clc = t["cl"][:, c:c + 1]
            P.act(a_, r_, AF.Exp, scale=clc)
            P.act(th, r_, AF.Tanh, scale=clc)
            self.stage(2.4)
            P.tt(m_, a_, a_, ALU.mult)
            P.stt(m_, m_, 1.0, th, ALU.add, ALU.mult)
            self.stage(2.5)
            P.act(m_, m_, AF.Sqrt, scale=-1.0)
            P.tt(i_, i_, xc, ALU.mult)
            P.tt(i_, i_, m_, ALU.mult)
            self.stage(2.6)
            hh = f(1, TS)
            P.scan(hh, a_, i_, t["hst"][:, c:c + 1])
            P.copy(t["hst"][:, c:c + 1], hh[:, nv - 1:nv])
            self.stage(2.7)
            P.tt(uT[:, c, 0:TS], hh, sg[:, c, 0:TS], ALU.mult)
        self.stage(3)
        self.merge(sq, l, 0, TS)

        self.stage(4)
        wb = self.load_w(l, COL["ssd_z"])
        for c in range(4):
            for (o0, w) in tts:
                ps = self.proj(wb, c * 128, o0, w)
                P.act(sg[:, c, o0:o0 + w], ps, AF.Silu)
        bc = t["QT"]
        for grp in range(2):
            wb = self.load_w(l, COL["ssd_x"] + grp * 512)
            for c in range(4):
                c8 = grp * 4 + c
                sxh = h(c8 % 2)
                P.copy(sxh[:, 0:3], t["shist"][:, c8, :])
                for (o0, w) in tts:
                    ps = self.proj(wb, c * 128, o0, w)
                    P.copy(sxh[:, 3 + o0:3 + o0 + w], ps, eng="act")
                    if last and in_tile(nv - 3, nv, o0, w):
                        P.copy(t["tail"][:, 16 + c8 * 3:16 + c8 * 3 + 3], ps[:, nv - 3 - o0:nv - o0])
                P.copy(t["shist"][:, c8, :], sxh[:, TS:TS + 3])
                for (o0, w) in tts:
                    psc = self.bank()[:, 0:w]
                    for k in range(4):
                        P.mm(psc, t["ds"][:, k * 8 + c8, :], sxh[:, o0 + k:o0 + k + w], start=(k == 0), stop=(k == 3))
                    dst = f(c)[:, o0:o0 + w] if grp == 0 else bc[:, c, o0:o0 + w]
                    P.act(dst, psc, AF.Silu, bias=vec[:, V_SCB + c8:V_SCB + c8 + 1])
        self.stage(5)
        dts = t["dts"]
        pd = self.bank()
        for b in range(nblk):
            for k in range(8):
                P.mm(pd[:, b * 8:b * 8 + 8], hT[:, k, b * 128:(b + 1) * 128], t["wdt"][:, k, :],
                     start=(k == 0), stop=(k == 7))
        nb8 = nblk * 8
        dx, dabs, dln, dt_, adt = dts[:, 0, 0:nb8], dts[:, 1, 0:nb8], dts[:, 2, 0:nb8], dts[:, 3, 0:nb8], dts[:, 1, 0:nb8]
        P.tt(dx.rearrange("p (b h) -> p b h", h=8), pd[:, 0:nb8].rearrange("p (b h) -> p b h", h=8),
             fap(t["vrep"][:, 0:8], [[0, nblk], [1, 8]]), ALU.add)
        P.act(dabs, dx, AF.Abs)
        P.act(dln, dabs, AF.Exp, scale=-1.0)
        P.act(dln, dln, AF.Ln, bias=1.0)
        P.ts(dabs, dx, 0.0, op0=ALU.max)
        P.tt(dt_, dabs, dln, ALU.add)
        P.ts(dt_, dt_, cst[:, sq.valcol:sq.valcol + 1])
        P.tt(adt.rearrange("p (b h) -> p b h", h=8), dt_.rearrange("p (b h) -> p b h", h=8),
             fap(t["aneg"][:], [[0, nblk], [1, 8]]), ALU.mult)
        self.stage(6)
        tri = cst[:, C_TRI:C_TRI + 128]
        Umat = cst[:, C_U:C_U + 128]
        ones = cst[:, C_ONE:C_ONE + 128]
        S, S16 = t["S"], t["S16"]
        for b in range(nblk):
            bs = slice(b * 128, (b + 1) * 128)
            dtb = dts[:, 3, b * 8:b * 8 + 8]
            adtb = dts[:, 1, b * 8:b * 8 + 8]
            psx = self.bank()
            for c in range(4):
                P.tr(psx[:, c * 128:(c + 1) * 128], f(c)[:, bs], ident)
            psx3 = psx.rearrange("p (h q) -> p h q", h=8)
            P.tt(t["xdt"][:].rearrange("p (h q) -> p h q", h=8), psx3, fap(dtb, [[1, 8], [0, 64]]), ALU.mult)
            pds = self.bank()
            P.mm(pds[:, 0:8], Umat, adtb)
            P.mm(pds[:, 8:16], ones, adtb)
            P.act(sm[:, 40:56], pds[:, 0:16], AF.Exp)
            P.tt(sm[:, 56:64], sm[:, 40:48], dtb, ALU.mult)
            P.tt(t["xdt2"][:].rearrange("p (h q) -> p h q", h=8), psx3, fap(sm[:, 56:64], [[1, 8], [0, 64]]), ALU.mult)
            pbt = self.bank()
            for cb in range(2):
                P.mm(pbt[:, cb * 128:(cb + 1) * 128], bc[:, cb, bs], t["identb"][:])
            P.copy(t["btm"][:], pbt[:, 0:256], eng="act")
            pcb = self.bank(2)
            for g_ in range(4):
                cc, gl = g_ // 2, g_ % 2
                P.mm(pcb[:, gl * 512 + cc * 128:gl * 512 + (cc + 1) * 128], bc[gl * 64:(gl + 1) * 64, cc, bs],
                     bc[gl * 64:(gl + 1) * 64, 2 + cc, bs])
            rall = self.ssd_scr(0, 1024)
            dec = self.ssd_scr(1024, 1024)
            P.tt(rall.rearrange("p (h q) -> p h q", h=8), fap(tri, [[0, 8], [1, 128]]),
                 fap(adtb, [[1, 8], [0, 128]]), ALU.mult)
            pseg = self.bank(2)
            for hf in range(2):
                P.mm(pseg[:, hf * 512:(hf + 1) * 512], Umat, rall[:, hf * 512:(hf + 1) * 512], start=True, stop=False)
                P.mm(pseg[:, hf * 512:(hf + 1) * 512], t["identb"][:], t["negb"][:], start=False, stop=True)
            P.act(dec, pseg, AF.Exp)
            for cc in range(2):
                P.tt(t["mt"][:, cc * 512:(cc + 1) * 512].rearrange("p (g r q) -> p g r q", g=2, r=2),
                     dec[:, cc * 512:(cc + 1) * 512].rearrange("p (g r q) -> p g r q", g=2, r=2),
                     fap(pcb[:, cc * 128:cc * 128 + 1], [[512, 2], [0, 2], [1, 128]]), ALU.mult)
            ea = self.ssd_scr(2048, 1024)
            for gl in range(2):
                pe_ = self.bank()
                rsel = bass.AP(rall.tensor, rall.offset + 2 * gl * 128, [list(rall.ap[0]), [512, 2], [128, 2], [1, 128]])
                P.mm(pe_, ones, rsel)
                hs = slice(gl * 64, (gl + 1) * 64)
                eav = ea[hs, gl * 512:(gl + 1) * 512]
                P.act(eav, pe_[hs, :], AF.Exp)
                P.tt(t["cdec"][hs, :].rearrange("p (c r q) -> p c r q", c=2, r=2),
                     eav.rearrange("p (c r q) -> p c r q", c=2, r=2),
                     fap(bc[hs, 2, bs], [[self.TSm, 2], [0, 2], [1, 128]]), ALU.mult)
            py = self.bank(2)
            for c in range(4):
                cc, gl = c // 2, c % 2
                for r in range(2):
                    hh_ = 2 * c + r
                    outp = py[r * 64:(r + 1) * 64, gl * 512 + cc * 128:gl * 512 + (cc + 1) * 128]
                    P.mm(outp, t["xdt"][:, hh_ * 64:(hh_ + 1) * 64], t["mt"][:, hh_ * 128:(hh_ + 1) * 128],
                         start=True, stop=False)
                    P.mm(outp, S16[gl * 64:(gl + 1) * 64, (cc * 2 + r) * 64:(cc * 2 + r + 1) * 64],
                         t["cdec"][gl * 64:(gl + 1) * 64, (cc * 2 + r) * 128:(cc * 2 + r + 1) * 128],
                         start=False, stop=True)
            for c in range(4):
                cc, gl = c // 2, c % 2
                P.stt(f(c)[:, bs], f(c)[:, bs], vec[:, V_SD + c:V_SD + c + 1],
                      py[:, gl * 512 + cc * 128:gl * 512 + (cc + 1) * 128], ALU.mult, ALU.add)
            pst = self.bank()
            for g_ in range(4):
                cc, gl = g_ // 2, g_ % 2
                P.mm(pst[gl * 64:(gl + 1) * 64, cc * 128:(cc + 1) * 128], t["btm"][:, g_ * 64:(g_ + 1) * 64],
                     t["xdt2"][:, g_ * 128:(g_ + 1) * 128])
            for gl in range(2):
                hs = slice(gl * 64, (gl + 1) * 64)
                sv4 = S[hs, :].rearrange("p (c r q) -> p c r q", c=2, r=2)
                P.tt(sv4, sv4, fap(sm[hs, 48 + 2 * gl:48 + 2 * gl + 1], [[4, 2], [1, 2], [0, 64]]), ALU.mult)
            P.tt(S[:], S[:], pst[:, 0:256], ALU.add)
            P.copy(S16[:], S[:])
        self.stage(7)
        for c in range(4):
            P.tt(f(c, TS), f(c, TS), sg[:, c, 0:TS], ALU.mult)
            P.act(h(c, TS), f(c, TS), AF.Square)
        for (o0, w) in tts:
            pss = self.bank()[:, 0:w]
            for c in range(4):
                P.mm(pss, t["onesm"][:], h(c)[:, o0:o0 + w], start=(c == 0), stop=(c == 3))
            rstd = self.gt(w)
            self.rsqrt_rep(rstd, pss)
            for c in range(4):
                P.stt(uT[:, c, o0:o0 + w], f(c)[:, o0:o0 + w], vec[:, V_SNORM + c:V_SNORM + c + 1], rstd,
                      ALU.mult, ALU.mult)
        self.merge(sq, l, 1, TS)

        self.stage(8)
        wb = self.load_w(l, COL["cf_g"])
        for c in range(4):
            for (o0, w) in tts:
                ps = self.proj(wb, c * 128, o0, w)
                P.act(sg[:, c, o0:o0 + w], ps, AF.Silu)
        wbb = self.load_w(l, COL["cf_b"])
        wba = self.load_w(l, COL["cf_a"])
        idf = ident
        for c in range(4):
            for k in range(31):
                P.ts(t["dc"][:, k, :], idf, vec[:, V_CCW + k * 4 + c:V_CCW + k * 4 + c + 1],
                     eng="pool" if k % 2 else "dve")
            gh = h(c % 2)
            P.copy(gh[:, 0:30], t["chist"][:, c, :])
            for (o0, w) in tts:
                pb = self.proj(wbb, c * 128, o0, w)
                sig = self.gt(w)
                P.act(sig, pb, AF.Sigmoid)
                pa = self.proj(wba, c * 128, o0, w)
                P.tt(gh[:, 30 + o0:30 + o0 + w], pa, sig, ALU.mult)
                if last and in_tile(nv - 30, nv, o0, w):
                    P.tt(t["tail"][:, 40 + c * 30:40 + c * 30 + 30], pa[:, nv - 30 - o0:nv - o0],
                         sig[:, nv - 30 - o0:nv - o0], ALU.mult)
            P.copy(t["chist"][:, c, :], gh[:, TS:TS + 30])
            for (o0, w) in tts:
                psc = self.bank()[:, 0:w]
                for k in range(31):
                    P.mm(psc, t["dc"][:, k, :], gh[:, o0 + k:o0 + k + w], start=(k == 0), stop=(k == 30))
                P.act(f(c)[:, o0:o0 + w], psc, AF.Identity, bias=vec[:, V_CCB + c:V_CCB + c + 1])
        for c in range(4):
            P.copy(h(c, TS), f(c, TS), eng="pool" if c % 2 else "dve")
        for (o0, w) in tts:
            pm = self.bank()[:, 0:w]
            for c in range(4):
                P.mm(pm, t["onesm"][:], h(c)[:, o0:o0 + w], start=(c == 0), stop=(c == 3))
            for c in range(4):
                P.tt(f(c)[:, o0:o0 + w], f(c)[:, o0:o0 + w], pm, ALU.subtract)
        for c in range(4):
            P.act(h(c, TS), f(c, TS), AF.Square)
        for (o0, w) in tts:
            pv_ = self.bank()[:, 0:w]
            for c in range(4):
                P.mm(pv_, t["onesm"][:], h(c)[:, o0:o0 + w], start=(c == 0), stop=(c == 3))
            rstd = self.gt(w)
            self.rsqrt_rep(rstd, pv_)
            for c in range(4):
                P.tt(f(c)[:, o0:o0 + w], f(c)[:, o0:o0 + w], rstd, ALU.mult)
        for c in range(4):
            P.act(f(4 + c % 2, TS), f(c, TS), AF.Silu, scale=vec[:, V_CLG + c:V_CLG + c + 1],
                  bias=vec[:, V_CLB + c:V_CLB + c + 1])
            P.tt(uT[:, c, 0:TS], f(4 + c % 2, TS), sg[:, c, 0:TS], ALU.mult)
        self.merge(sq, l, 2, TS)

        self.stage(9)
        QT = t["QT"]
        wb = self.load_w(l, COL["q"])
        for c in range(4):
            for (o0, w) in tts:
                ps = self.proj(wb, c * 128, o0, w)
                P.act(QT[:, c, o0:o0 + w], ps, AF.Copy, scale=0.125)
        wb = self.load_w(l, COL["k"])
        for c in range(4):
            kt = h(c, TS)
            for (o0, w) in tts:
                ps = self.proj(wb, c * 128, o0, w)
                P.copy(kt[:, o0:o0 + w], ps, eng="act" if c % 2 else "dve")
            P.dma("sp", self.kscr[c, :, t0:t0 + TS], kt)
        for b in range(nblk):
            pk = self.bank()
            for k in range(8):
                P.mm(pk, hT[:, k, b * 128:(b + 1) * 128], wb[:, k, :], start=(k == 0), stop=(k == 7))
            kv = self.gt(512)
            P.copy(kv, pk, eng="act")
            P.dma("sp", o["k_" + sfx][l, t0 + b * 128:t0 + (b + 1) * 128, :], kv)
        wb = self.load_w(l, COL["v"])
        for b in range(nblk):
            pv_ = self.bank()
            for k in range(8):
                P.mm(pv_, hT[:, k, b * 128:(b + 1) * 128], wb[:, k, :], start=(k == 0), stop=(k == 7))
            kv = self.gt(512)
            P.copy(kv, pv_, eng="act")
            P.dma("sp", o["v_" + sfx][l, t0 + b * 128:t0 + (b + 1) * 128, :], kv)
            P.copy(t["v16"][b % 2][:], pv_)
            P.dma("sp", self.vscr[t0 + b * 128:t0 + (b + 1) * 128, :], t["v16"][b % 2][:])
        wb = self.load_w(l, COL["sb_g"])
        for c in range(4):
            for (o0, w) in tts:
                ps = self.proj(wb, c * 128, o0, w)
                P.act(sg[:, c, o0:o0 + w], ps, AF.Silu)

        self.stage(10)
        QW = sq.QW
        nsub = QW // 128
        for qt in range(TS // QW):
            self.attention(sq, l, t0, qt)
        self.stage(11)
        self.merge(sq, l, 3, TS)

        self.stage(12)
        wo = []
        for half in range(2):
            buf = self.wbuf()
            P.dma("sp", buf[:], self.woutb[l, :, half * 512:(half + 1) * 512].rearrange("(kc p) n -> p kc n", p=128))
            wo.append(buf)
        ot = fap(self.G[:, 0, :], [[1, 1024]])
        for b in range(nblk):
            xt = t["xin"][b % 2]
            P.dma("sp", xt[:], xsrc[t0 + b * 128:t0 + (b + 1) * 128, :])
            for half in range(2):
                po = self.bank()
                for k in range(8):
                    P.mm(po, self.mixb(k, b * 128, 128), wo[half][:, k, :], start=(k == 0), stop=(k == 7))
                P.copy(ot[:, half * 512:(half + 1) * 512], po, eng="act")
            ssq = sm[:, 36:37]
            P.act(self.junk[:], ot, AF.Square, accum_out=ssq)
            rs = sm[:, 38:39]
            P.act(rs, ssq, AF.Ln, bias=EPS, scale=1.0 / D)
            P.act(rs, rs, AF.Exp, scale=-0.5)
            P.stt(ot, ot, rs, t["gpost"][:], ALU.mult, ALU.mult)
            P.tt(ot, ot, xt[:], ALU.add)
            P.dma("sp", sq.y[t0 + b * 128:t0 + (b + 1) * 128, :], ot)

    def attention(self, sq, l, t0, qt):
        P, t, d = self.P, self.t, self.d
        QW = sq.QW
        nsub = QW // 128
        QT, sg, uT = t["QT"], t["sg"], t["uT"]
        q0 = qt * QW
        gq0 = t0 + q0
        acc = t["acc"]
        eaccs = [t["eacc"], t["eacc2"]]
        oacc = self.att_scr(2048, nsub * 512)
        tmpo = self.G[:, 3, :]
        P.memset(acc[:, 0:nsub * 8], 0.0)
        P.memset(eaccs[0][:, 0:nsub * 8], 1.0)
        P.memset(oacc, 0.0, eng="pool")
        spans = []
        if sq.past == 0:
            assert QW == 512
            nsp = gq0 // 512 + 1
            for sp_ in range(nsp - 1, -1, -1):
                spans.append(("cur", sp_ * 512, 512, sp_ == nsp - 1))
        else:
            spans.append(("cur", 0, QW, True))
            for sp_ in range(sq.past // 512 - 1, -1, -1):
                spans.append(("past", sp_ * 512, 512, False))

        kvbufs = {}

        def load_span(si):
            if si >= len(spans) or si in kvbufs:
                return
            kind, k0, klen, diag = spans[si]
            kvb = self.wbuf()
            nkb = klen // 128
            if kind == "cur":
                P.dma("sp", kvb[:, 0:4, 0:klen], self.kscr[:, :, k0:k0 + klen].rearrange("c p n -> p c n"))
                P.dma("sp", kvb[:, 4:4 + nkb, :], self.vscr[k0:k0 + klen, :].rearrange("(j p) n -> p j n", p=128))
            else:
                stg = self.att_scr(4096, 4096)
                sk = fap(stg[:, 0:1], [[512, 4], [1, klen]])
                svv = fap(stg[:, 2048:2049], [[512, nkb], [1, 512]])
                P.dma("sp", sk, d["pk"][l, :, :, k0:k0 + klen].rearrange("c p n -> p c n"))
                P.dma("sp", svv, d["pv"][l, k0:k0 + klen, :].rearrange("(j p) n -> p j n", p=128))
                P.copy(kvb[:, 0:4, 0:klen], sk)
                P.copy(kvb[:, 4:4 + nkb, :], svv, eng="act")
            kvbufs[si] = kvb

        items = []
        blk = 0
        for si, (kind, k0, klen, diag) in enumerate(spans):
            nkb = klen // 128
            for j in range(nkb - 1, -1, -1):
                for hp in range(4):
                    items.append(dict(si=si, j=j, hp=hp, diag=diag, c0=(j * 128 if diag else 0), blk=blk,
                                      first=(hp == 0), last=(hp == 3), sfirst=(j == nkb - 1 and hp == 0)))
                blk += 1
        n = len(items)
        paccs = {}

        def bufs(i):
            b = i % 2
            pz = self.psum[:, b * 1024:(b + 1) * 1024]
            et = self.att_scr(b * 1024, 1024)
            spt = self.att_h(b)
            wt = self.att_h(2 + b)
            po = self.psum[:, (4 + b) * 512:(5 + b) * 512]
            return pz, et, spt, wt, po

        def views(it, pz, et, spt, wt):
            c0 = it["c0"]
            wq = QW - c0
            return (fap(pz[:, c0:c0 + 1], [[512, 2], [1, wq]]), fap(et[:, c0:c0 + 1], [[QW, 2], [1, wq]]),
                    fap(spt[:, c0:c0 + 1], [[QW, 2], [1, wq]]), fap(wt[:, c0:c0 + 1], [[QW, 2], [1, wq]]))

        def stA(i):
            it = items[i]
            if it["sfirst"]:
                load_span(it["si"])
                load_span(it["si"] + 1)
            kvb = kvbufs[it["si"]]
            pz, et, spt, wt, po = bufs(i)
            c0, j, hp = it["c0"], it["j"], it["hp"]
            for r in range(2):
                rs_ = slice(r * 64, (r + 1) * 64)
                P.mm(pz[:, r * 512 + c0:r * 512 + QW], kvb[rs_, hp, j * 128:(j + 1) * 128],
                     QT[rs_, hp, q0 + c0:q0 + QW])

        def stB1(i):
            it = items[i]
            pz, et, spt, wt, po = bufs(i)
            pzv, etv, spv, wtv = views(it, pz, et, spt, wt)
            P.act(etv, pzv, AF.Exp)

        def stB(i):
            it = items[i]
            pz, et, spt, wt, po = bufs(i)
            pzv, etv, spv, wtv = views(it, pz, et, spt, wt)
            c0 = it["c0"]
            P.act(spv, etv, AF.Ln, bias=1.0)
            if it["diag"]:
                dv = fap(spt[:, c0:c0 + 1], [[QW, 2], [1, 128]])
                P.tt(dv, dv, fap(t["cmb"][:, 0:1], [[0, 2], [1, 128]]), ALU.mult)

        def stC(i):
            it = items[i]
            pz, et, spt, wt, po = bufs(i)
            c0, hp = it["c0"], it["hp"]
            sub0 = c0 // 128
            for r in range(2):
                P.mm(pz[:, r * 512 + c0:r * 512 + QW], t["negL"][:], spt[:, r * QW + c0:(r + 1) * QW],
                     start=False, stop=True, skip_group_check=True)
            pacc = self.psum[:, (6 + it["blk"] % 2) * 512:(7 + it["blk"] % 2) * 512]
            for r in range(2):
                hd = 2 * hp + r
                for sub in range(sub0, nsub):
                    qs = slice(r * QW + sub * 128, r * QW + (sub + 1) * 128)
                    P.mm(pacc[:, sub * 8 + hd:sub * 8 + hd + 1], spt[:, qs], t["negone"][:, 0:1])

        def stD(i):
            it = items[i]
            pz, et, spt, wt, po = bufs(i)
            pzv, etv, spv, wtv = views(it, pz, et, spt, wt)
            c0 = it["c0"]
            P.act(wtv, pzv, AF.Exp)
            if it["diag"]:
                dv = fap(wt[:, c0:c0 + 1], [[QW, 2], [1, 128]])
                P.tt(dv, dv, fap(t["cmb"][:, 0:1], [[0, 2], [1, 128]]), ALU.mult)
            if it["last"]:
                sub0 = c0 // 128
                pacc = self.psum[:, (6 + it["blk"] % 2) * 512:(7 + it["blk"] % 2) * 512]
                a_ = acc[:, sub0 * 8:nsub * 8]
                P.tt(a_, a_, pacc[:, sub0 * 8:nsub * 8], ALU.add)
                P.act(eaccs[(it["blk"] + 1) % 2][:, 0:nsub * 8], acc[:, 0:nsub * 8], AF.Exp)

        def stE(i):
            it = items[i]
            kvb = kvbufs[it["si"]]
            pz, et, spt, wt, po = bufs(i)
            c0, j, hp = it["c0"], it["j"], it["hp"]
            sub0 = c0 // 128
            for r in range(2):
                hd = 2 * hp + r
                for sub in range(sub0, nsub):
                    qs = slice(r * QW + sub * 128, r * QW + (sub + 1) * 128)
                    P.mm(po[:, (sub * 2 + r) * 64:(sub * 2 + r + 1) * 64], wt[:, qs],
                         kvb[:, 4 + j, hd * 64:(hd + 1) * 64])

        def stF(i):
            it = items[i]
            pz, et, spt, wt, po = bufs(i)
            c0, hp = it["c0"], it["hp"]
            sub0 = c0 // 128
            ns = nsub - sub0
            eacc = eaccs[it["blk"] % 2]
            pov = fap(po[:, sub0 * 128:sub0 * 128 + 1], [[128, ns], [64, 2], [1, 64]])
            eav = fap(eacc[:, sub0 * 8 + 2 * hp:sub0 * 8 + 2 * hp + 1], [[8, ns], [1, 2], [0, 64]])
            tv = fap(tmpo[:, 0:1], [[128, ns], [64, 2], [1, 64]])
            P.tt(tv, pov, eav, ALU.mult)
            ov = fap(oacc[:, sub0 * 512 + 2 * hp * 64:sub0 * 512 + 2 * hp * 64 + 1], [[512, ns], [64, 2], [1, 64]])
            P.tt(ov, ov, tv, ALU.add, eng="pool")

        for s_ in range(n + 2):
            if s_ < n:
                stA(s_)
                stB1(s_)
            if 0 <= s_ - 1 < n:
                stC(s_ - 1)
                stD(s_ - 1)
            if s_ < n:
                stB(s_)
            if 0 <= s_ - 2 < n:
                stE(s_ - 2)
                stF(s_ - 2)
        ident = t["cst"][:, C_ID:C_ID + 128]
        for sub in range(nsub):
            pt = self.bank(1, 4, 6)
            for c in range(4):
                P.tr(pt[:, c * 128:(c + 1) * 128], oacc[:, sub * 512 + c * 128:sub * 512 + (c + 1) * 128], ident)
            cs = slice(q0 + sub * 128, q0 + (sub + 1) * 128)
            P.tt(uT[:, :, cs], pt.rearrange("p (c q) -> p c q", c=4), sg[:, :, cs], ALU.mult)


def make_consts():
    c = np.zeros((128, NC), np.float32)
    j = np.arange(128)[:, None]
    s = np.arange(128)[None, :]
    c[:, C_ID:C_ID + 128] = (j == s)
    c[:, C_U:C_U + 128] = (j > s)
    c[:, C_TRI:C_TRI + 128] = (j <= s)
    for rep in range(4):
        c[:, C_NEG + rep * 128:C_NEG + (rep + 1) * 128] = np.where(s < j, -30000.0, 0.0)
    c[:, C_CM:C_CM + 128] = (j < s)
    c[:, C_ONE:C_ONE + 128] = 1.0
    c[:, C_VALP] = 1.0
    return c


def fm(v, nch):
    return np.ascontiguousarray(np.asarray(v, np.float32).reshape(nch, 128).T)


def host_prep(inp, cfg):
    L = cfg.DEPTH
    f32 = np.float32
    vecs = np.zeros((L, 128, NV), f32)
    vrep = np.zeros((L, 128, 16), f32)
    wab = np.zeros((L, 8, 128, 128), f32)
    for l in range(L):
        v = vecs[l]
        v[:, V_GPRE:V_GPRE + 8] = fm(inp["norm_pre"][l], 8)
        for k in range(4):
            v[:, V_LCW + k * 4:V_LCW + k * 4 + 4] = fm(inp["lru_conv_w"][l, k], 4)
            v[:, V_SCW + k * 8:V_SCW + k * 8 + 8] = fm(inp["ssd_conv_w"][l, k], 8)
        v[:, V_LCB:V_LCB + 4] = fm(inp["lru_conv_b"][l], 4)
        v[:, V_LBA:V_LBA + 4] = fm(inp["lru_ba"][l], 4)
        v[:, V_LBX:V_LBX + 4] = fm(inp["lru_bx"][l], 4)
        v[:, V_LLAM:V_LLAM + 4] = fm(inp["lru_lambda"][l], 4)
        v[:, V_SCB:V_SCB + 8] = fm(inp["ssd_conv_b"][l], 8)
        v[:, V_SD:V_SD + 4] = fm(np.repeat(np.asarray(inp["ssd_d"][l]), 64), 4)
        v[:, V_SNORM:V_SNORM + 4] = fm(inp["ssd_norm"][l], 4)
        for k in range(31):
            v[:, V_CCW + k * 4:V_CCW + k * 4 + 4] = fm(inp["cf_conv_w"][l, k], 4)
        v[:, V_CCB:V_CCB + 4] = fm(inp["cf_conv_b"][l], 4)
        v[:, V_CLG:V_CLG + 4] = fm(inp["cf_ln_g"][l], 4)
        v[:, V_CLB:V_CLB + 4] = fm(inp["cf_ln_b"][l], 4)
        vrep[l, :, 0:8] = np.asarray(inp["ssd_dt_bias"][l])[None, :]
        vrep[l, :, 8:16] = np.asarray(inp["ssd_a_log"][l])[None, :]
        for which, nm in enumerate(("lru_wa", "lru_wx")):
            w = np.asarray(inp[nm][l])
            for c in range(4):
                wab[l, which * 4 + c, 0:64, 0:64] = w[2 * c]
                wab[l, which * 4 + c, 64:128, 64:128] = w[2 * c + 1]
    consts = make_consts()
    consts[:cfg.SVALID, C_VALS] = 1.0
    shared = dict(w_in=np.ascontiguousarray(inp["w_in"], f32), w_down=np.ascontiguousarray(inp["w_down"], f32),
                  w_out=np.ascontiguousarray(inp["w_out"], f32), wab=wab, vecs=vecs, vrep=vrep,
                  gpost=np.ascontiguousarray(inp["norm_post"], f32), consts=consts)
    return shared


def sample_inputs(inp, cfg, s):
    L = cfg.DEPTH
    f32 = np.float32
    SV = cfg.SVALID
    xs = np.zeros((128, D), f32)
    xs[:SV] = inp["x_sample"][s]
    svec = np.zeros((L, 128, NSV), f32)
    sssd = np.zeros((L, 128, 256), f32)
    pk = np.zeros((L, 4, 128, cfg.PAST), f32)
    for l in range(L):
        svec[l, :, SV_H0:SV_H0 + 4] = fm(inp["state_lru_h"][l, s], 4)
        lc = np.asarray(inp["state_lru_conv"][l, s])
        svec[l, :, SV_LH:SV_LH + 12] = lc.T.reshape(4, 128, 3).transpose(1, 0, 2).reshape(128, 12)
        sc = np.asarray(inp["state_ssd_conv"][l, s])
        svec[l, :, SV_SH:SV_SH + 24] = sc.T.reshape(8, 128, 3).transpose(1, 0, 2).reshape(128, 24)
        cc = np.asarray(inp["state_cf_conv"][l, s])
        svec[l, :, SV_CH:SV_CH + 120] = cc.T.reshape(4, 128, 30).transpose(1, 0, 2).reshape(128, 120)
        st = np.asarray(inp["state_ssd"][l, s])
        st = st.reshape(2, 2, 2, 64, 64)
        sssd[l] = st.transpose(1, 4, 0, 2, 3).reshape(128, 256)
        k = np.asarray(inp["cache_sb_k"][l, s]).reshape(cfg.PAST, 512)
        pk[l] = k.T.reshape(4, 128, cfg.PAST)
    pv = np.ascontiguousarray(np.asarray(inp["cache_sb_v"][:, s]).reshape(L, cfg.PAST, 512), f32)
    return dict(xs=xs, svec=svec, sssd=sssd, pk=pk, pv=pv)


def unpack_states(res, sfx, L):
    lruh = res["lruh_" + sfx].transpose(0, 2, 1).reshape(L, 512)
    lruc = res["lruc_" + sfx].reshape(L, 128, 4, 3).transpose(0, 3, 2, 1).reshape(L, 3, 512)
    ssdc = res["ssdc_" + sfx][:, :, 0:24].reshape(L, 128, 8, 3).transpose(0, 3, 2, 1).reshape(L, 3, 1024)
    cfc = res["cfc_" + sfx].reshape(L, 128, 4, 30).transpose(0, 3, 2, 1).reshape(L, 30, 512)
    ssd = res["ssd_" + sfx].reshape(L, 2, 64, 2, 2, 64)
    ssd = ssd.transpose(0, 3, 1, 4, 5, 2).reshape(L, 8, 64, 64)
    return lruh, lruc, ssd, ssdc, cfc


_NC_CACHE = {}


def run_cfg(inp, cfg, n_cores=8):
    key = (cfg.TP, cfg.TS, cfg.DEPTH, cfg.PAST, cfg.SVALID)
    if key not in _NC_CACHE:
        _NC_CACHE[key] = Builder(cfg).build()
    nc = _NC_CACHE[key]
    shared = host_prep(inp, cfg)
    L = cfg.DEPTH
    nb = inp["x_prompt"].shape[0] if cfg.TP > 0 else 0
    nsmp = inp["x_sample"].shape[0]
    in_maps = []
    for c in range(n_cores):
        m = dict(shared)
        if cfg.TP > 0 and c % 2 == 0:
            m["xp"] = np.ascontiguousarray(inp["x_prompt"][(c // 2) % nb], np.float32)
        elif cfg.TP > 0:
            m["xp"] = np.zeros((cfg.TP, D), np.float32)
        else:
            m["xp"] = np.zeros((128, D), np.float32)
        m.update(sample_inputs(inp, cfg, c % nsmp))
        in_maps.append(m)
    res = run_bass_kernel_spmd(nc, in_maps, core_ids=list(range(n_cores))).results
    SV = cfg.SVALID
    outs = {}
    if cfg.TP > 0:
        pcs = [2 * b for b in range(nb)]
        outs["y_p"] = np.stack([res[c]["yp"] for c in pcs])
        st = [unpack_states(res[c], "p", L) for c in pcs]
        for i, nm in enumerate(("lru_h_p", "lru_conv_p", "ssd_p", "ssd_conv_p", "cf_conv_p")):
            outs[nm] = np.stack([s[i] for s in st], axis=1)
        outs["k_p"] = np.stack([res[c]["k_p"].reshape(L, cfg.TP, 8, 64) for c in pcs], axis=1)
        outs["v_p"] = np.stack([res[c]["v_p"].reshape(L, cfg.TP, 8, 64) for c in pcs], axis=1)
    scs = list(range(min(nsmp, n_cores)))
    outs["y_s"] = np.stack([res[c]["ys"][:SV] for c in scs])
    st = [unpack_states(res[c], "s", L) for c in scs]
    for i, nm in enumerate(("lru_h_s", "lru_conv_s", "ssd_s", "ssd_conv_s", "cf_conv_s")):
        outs[nm] = np.stack([s[i] for s in st], axis=1)
    outs["k_s"] = np.stack([res[c]["k_s"][:, :SV].reshape(L, SV, 8, 64) for c in scs], axis=1)
    outs["v_s"] = np.stack([res[c]["v_s"][:, :SV].reshape(L, SV, 8, 64) for c in scs], axis=1)
    return outs


ORDER = ("y_p", "y_s", "lru_h_p", "lru_h_s", "lru_conv_p", "lru_conv_s", "ssd_p", "ssd_s",
         "ssd_conv_p", "ssd_conv_s", "cf_conv_p", "cf_conv_s", "k_p", "k_s", "v_p", "v_s")


def kernel(**inputs):
    inp = {k: np.asarray(v) for k, v in inputs.items()}
    cfg = Cfg(TP=inp["x_prompt"].shape[1], TS=1024, DEPTH=inp["w_in"].shape[0],
              PAST=inp["cache_sb_k"].shape[2], SVALID=inp["x_sample"].shape[1])
    outs = run_cfg(inp, cfg, 8)
    return tuple(np.ascontiguousarray(outs[k], dtype=np.float32) for k in ORDER)
```
